# Optimizing a Trainium2 kernel written in Bass

```python
import jax, jax.numpy as jnp
from jax import lax
import numpy as np

D_MODEL = 1024
BATCH = 2
SEQ = 8192
DEPTH = 4
DEC_BATCH = 128
DEC_SEQ = 1
PAST_LEN = 8192
PAGE_SIZE = 128

N_A_LAYERS = DEPTH // 2
N_B_LAYERS = DEPTH - N_A_LAYERS
CONV_WIDTH = 3
HEAD_DIM = 64
N_HEADS = D_MODEL // HEAD_DIM
N_KV_HEADS = N_HEADS // 4
GROUP = N_HEADS // N_KV_HEADS
WINDOW = 128
D_FF = -(-8 * D_MODEL // (3 * 256)) * 256
ROPE_THETA = 10000.0
EPS = 1e-6

kernel_name = "yoco_shortconv_swa_sink_decoder_step"


def rmsnorm(x, g):
    xf = x.astype(jnp.float32)
    inv = lax.rsqrt(jnp.mean(xf * xf, axis=-1, keepdims=True) + EPS)
    return (xf * inv).astype(x.dtype) * g


def rope(x, pos):
    half = HEAD_DIM // 2
    freqs = ROPE_THETA ** (-jnp.arange(half, dtype=jnp.float32) / half)
    ang = pos.astype(jnp.float32)[:, None] * freqs[None, :]
    cos = jnp.cos(ang)[:, None, :].astype(x.dtype)
    sin = jnp.sin(ang)[:, None, :].astype(x.dtype)
    x1, x2 = x[..., :half], x[..., half:]
    return jnp.concatenate([x1 * cos - x2 * sin, x2 * cos + x1 * sin], axis=-1)


def short_conv(h, prefix, w_in, conv_w, w_out):
    b, c, xv = jnp.split(h @ w_in, 3, axis=-1)
    u = c * xv
    u_ext = jnp.concatenate([prefix.astype(u.dtype), u], axis=1)
    L = h.shape[1]
    y = conv_w[0] * u_ext[:, 0:L]
    for j in range(1, CONV_WIDTH):
        y = y + conv_w[j] * u_ext[:, j:j + L]
    return (b * y) @ w_out, u_ext[:, -(CONV_WIDTH - 1):]


def swiglu(h, w_gu, w_down):
    g, u = jnp.split(h @ w_gu, 2, axis=-1)
    return (jax.nn.silu(g) * u) @ w_down


def window_attention(q, k, v, sinks, prefix_valid):
    N, Lq = q.shape[0], q.shape[1]
    Lk = k.shape[1]
    qg = q.reshape(N, Lq, N_KV_HEADS, GROUP, HEAD_DIM)
    s = jnp.einsum('nqkgd,nskd->nkgqs', qg, k).astype(jnp.float32) * (HEAD_DIM ** -0.5)
    a = jnp.arange(Lq)[:, None]
    bidx = jnp.arange(Lk)[None, :]
    band = (bidx > a) & (bidx <= a + WINDOW)
    valid = band[None] & ((bidx >= WINDOW)[None] | prefix_valid[:, None, None])
    s = jnp.where(valid[:, None, None], s, -jnp.inf)
    sink = sinks.astype(jnp.float32).reshape(1, N_KV_HEADS, GROUP, 1, 1)
    m = jnp.maximum(jnp.max(s, axis=-1, keepdims=True), sink)
    p = jnp.exp(s - m)
    denom = jnp.sum(p, axis=-1, keepdims=True) + jnp.exp(sink - m)
    o = jnp.einsum('nkgqs,nskd->nqkgd', (p / denom).astype(v.dtype), v)
    return o.reshape(N, Lq, N_HEADS * HEAD_DIM)


def trunk(x, pos, conv_prefix, k_prefix, v_prefix,
          norm_a, w_in_a, conv_w, w_out_a,
          norm_kv, w_kv,
          norm_b, w_q, w_o, sinks,
          norm_ffn, w_gu, w_down, norm_final):
    N, L = x.shape[0], x.shape[1]
    conv_states = []
    for l in range(N_A_LAYERS):
        out, st = short_conv(rmsnorm(x, norm_a[l]), conv_prefix[l], w_in_a[l], conv_w[l], w_out_a[l])
        x = x + out
        x = x + swiglu(rmsnorm(x, norm_ffn[l]), w_gu[l], w_down[l])
        conv_states.append(st)
    conv_states = jnp.stack(conv_states, axis=0)

    k, v = jnp.split(rmsnorm(x, norm_kv) @ w_kv, 2, axis=-1)
    k = rope(k.reshape(N, L, N_KV_HEADS, HEAD_DIM), pos)
    v = v.reshape(N, L, N_KV_HEADS, HEAD_DIM)

    if k_prefix is None:
        nblk = L // WINDOW
        zpad = jnp.zeros((N, WINDOW, N_KV_HEADS, HEAD_DIM), k.dtype)
        def band_keys(t):
            prev = jnp.concatenate([zpad, t[:, :-WINDOW]], axis=1).reshape(N * nblk, WINDOW, N_KV_HEADS, HEAD_DIM)
            cur = t.reshape(N * nblk, WINDOW, N_KV_HEADS, HEAD_DIM)
            return jnp.concatenate([prev, cur], axis=1)
        kb, vb = band_keys(k), band_keys(v)
        prefix_valid = jnp.tile(jnp.arange(nblk) > 0, N)
        k_win, v_win = k[:, -WINDOW:], v[:, -WINDOW:]
    else:
        kb = jnp.concatenate([k_prefix.astype(k.dtype), k], axis=1)
        vb = jnp.concatenate([v_prefix.astype(v.dtype), v], axis=1)
        prefix_valid = jnp.ones((N,), dtype=bool)
        k_win, v_win = kb[:, -WINDOW:], vb[:, -WINDOW:]

    for l in range(N_B_LAYERS):
        h = rmsnorm(x, norm_b[l])
        q = rope((h @ w_q[l]).reshape(N, L, N_HEADS, HEAD_DIM), pos)
        if k_prefix is None:
            qb = q.reshape(N * (L // WINDOW), WINDOW, N_HEADS, HEAD_DIM)
            o = window_attention(qb, kb, vb, sinks[l], prefix_valid).reshape(N, L, N_HEADS * HEAD_DIM)
        else:
            o = window_attention(q, kb, vb, sinks[l], prefix_valid)
        x = x + o @ w_o[l]
        li = N_A_LAYERS + l
        x = x + swiglu(rmsnorm(x, norm_ffn[li]), w_gu[li], w_down[li])
    return rmsnorm(x, norm_final), conv_states, k_win, v_win


def setup_inputs(seed: int = 0) -> dict:
    key = jax.random.key(seed)
    ks = jax.random.split(key, 24)
    f32 = jnp.float32
    D = D_MODEL
    def nrm(k, shape, scale):
        return jax.random.normal(k, shape, f32) * scale
    def gain(k, shape):
        return 1.0 + 0.05 * jax.random.normal(k, shape, f32)
    return {
        "x_prompt": nrm(ks[0], (BATCH, SEQ, D), 1.0),
        "x_sample": nrm(ks[1], (DEC_BATCH, DEC_SEQ, D), 1.0),
        "cache_conv": nrm(ks[2], (N_A_LAYERS, DEC_BATCH, CONV_WIDTH - 1, D), 1.0),
        "cache_k": nrm(ks[3], (DEC_BATCH, WINDOW, N_KV_HEADS, HEAD_DIM), 1.0),
        "cache_v": nrm(ks[4], (DEC_BATCH, WINDOW, N_KV_HEADS, HEAD_DIM), 1.0),
        "norm_a": gain(ks[5], (N_A_LAYERS, D)),
        "w_in_a": nrm(ks[6], (N_A_LAYERS, D, 3 * D), D ** -0.5),
        "conv_w": nrm(ks[7], (N_A_LAYERS, CONV_WIDTH, D), CONV_WIDTH ** -0.5),
        "w_out_a": nrm(ks[8], (N_A_LAYERS, D, D), D ** -0.5),
        "norm_kv": gain(ks[9], (D,)),
        "w_kv": nrm(ks[10], (D, 2 * N_KV_HEADS * HEAD_DIM), D ** -0.5),
        "norm_b": gain(ks[11], (N_B_LAYERS, D)),
        "w_q": nrm(ks[12], (N_B_LAYERS, D, N_HEADS * HEAD_DIM), D ** -0.5),
        "w_o": nrm(ks[13], (N_B_LAYERS, N_HEADS * HEAD_DIM, D), (N_HEADS * HEAD_DIM) ** -0.5),
        "sinks": nrm(ks[14], (N_B_LAYERS, N_HEADS), 0.5),
        "norm_ffn": gain(ks[15], (DEPTH, D)),
        "w_gu": nrm(ks[16], (DEPTH, D, 2 * D_FF), D ** -0.5),
        "w_down": nrm(ks[17], (DEPTH, D_FF, D), D_FF ** -0.5),
        "norm_final": gain(ks[18], (D,)),
    }


def reference(x_prompt, x_sample, cache_conv, cache_k, cache_v,
              norm_a, w_in_a, conv_w, w_out_a, norm_kv, w_kv,
              norm_b, w_q, w_o, sinks, norm_ffn, w_gu, w_down, norm_final):
    B, L = x_prompt.shape[0], x_prompt.shape[1]
    S = x_sample.shape[1]
    pos_prompt = jnp.arange(L, dtype=jnp.int32)
    pos_sample = PAST_LEN + jnp.arange(S, dtype=jnp.int32)
    conv_zero = jnp.zeros((N_A_LAYERS, B, CONV_WIDTH - 1, D_MODEL), x_prompt.dtype)
    y_prompt, conv_prompt, k_win_prompt, v_win_prompt = trunk(
        x_prompt, pos_prompt, conv_zero, None, None,
        norm_a, w_in_a, conv_w, w_out_a, norm_kv, w_kv,
        norm_b, w_q, w_o, sinks, norm_ffn, w_gu, w_down, norm_final)
    y_sample, conv_sample, k_win_sample, v_win_sample = trunk(
        x_sample, pos_sample, cache_conv, cache_k, cache_v,
        norm_a, w_in_a, conv_w, w_out_a, norm_kv, w_kv,
        norm_b, w_q, w_o, sinks, norm_ffn, w_gu, w_down, norm_final)
    return (y_prompt, y_sample, conv_prompt, k_win_prompt, v_win_prompt, conv_sample, k_win_sample, v_win_sample)
```

```python
import contextlib
import os
import types
import numpy as np
import concourse.bass as bass
import concourse.mybir as mybir
from concourse.bass_utils import run_bass_kernel_spmd

F32 = mybir.dt.float32
BF16 = mybir.dt.bfloat16
AF = mybir.ActivationFunctionType
ALU = mybir.AluOpType
AX = mybir.AxisListType.X

ENG_NAMES = ["pe", "act", "dve", "pool", "sp"]
D = 1024
KC = 8
NJ = 22
H = 136
TOK = 2048
NS = 16
NP = 24
XW = 1160
EPS = 1e-6
NEG = -30000.0
FFN_GROUPS = [list(range(0, 6)), list(range(6, 12)), list(range(12, 17)), list(range(17, 22))]
CONV_GROUPS = [list(range(0, 4)), list(range(4, 8))]
GI_A = [0, 1]
GI_FFN = [2, 3, 4, 5]
GI_KV = 6
GI_B = [7, 8]
GI_FINAL = 9


def _freeze(fn, depth=0):
    if not isinstance(fn, types.FunctionType) or fn.__closure__ is None or depth > 4:
        return fn
    cells = []
    for c in fn.__closure__:
        try:
            v = c.cell_contents
        except ValueError:
            cells.append(c)
            continue
        if isinstance(v, types.FunctionType):
            v = _freeze(v, depth + 1)
        cells.append(types.CellType(v))
    return types.FunctionType(fn.__code__, fn.__globals__, fn.__name__, fn.__defaults__, tuple(cells))


class _Stop(Exception):
    pass


_STOP = int(os.environ.get("MK_STOP", "999"))
_SUB = int(os.environ.get("MK_SUB", "255"))


def _stage(k):
    if k >= _STOP:
        raise _Stop()


class Sched:
    def __init__(self):
        self.ops = {e: [] for e in ENG_NAMES}
        self.last_w = {}
        self.rd_eng = {}
        self.rd_dma = {}
        self.dma_cnt = {}
        self.tag = ""

    def op(self, eng, fn, reads=(), writes=(), inc=True, dma=None):
        idx = len(self.ops[eng])
        psk = [k for k in reads if isinstance(k, tuple) and k[0] == "ps"]
        if psk:
            reads = [k for k in reads if k not in psk]
            writes = list(writes) + psk
        deps = []
        for k in reads:
            t = self.last_w.get(k)
            if t is not None:
                deps.append(t)
        for k in writes:
            t = self.last_w.get(k)
            if t is not None:
                deps.append(t)
            for e2, i2 in self.rd_eng.get(k, {}).items():
                deps.append(("op", e2, i2))
            deps.extend(self.rd_dma.get(k, ()))
        if dma is not None:
            self.dma_cnt[dma] = self.dma_cnt.get(dma, 0) + 16
            tok = ("dma", dma, self.dma_cnt[dma])
        else:
            tok = ("op", eng, idx)
        self.ops[eng].append(dict(tag=self.tag, fn=_freeze(fn), deps=deps, inc=(inc and dma is None), dma=dma))
        for k in reads:
            if tok[0] == "op":
                self.rd_eng.setdefault(k, {})[eng] = idx
            else:
                self.rd_dma.setdefault(k, []).append(tok)
        for k in writes:
            self.last_w[k] = tok
            self.rd_eng[k] = {}
            self.rd_dma[k] = []
        return tok

    def finalize(self):
        self.ms = {}
        for e in ENG_NAMES:
            vals = []
            c = 0
            for o in self.ops[e]:
                if o["inc"]:
                    c += 1
                vals.append(c if o["inc"] else None)
            nxt = None
            res = [None] * len(vals)
            for i in range(len(vals) - 1, -1, -1):
                if vals[i] is not None:
                    nxt = vals[i]
                res[i] = nxt
            self.ms[e] = res

    def resolve(self, t):
        if t[0] == "dma":
            return ("dma", t[1]), t[2]
        v = self.ms[t[1]][t[2]]
        assert v is not None, t
        return ("eng", t[1]), v

    def emit(self, nc, es):
        self.finalize()
        self.sems = {}
        for e in ENG_NAMES:
            self.sems[("eng", e)] = es.enter_context(nc.semaphore("sem_" + e))
        for i, d in enumerate(self.dma_cnt):
            self.sems[("dma", d)] = es.enter_context(nc.semaphore("dsem%d" % i))
        block = es.enter_context(nc.Block())

        def mk(ename):
            def body(e):
                waited = {}
                for op in self.ops[ename]:
                    need = {}
                    for t in op["deps"]:
                        if ename == "pe" and t[0] == "op" and t[1] == "pe":
                            continue
                        s, v = self.resolve(t)
                        if need.get(s, 0) < v:
                            need[s] = v
                    for s, v in need.items():
                        if waited.get(s, 0) < v:
                            e.wait_ge(self.sems[s], v)
                            waited[s] = v
                    ins = op["fn"](e)
                    if op["dma"] is not None:
                        ins.then_inc(self.sems[("dma", op["dma"])], 16)
                    elif op["inc"]:
                        ins.then_inc(self.sems[("eng", ename)], 1)
                if ename == "sp":
                    for d, c in self.dma_cnt.items():
                        e.wait_ge(self.sems[("dma", d)], c)
            return body

        block.tensor(mk("pe"))
        block.scalar(mk("act"))
        block.vector(mk("dve"))
        block.gpsimd(mk("pool"))
        block.sync(mk("sp"))


def page_plan():
    pages = []
    idx = {}

    def add(key, spec):
        idx[key] = len(pages)
        pages.append(spec)

    def conv(l):
        for g, grp in enumerate(CONV_GROUPS):
            for j in grp:
                for s in range(3):
                    add(("cin", l, j, s), ("col", "w_in_a", l, s * 1024 + j * 128))
            for j in grp:
                add(("cout", l, j), ("row", "w_out_a", l, j * 128))

    def ffn(li):
        for grp in FFN_GROUPS:
            for j in grp:
                add(("gu", li, j, 0), ("col", "w_gu", li, j * 128))
                add(("gu", li, j, 1), ("col", "w_gu", li, 2816 + j * 128))
            for j in grp:
                add(("dn", li, j), ("row", "w_down", li, j * 128))

    def attn(l):
        for kc in range(8):
            add(("wq", l, kc), ("row", "w_q", l, kc * 128))
        for kc in range(8):
            add(("wo", l, kc), ("row", "w_o", l, kc * 128))

    conv(0); ffn(0); conv(1); ffn(1)
    for q in range(4):
        add(("kv", q), ("kv", "w_kv", None, q))
    attn(0); ffn(2); attn(1); ffn(3)
    return pages, idx


def build_pages_host(inputs):
    pages, _ = page_plan()
    out = np.empty((len(pages), 128, 1024), np.float32)
    for i, (kind, name, l, off) in enumerate(pages):
        W = inputs[name] if l is None else inputs[name][l]
        if kind == "col":
            out[i] = W[:, off:off + 128].reshape(8, 128, 128).transpose(1, 0, 2).reshape(128, 1024)
        elif kind == "row":
            out[i] = W[off:off + 128, :]
        else:
            q = off
            out[i] = W[2 * q * 128:(2 * q + 2) * 128].reshape(2, 128, 512).transpose(1, 0, 2).reshape(128, 1024)
    return out


def build_program():
    nc = bass.Bass("TRN2", target_bir_lowering=False)
    pages, pidx = page_plan()
    NPG = len(pages)

    def din(name, shape):
        return nc.dram_tensor(name, shape, F32, kind="ExternalInput").ap()

    def dout(name, shape):
        return nc.dram_tensor(name, shape, F32, kind="ExternalOutput").ap()

    xp_d = din("xp", [H + TOK, D])
    xs_d = din("xs", [NS, D])
    cconv_d = din("cconv", [2, NS, 2, D])
    ck_d = din("ck", [NS, 128, 256])
    cv_d = din("cv", [NS, 128, 256])
    wp_d = din("wpages", [NPG, 128, 1024])
    gains_d = din("gains", [128, 10 * 8])
    convw_d = din("convw", [128, 2 * 3 * 8])
    sinkb_d = din("sinkb", [128, 32])
    sinkrow_d = din("sinkrow", [128, 2])
    cos_d = din("cost", [128, 17 * 32])
    sin_d = din("sint", [128, 17 * 32])
    scs_d = din("scs", [NS, 64])
    masks_d = din("masks", [128, 512])
    ident_d = din("ident", [128, 128])

    yp_d = dout("yp", [TOK, D])
    ys_d = dout("ys", [NS, D])
    convp_d = dout("convp", [2, 2, D])
    kwinp_d = dout("kwinp", [128, 256])
    vwinp_d = dout("vwinp", [128, 256])
    convs_d = dout("convs", [2, NS, 2, D])
    kwins_d = dout("kwins", [NS, 128, 256])
    vwins_d = dout("vwins", [NS, 128, 256])

    S = Sched()
    global _LAST_SCHED
    _LAST_SCHED = S
    es = contextlib.ExitStack()
    with es:
        def sb(name, shape, dt):
            return es.enter_context(nc.sbuf_tensor(name, shape, dt))

        xT = sb("xT", [128, KC, XW], F32)
        hT = sb("hT", [128, KC, XW], BF16)
        wring = sb("wring", [128, NP, 1024], BF16)
        kT = sb("kT", [128, 4, 9 * 128], BF16)
        vA = sb("vA", [128, 9, 256], BF16)
        sKb = sb("sKb", [128, NS, 256], BF16)
        sVb = sb("sVb", [128, NS, 256], BF16)
        soT = sb("soT", [128, 8, NS], BF16)
        scr = sb("scr", [128, 12, 512], BF16)
        fp = sb("fp", [128, 8, 512], F32)
        stg = sb("stg", [128, 2, 1024], F32)
        sqb = sb("sqb", [128, 2, 512], BF16)
        uext = sb("uext", [128, 2, 516], F32)
        qrot = sb("qrot", [128, 1024], BF16)
        qT = sb("qT", [128, 8, 128], BF16)
        kdup = sb("kdup", [128, 2, 4, 2, 64], BF16)
        NB = 4
        maskb2 = sb("maskb2", [128, 2, 2, 256], BF16)
        PG = sb("PG", [128, NB, 4, 256], BF16)
        svG = sb("svG", [128, NB, 24], F32)
        Pb = PG[:, 0, :, :]
        PT = sb("PT", [128, 4, 256], BF16)
        negsinkb = sb("negsinkb", [128, 32], F32)
        sv = sb("sv", [128, 4, 8], F32)
        cosT = sb("cosT", [128, 17, 32], F32)
        sinT = sb("sinT", [128, 17, 32], F32)
        scs = sb("scs_sb", [NS, 64], F32)
        masks = sb("masks_sb", [128, 2, 256], F32)
        ident_f = sb("ident_f", [128, 128], F32)
        ident_b = sb("ident_b", [128, 128], BF16)
        ones_b = sb("ones_b", [128, 128], BF16)
        gains = sb("gains_sb", [128, 10, 8], F32)
        convw = sb("convw_sb", [128, 2, 3, 8], F32)
        sinkb = sb("sinkb_sb", [128, 32], F32)
        sinkrow = sb("sinkrow_sb", [128, 2], F32)
        uprev = sb("uprev", [128, 2, 8, 2], F32)
        scT = sb("scT", [128, 2, 2, 8, NS], F32)
        usT = sb("usT", [128, 8, NS], F32)
        sqT = sb("sqT", [128, 128], BF16)
        knew = sb("knew", [NS, 2, 256], BF16)
        ps = [es.enter_context(nc.psum_tensor("ps%d" % i, [128, 512], F32)) for i in range(8)]

        def PSK(i):
            return ("ps", i)

        const_keys = []

        def cload(dst, src, key):
            S.op("sp", lambda e, dst=dst, src=src: e.dma_start(out=dst, in_=src), writes=[key], dma="const")
            const_keys.append(key)

        cload(gains[:].rearrange("p a b -> p (a b)"), gains_d, "gains")
        cload(convw[:].rearrange("p a b c -> p (a b c)"), convw_d, "convw")
        cload(sinkb[:], sinkb_d, "sinkb")
        cload(sinkrow[:], sinkrow_d, "sinkrow")
        cload(cosT[:].rearrange("p a b -> p (a b)"), cos_d, "cosT")
        cload(sinT[:].rearrange("p a b -> p (a b)"), sin_d, "sinT")
        cload(scs[:], scs_d, "scs")
        cload(masks[:].rearrange("p a b -> p (a b)"), masks_d, "masks")
        cload(ident_f[:], ident_d, "ident_f")
        for k in const_keys:
            S.last_w[k] = ("dma", "const", S.dma_cnt["const"])
        S.op("dve", lambda e: e.tensor_copy(out=ident_b[:], in_=ident_f[:]), reads=["ident_f"], writes=["ident_b"])
        S.op("dve", lambda e: e.memset(ones_b[:], 1.0), writes=["ones_b"])
        for dup_ in range(2):
            S.op("dve", lambda e, dup_=dup_: e.tensor_scalar(out=maskb2[:, :, dup_, :], in0=masks[:], scalar1=8.0, scalar2=None, op0=ALU.mult),
                 reads=["masks"], writes=["maskb"])
        S.op("dve", lambda e: e.tensor_scalar(out=negsinkb[:], in0=sinkb[:], scalar1=-1.0, scalar2=None, op0=ALU.mult), reads=["sinkb"], writes=["negsinkb"])
        S.op("dve", lambda e: e.memset(uprev[:].rearrange("p a b c -> p (a b c)"), 0.0), writes=["uprev"])

        st = dict(next_load=0, slot_page={}, extra=[])

        def sample_cache_load(n_):
            def run():
                S.op("pool", lambda e: e.dma_start(out=sKb[0:127, n_, :], in_=ck_d[n_, 1:128, :]), writes=[("sKb", n_)], dma=("sc", n_))
                S.op("pool", lambda e: e.dma_start(out=sVb[0:127, n_, :], in_=cv_d[n_, 1:128, :]), writes=[("sVb", n_)], dma=("sc", n_))
                S.last_w[("sKb", n_)] = ("dma", ("sc", n_), 32)
                S.last_w[("sVb", n_)] = ("dma", ("sc", n_), 32)
            return run
        for n_ in range(NS):
            st["extra"].append(sample_cache_load(n_))

        def page(seq):
            while st["next_load"] <= seq:
                i = st["next_load"]
                slot = i % NP
                src = wp_d[i % NPG]
                S.op("pool", lambda e, slot=slot, src=src: e.dma_start(out=wring[:, slot, :], in_=src),
                     writes=[("wp", slot)], dma=("wp", slot))
                st["slot_page"][slot] = i
                st["next_load"] += 1
                if st["extra"] and i >= 4:
                    st["extra"].pop(0)()
            slot = seq % NP
            assert st["slot_page"][slot] == seq, (seq, slot, st["slot_page"][slot])
            return slot, ("wp", slot)

        def norm(c0, n, gi, out_fn, out_keys):
            for kc in range(KC):
                sl = kc % 2
                S.op("act", lambda e, kc=kc, sl=sl: e.activation(out=sqb[:, sl, 0:n], in_=xT[:, kc, c0:c0 + n], func=AF.Square),
                     reads=[("x", c0)], writes=[("sqb", sl)])
                S.op("pe", lambda e, kc=kc, sl=sl: e.matmul(ps[7][:, 0:n], lhsT=ones_b[:], rhs=sqb[:, sl, 0:n],
                                                              start=(kc == 0), stop=(kc == KC - 1)),
                     reads=[("sqb", sl), "ones_b"], writes=[PSK(7)])
            S.op("act", lambda e: e.activation(out=fp[:, 0, 0:n], in_=ps[7][:, 0:n], func=AF.Sqrt, scale=1.0 / D, bias=EPS),
                 reads=[PSK(7)], writes=[("fp", 0)])
            S.op("dve", lambda e: e.reciprocal(out=fp[:, 1, 0:n], in_=fp[:, 0, 0:n]), reads=[("fp", 0)], writes=[("fp", 1)])
            for kc in range(KC):
                S.op("dve", lambda e, kc=kc: e.scalar_tensor_tensor(out=out_fn(kc), in0=xT[:, kc, c0:c0 + n],
                                                                      scalar=gains[:, gi, kc:kc + 1], in1=fp[:, 1, 0:n],
                                                                      op0=ALU.mult, op1=ALU.mult),
                     reads=[("x", c0), ("fp", 1), "gains"], writes=out_keys)

        def norm_h(c0, n, gi):
            norm(c0, n, gi, lambda kc: hT[:, kc, c0:c0 + n], [("h", c0)])

        def proj_fm(bank, n, pslot, pkey, sub, rhs_fn, rkeys, nk=KC, first=True, last=True, lhs_cols=None, co=0):
            for kc in range(nk):
                S.op("pe", lambda e, kc=kc: e.matmul(ps[bank][:, co:co + n], lhsT=wring[:, pslot, kc * 128:(kc + 1) * 128],
                                                       rhs=rhs_fn(kc), start=(first and kc == 0), stop=(last and kc == nk - 1)),
                     reads=[pkey] + rkeys, writes=[PSK(bank)], inc=(kc == nk - 1))

        def resid_add(bank, oc, c0, n):
            S.op("dve", lambda e: e.tensor_tensor(out=xT[:, oc, c0:c0 + n], in0=ps[bank][:, 0:n], in1=xT[:, oc, c0:c0 + n], op=ALU.add),
                 reads=[PSK(bank), ("x", c0)], writes=[("x", c0)])

        cnt = dict(item=0, pair=0, trip=0, s2=0, ait=0, kvb=0)
        xkeys = set()

        pending = []

        def flush_pending(keep=0):
            while len(pending) > keep:
                pending.pop(0)()

        def down_stage(c0, n, par, grp, row_key_fn, pbase, after=None):
            def run_small():
                bank = 6 + (cnt["s2"] % 2)
                cnt["s2"] += 1
                for oc in range(KC):
                    for jj, j in enumerate(grp):
                        pslot, pkey = page(pbase + pidx[row_key_fn(j)])
                        S.op("pe", lambda e, jj=jj, pslot=pslot, oc=oc: e.matmul(
                            ps[bank][:, oc * n:(oc + 1) * n], lhsT=wring[:, pslot, oc * 128:(oc + 1) * 128], rhs=scr[:, par * 6 + jj, 0:n],
                            start=(jj == 0), stop=(jj == len(grp) - 1)),
                            reads=[pkey, ("scr", par * 6 + jj)], writes=[PSK(bank)], inc=(jj == len(grp) - 1))
                S.op("dve", lambda e: e.tensor_tensor(out=xT[:, :, c0:c0 + n], in0=ps[bank][:, 0:KC * n].rearrange("p (o t) -> p o t", t=n),
                                                      in1=xT[:, :, c0:c0 + n], op=ALU.add),
                     reads=[PSK(bank), ("x", c0)], writes=[("x", c0)])
                if after is not None:
                    after()
            if n <= 32:
                return run_small

            def run():
                for oc in range(KC):
                    bank = 6 + (cnt["s2"] % 2)
                    cnt["s2"] += 1
                    for jj, j in enumerate(grp):
                        pslot, pkey = page(pbase + pidx[row_key_fn(j)])
                        S.op("pe", lambda e, jj=jj, pslot=pslot, oc=oc, bank=bank: e.matmul(
                            ps[bank][:, 0:n], lhsT=wring[:, pslot, oc * 128:(oc + 1) * 128], rhs=scr[:, par * 6 + jj, 0:n],
                            start=(jj == 0), stop=(jj == len(grp) - 1)),
                            reads=[pkey, ("scr", par * 6 + jj)], writes=[PSK(bank)], inc=(jj == len(grp) - 1))
                    resid_add(bank, oc, c0, n)
                if after is not None:
                    after()
            return run

        def ffn_layer(li, tiles, pbase, skip_norm=False, after_last=None):
            if not skip_norm:
                for (c0, n, kind) in tiles:
                    norm_h(c0, n, GI_FFN[li])
            for grp in FFN_GROUPS:
                for (c0, n, kind) in tiles:
                    par = cnt["item"] % 2
                    cnt["item"] += 1
                    if n <= 32:
                        G = len(grp)
                        pr = cnt["pair"] % 3
                        cnt["pair"] += 1
                        bg, bu = 2 * pr, 2 * pr + 1
                        for jj, j in enumerate(grp):
                            for s_, bank in ((0, bg), (1, bu)):
                                pslot, pkey = page(pbase + pidx[("gu", li, j, s_)])
                                proj_fm(bank, n, pslot, pkey, s_, lambda kc: hT[:, kc, c0:c0 + n], [("h", c0)], co=jj * n)
                        sl = 5 + (cnt["pair"] % 2)
                        S.op("act", lambda e: e.activation(out=fp[:, sl, 0:G * n], in_=ps[bg][:, 0:G * n], func=AF.Silu),
                             reads=[PSK(bg)], writes=[("fp", sl)])
                        S.op("dve", lambda e: e.tensor_tensor(out=scr[:, par * 6:par * 6 + G, 0:n], in0=ps[bu][:, 0:G * n].rearrange("p (g t) -> p g t", t=n),
                                                              in1=fp[:, sl, 0:G * n].rearrange("p (g t) -> p g t", t=n), op=ALU.mult),
                             reads=[PSK(bu), ("fp", sl)], writes=[("scr", par * 6 + jj) for jj in range(G)])
                        flush_pending(0)
                        aft = None
                        if after_last is not None and grp is FFN_GROUPS[-1]:
                            aft = (lambda c0=c0, n=n, kind=kind: after_last(c0, n, kind))
                        pending.append(down_stage(c0, n, par, grp, lambda j: ("dn", li, j), pbase, aft))
                        continue
                    for jj, j in enumerate(grp):
                        pr = cnt["pair"] % 3
                        cnt["pair"] += 1
                        bg, bu = 2 * pr, 2 * pr + 1
                        for s, bank in ((0, bg), (1, bu)):
                            pslot, pkey = page(pbase + pidx[("gu", li, j, s)])
                            proj_fm(bank, n, pslot, pkey, s, lambda kc: hT[:, kc, c0:c0 + n], [("h", c0)])
                        sl = 5 + (cnt["pair"] % 2)
                        S.op("act", lambda e, bg=bg, sl=sl: e.activation(out=fp[:, sl, 0:n], in_=ps[bg][:, 0:n], func=AF.Silu),
                             reads=[PSK(bg)], writes=[("fp", sl)])
                        S.op("dve", lambda e, bu=bu, sl=sl, jj=jj: e.tensor_tensor(out=scr[:, par * 6 + jj, 0:n], in0=ps[bu][:, 0:n],
                                                                                     in1=fp[:, sl, 0:n], op=ALU.mult),
                             reads=[PSK(bu), ("fp", sl)], writes=[("scr", par * 6 + jj)])
                    flush_pending(0)
                    aft = None
                    if after_last is not None and grp is FFN_GROUPS[-1]:
                        aft = (lambda c0=c0, n=n, kind=kind: after_last(c0, n, kind))
                    pending.append(down_stage(c0, n, par, grp, lambda j: ("dn", li, j), pbase, aft))
            flush_pending(0)

        def conv_layer(l, tiles, pbase, is_last_pass, skip_norm=False, after_last=None):
            if not skip_norm:
                for (c0, n, kind) in tiles:
                    norm_h(c0, n, GI_A[l])
            for grp in CONV_GROUPS:
                for (c0, n, kind) in tiles:
                    par = cnt["item"] % 2
                    cnt["item"] += 1
                    if kind == "sample":
                        G = len(grp)
                        j0 = grp[0]
                        tr = cnt["trip"] % 2
                        cnt["trip"] += 1
                        banks = [3 * tr, 3 * tr + 1, 3 * tr + 2]
                        bb, bc, bx = banks
                        for jj, j in enumerate(grp):
                            for s_ in (1, 2, 0):
                                pslot, pkey = page(pbase + pidx[("cin", l, j, s_)])
                                proj_fm(banks[s_], n, pslot, pkey, s_, lambda kc: hT[:, kc, c0:c0 + n], [("h", c0)], co=jj * n)

                        def v3(ap):
                            return ap.rearrange("p (g t) -> p g t", t=n)

                        def wb(r_):
                            return convw[:, l, r_, j0:j0 + G].unsqueeze(2).to_broadcast([128, G, n])
                        ukeys = [("usT", j) for j in grp]
                        S.op("act", lambda e: e.copy(out=fp[:, 2, 0:G * n], in_=ps[bc][:, 0:G * n]), reads=[PSK(bc)], writes=[("fp", 2)])
                        S.op("dve", lambda e: e.tensor_tensor(out=usT[:, j0:j0 + G, :], in0=v3(ps[bx][:, 0:G * n]), in1=v3(fp[:, 2, 0:G * n]), op=ALU.mult),
                             reads=[PSK(bx), ("fp", 2)], writes=ukeys)
                        S.op("dve", lambda e: e.tensor_tensor(out=v3(fp[:, 3, 0:G * n]), in0=scT[:, l, 0, j0:j0 + G, :], in1=wb(0), op=ALU.mult),
                             reads=["scT", "convw"], writes=[("fp", 3)])
                        S.op("dve", lambda e: e.tensor_tensor(out=v3(fp[:, 4, 0:G * n]), in0=scT[:, l, 1, j0:j0 + G, :], in1=wb(1), op=ALU.mult),
                             reads=["scT", "convw"], writes=[("fp", 4)])
                        S.op("dve", lambda e: e.tensor_tensor(out=fp[:, 3, 0:G * n], in0=fp[:, 3, 0:G * n], in1=fp[:, 4, 0:G * n], op=ALU.add),
                             reads=[("fp", 3), ("fp", 4)], writes=[("fp", 3)])
                        S.op("dve", lambda e: e.tensor_tensor(out=v3(fp[:, 4, 0:G * n]), in0=usT[:, j0:j0 + G, :], in1=wb(2), op=ALU.mult),
                             reads=ukeys + ["convw"], writes=[("fp", 4)])
                        S.op("dve", lambda e: e.tensor_tensor(out=fp[:, 3, 0:G * n], in0=fp[:, 3, 0:G * n], in1=fp[:, 4, 0:G * n], op=ALU.add),
                             reads=[("fp", 3), ("fp", 4)], writes=[("fp", 3)])
                        S.op("dve", lambda e: e.tensor_tensor(out=scr[:, par * 6:par * 6 + G, 0:n], in0=v3(ps[bb][:, 0:G * n]), in1=v3(fp[:, 3, 0:G * n]), op=ALU.mult),
                             reads=[PSK(bb), ("fp", 3)], writes=[("scr", par * 6 + jj) for jj in range(G)])
                        flush_pending(0)
                        aft = None
                        if after_last is not None and grp is CONV_GROUPS[-1]:
                            aft = (lambda c0=c0, n=n, kind=kind: after_last(c0, n, kind))
                        pending.append(down_stage(c0, n, par, grp, lambda j: ("cout", l, j), pbase, aft))
                        continue
                    for jj, j in enumerate(grp):
                        tr = cnt["trip"] % 2
                        cnt["trip"] += 1
                        banks = [3 * tr, 3 * tr + 1, 3 * tr + 2]
                        for s in (1, 2, 0):
                            pslot, pkey = page(pbase + pidx[("cin", l, j, s)])
                            proj_fm(banks[s], n, pslot, pkey, s, lambda kc: hT[:, kc, c0:c0 + n], [("h", c0)])
                        bb, bc, bx = banks
                        ub = tr
                        S.op("act", lambda e, bc=bc: e.copy(out=fp[:, 2, 0:n], in_=ps[bc][:, 0:n]), reads=[PSK(bc)], writes=[("fp", 2)])
                        if kind == "sample":
                            S.op("dve", lambda e, bx=bx, j=j: e.tensor_tensor(out=usT[:, j, :], in0=ps[bx][:, 0:n], in1=fp[:, 2, 0:n], op=ALU.mult),
                                 reads=[PSK(bx), ("fp", 2)], writes=[("usT", j)])
                            a0 = scT[:, l, 0, j, :]
                            a1 = scT[:, l, 1, j, :]
                            a2 = usT[:, j, :]
                            rk = ["scT", ("usT", j)]
                        else:
                            S.op("dve", lambda e, bx=bx, ub=ub: e.tensor_tensor(out=uext[:, ub, 2:2 + n], in0=ps[bx][:, 0:n], in1=fp[:, 2, 0:n], op=ALU.mult),
                                 reads=[PSK(bx), ("fp", 2)], writes=[("uext", ub)])
                            S.op("act", lambda e, ub=ub, j=j: e.copy(out=uext[:, ub, 0:2], in_=uprev[:, l, j, :]),
                                 reads=[("uprev", l, j), "uprev"], writes=[("uext", ub)])
                            S.op("act", lambda e, ub=ub, j=j: e.copy(out=uprev[:, l, j, :], in_=uext[:, ub, n:n + 2]),
                                 reads=[("uext", ub)], writes=[("uprev", l, j)])
                            a0 = uext[:, ub, 0:n]
                            a1 = uext[:, ub, 1:n + 1]
                            a2 = uext[:, ub, 2:n + 2]
                            rk = [("uext", ub)]
                        S.op("act", lambda e, a0=a0, j=j: e.activation(out=fp[:, 3, 0:n], in_=a0, func=AF.Copy, scale=convw[:, l, 0, j:j + 1]),
                             reads=rk + ["convw"], writes=[("fp", 3)])
                        S.op("dve", lambda e, a1=a1, j=j: e.scalar_tensor_tensor(out=fp[:, 4, 0:n], in0=a1, scalar=convw[:, l, 1, j:j + 1], in1=fp[:, 3, 0:n],
                                                                                   op0=ALU.mult, op1=ALU.add),
                             reads=rk + ["convw", ("fp", 3)], writes=[("fp", 4)])
                        S.op("dve", lambda e, a2=a2, j=j: e.scalar_tensor_tensor(out=fp[:, 3, 0:n], in0=a2, scalar=convw[:, l, 2, j:j + 1], in1=fp[:, 4, 0:n],
                                                                                   op0=ALU.mult, op1=ALU.add),
                             reads=rk + ["convw", ("fp", 4)], writes=[("fp", 3)])
                        S.op("dve", lambda e, bb=bb, jj=jj: e.tensor_tensor(out=scr[:, par * 6 + jj, 0:n], in0=ps[bb][:, 0:n], in1=fp[:, 3, 0:n], op=ALU.mult),
                             reads=[PSK(bb), ("fp", 3)], writes=[("scr", par * 6 + jj)])
                    flush_pending(0)
                    aft = None
                    if after_last is not None and grp is CONV_GROUPS[-1]:
                        aft = (lambda c0=c0, n=n, kind=kind: after_last(c0, n, kind))
                    pending.append(down_stage(c0, n, par, grp, lambda j: ("cout", l, j), pbase, aft))
            flush_pending(0)
            if is_last_pass:
                for r_ in range(2):
                    dst = convp_d[l, r_].rearrange("(j p) -> p j", p=128)
                    S.op("sp", lambda e: e.dma_start(out=dst, in_=uprev[:, l, :, r_], allow_slow_non_contiguous=True),
                         reads=[("uprev", l, j) for j in range(8)], dma="o_convp")
                S.op("sp", lambda e: e.dma_start(out=convs_d[l, :, 0, :], in_=cconv_d[l, :, 1, :]), dma="o_convs0")
                for half in range(2):
                    for q in range(4):
                        j = half * 4 + q
                        S.op("pe", lambda e, j=j, half=half, q=q: e.transpose(out=ps[half][0:NS, q * 128:(q + 1) * 128], in_=usT[:, j, :], identity=ident_f[:]),
                             reads=[("usT", j), "ident_f"], writes=[PSK(half)])
                    S.op("act", lambda e, half=half: e.copy(out=stg[0:NS, 0, half * 512:(half + 1) * 512], in_=ps[half][0:NS, :]),
                         reads=[PSK(half)], writes=[("stg", 0)])
                S.op("sp", lambda e: e.dma_start(out=convs_d[l, :, 1, :], in_=stg[0:NS, 0, :]), reads=[("stg", 0)], dma=("stg", 0))

        def rope_tm(bank, rows, nh, cos_ap, sin_ap, out1, out2, srckeys, outkeys):
            v = ps[bank][0:rows, 0:nh * 64].rearrange("p (h t f) -> p h t f", t=2, f=32)
            x1 = v[:, :, 0, :]
            x2 = v[:, :, 1, :]
            cb = cos_ap.to_broadcast([rows, nh, 32])
            sbb = sin_ap.to_broadcast([rows, nh, 32])
            W = nh * 32

            def t(i):
                return fp[0:rows, i, 0:W].rearrange("p (h f) -> p h f", f=32)
            S.op("dve", lambda e: e.tensor_tensor(out=t(2), in0=x1, in1=cb, op=ALU.mult), reads=[PSK(bank)] + srckeys, writes=[("fp", 2)])
            S.op("dve", lambda e: e.tensor_tensor(out=t(3), in0=x2, in1=sbb, op=ALU.mult), reads=[PSK(bank)] + srckeys, writes=[("fp", 3)])
            S.op("dve", lambda e: e.tensor_tensor(out=out1, in0=t(2), in1=t(3), op=ALU.subtract), reads=[("fp", 2), ("fp", 3)], writes=outkeys)
            S.op("dve", lambda e: e.tensor_tensor(out=t(4), in0=x2, in1=cb, op=ALU.mult), reads=[PSK(bank)] + srckeys, writes=[("fp", 4)])
            S.op("dve", lambda e: e.tensor_tensor(out=t(7), in0=x1, in1=sbb, op=ALU.mult), reads=[PSK(bank)] + srckeys, writes=[("fp", 7)])
            S.op("dve", lambda e: e.tensor_tensor(out=out2, in0=t(4), in1=t(7), op=ALU.add), reads=[("fp", 4), ("fp", 7)], writes=outkeys)

        kv_pending = []

        def kv_flush():
            while kv_pending:
                kv_pending.pop(0)()

        def kv_block(cb, tb, ks, pbase, write_win, hkey):
            par = cnt["kvb"] % 2
            cnt["kvb"] += 1
            b0, b1 = 2 * par, 2 * par + 1
            ksl = 5 + par
            for kc in range(KC):
                pslot, pkey = page(pbase + pidx[("kv", kc // 2)])
                S.op("pe", lambda e, kc=kc, pslot=pslot: e.matmul(ps[b0][:, 0:512], lhsT=hT[:, kc, cb:cb + 128],
                                                                    rhs=wring[:, pslot, (kc % 2) * 512:(kc % 2) * 512 + 512],
                                                                    start=(kc == 0), stop=(kc == KC - 1)),
                     reads=[pkey, hkey], writes=[PSK(b0)], inc=(kc == KC - 1))
            kv_flush()
            kf = fp[:, ksl, 0:256].rearrange("p (h t f) -> p h t f", t=2, f=32)
            S.op("act", lambda e: e.copy(out=vA[:, ks, :], in_=ps[b0][:, 256:512]), reads=[PSK(b0)], writes=[("vA", ks)])
            if write_win:
                S.op("act", lambda e: e.copy(out=fp[:, 1, 0:256], in_=ps[b0][:, 256:512]), reads=[PSK(b0)], writes=[("fp", 1)])
                S.op("sp", lambda e: e.dma_start(out=vwinp_d, in_=fp[:, 1, 0:256]), reads=[("fp", 1)], dma="o_vwinp")
            rope_tm(b0, 128, 4, cosT[:, tb:tb + 1, :], sinT[:, tb:tb + 1, :], kf[:, :, 0, :], kf[:, :, 1, :], ["cosT", "sinT"], [("fp", ksl)])
            kf2 = fp[:, ksl, 0:256].rearrange("p (h d) -> p h d", d=64)
            for dup in range(2):
                S.op("act", lambda e, dup=dup: e.copy(out=kdup[:, par, :, dup, :], in_=kf2), reads=[("fp", ksl)], writes=[("kdup", par)])
            if write_win:
                S.op("sp", lambda e: e.dma_start(out=kwinp_d, in_=fp[:, ksl, 0:256]), reads=[("fp", ksl)], dma="o_kwinp")

            def stage2():
                pb = ps[b1][:, 0:256].bitcast(BF16)
                for j in range(4):
                    S.op("pe", lambda e, j=j: e.transpose(out=pb[:, j * 128:(j + 1) * 128], in_=kdup[:, par, j, :, :].rearrange("p a b -> p (a b)"), identity=ident_b[:]),
                         reads=[("kdup", par), "ident_b"], writes=[PSK(b1)])
                S.op("act", lambda e: e.copy(out=kT[:, :, ks * 128:(ks + 1) * 128], in_=pb.rearrange("p (j t) -> p j t", t=128)),
                     reads=[PSK(b1)], writes=[("kT", ks)])
            kv_pending.append(stage2)

        def kv_sample(cs, pbase, hkey):
            for kc in range(KC):
                pslot, pkey = page(pbase + pidx[("kv", kc // 2)])
                S.op("pe", lambda e, kc=kc, pslot=pslot: e.matmul(ps[0][0:NS, 0:512], lhsT=hT[:, kc, cs:cs + NS],
                                                                    rhs=wring[:, pslot, (kc % 2) * 512:(kc % 2) * 512 + 512],
                                                                    start=(kc == 0), stop=(kc == KC - 1)),
                     reads=[pkey, hkey], writes=[PSK(0)], inc=(kc == KC - 1))
            kf = fp[0:NS, 0, 0:256].rearrange("p (h t f) -> p h t f", t=2, f=32)
            rope_tm(0, NS, 4, scs[:, 0:32].rearrange("p (o f) -> p o f", o=1), scs[:, 32:64].rearrange("p (o f) -> p o f", o=1),
                    kf[:, :, 0, :], kf[:, :, 1, :], ["scs"], [("fp", 0)])
            S.op("act", lambda e: e.copy(out=fp[0:NS, 1, 0:256], in_=ps[0][0:NS, 256:512]), reads=[PSK(0)], writes=[("fp", 1)])
            S.op("act", lambda e: e.copy(out=knew[:, 0, :], in_=fp[0:NS, 0, 0:256]), reads=[("fp", 0)], writes=["knew0"])
            S.op("act", lambda e: e.copy(out=knew[:, 1, :], in_=fp[0:NS, 1, 0:256]), reads=[("fp", 1)], writes=["knew1"])
            if _SUB & 1:
                S.op("sp", lambda e: e.dma_start(out=kwins_d[:, 0:127, :], in_=ck_d[:, 1:128, :]), dma="o_kw0")
                S.op("sp", lambda e: e.dma_start(out=vwins_d[:, 0:127, :], in_=cv_d[:, 1:128, :]), dma="o_vw0")
            if _SUB & 2:
                S.op("sp", lambda e: e.dma_start(out=kwins_d[:, 127, :], in_=fp[0:NS, 0, 0:256]), reads=[("fp", 0)], dma="o_kw1")
                S.op("sp", lambda e: e.dma_start(out=vwins_d[:, 127, :], in_=fp[0:NS, 1, 0:256]), reads=[("fp", 1)], dma="o_vw1")
            if not (_SUB & 4):
                return
            for n_ in range(NS):
                S.op("sp", lambda e, n_=n_: e.dma_start(out=sKb[127:128, n_, :], in_=knew[n_:n_ + 1, 0, :]), reads=["knew0"], writes=[("sKb", n_)], dma=("sc2", n_ % 4))
                S.op("sp", lambda e, n_=n_: e.dma_start(out=sVb[127:128, n_, :], in_=knew[n_:n_ + 1, 1, :]), reads=["knew1"], writes=[("sVb", n_)], dma=("sc2", n_ % 4))
            for n_ in range(NS):
                S.last_w[("sKb", n_)] = ("dma", ("sc2", n_ % 4), S.dma_cnt[("sc2", n_ % 4)])
                S.last_w[("sVb", n_)] = ("dma", ("sc2", n_ % 4), S.dma_cnt[("sc2", n_ % 4)])

        def attn_prologue(l, cb, tb, pbase, hkey):
            for half in range(2):
                for kc in range(KC):
                    pslot, pkey = page(pbase + pidx[("wq", l, kc)])
                    S.op("pe", lambda e, kc=kc, pslot=pslot, half=half: e.matmul(ps[6 + half][:, 0:512], lhsT=hT[:, kc, cb:cb + 128],
                                                                                   rhs=wring[:, pslot, half * 512:(half + 1) * 512],
                                                                                   start=(kc == 0), stop=(kc == KC - 1)),
                         reads=[pkey, hkey], writes=[PSK(6 + half)], inc=(kc == KC - 1))
            for half in range(2):
                qv = qrot[:, half * 512:(half + 1) * 512].rearrange("p (h t f) -> p h t f", t=2, f=32)
                rope_tm(6 + half, 128, 8, cosT[:, tb:tb + 1, :], sinT[:, tb:tb + 1, :], qv[:, :, 0, :], qv[:, :, 1, :], ["cosT", "sinT"], ["qrot"])

        def attn_prologue2():
            pb6 = ps[6][:, :].bitcast(BF16)
            for c in range(8):
                S.op("pe", lambda e, c=c: e.transpose(out=pb6[:, c * 128:(c + 1) * 128], in_=qrot[:, c * 128:(c + 1) * 128], identity=ident_b[:]),
                     reads=["qrot", "ident_b"], writes=[PSK(6)])
            S.op("act", lambda e: e.copy(out=qT[:].rearrange("p a b -> p (a b)"), in_=pb6), reads=[PSK(6)], writes=["qT"])

        def attn_group(l, j, ks, mi, it):
            bf = it % NB
            par = it % 2
            bA, bB = 2 * par, 2 * par + 1
            gb_ = []
            for bank in (bA, bB):
                S.op("pe", lambda e, bank=bank: e.matmul(ps[bank][:, :], lhsT=ident_b[:], rhs=maskb2[:, mi, :, :].rearrange("p a b -> p (a b)"),
                                                         start=True, stop=False),
                     reads=["ident_b", "maskb"], writes=[PSK(bank)], inc=False)
            for g in range(4):
                h = 4 * j + g
                base = (g % 2) * 64
                bank = bA if g % 2 == 0 else bB
                half = g // 2
                gb_.append((bank, half))
                S.op("pe", lambda e, h=h, base=base, bank=bank, half=half, g=g: e.matmul(
                    ps[bank][:, half * 256:(half + 1) * 256], lhsT=qT[base:base + 64, h // 2, :],
                    rhs=kT[base:base + 64, j, (ks - 1) * 128:(ks + 1) * 128], start=False, stop=(g >= 2)),
                    reads=["qT", ("kT", ks - 1), ("kT", ks)], writes=[PSK(bank)])
            v = svG[:, bf, :]
            sk = sinkb[:, l * 16 + 4 * j:l * 16 + 4 * j + 4]
            nsk = negsinkb[:, l * 16 + 4 * j:l * 16 + 4 * j + 4]
            mv = v[:, 0:4].rearrange("q (hf p) -> q hf p", p=2)
            for pr, bank in ((0, bA), (1, bB)):
                S.op("dve", lambda e, pr=pr, bank=bank: e.reduce_max(out=mv[:, :, pr], in_=ps[bank][:, :].rearrange("q (hf s) -> q hf s", s=256), axis=AX),
                     reads=[PSK(bank)], writes=[("sv0", bf, pr)])
            S.op("dve", lambda e: e.scalar_tensor_tensor(out=v[:, 4:8], in0=v[:, 0:4], scalar=-0.125, in1=nsk, op0=ALU.mult, op1=ALU.min),
                 reads=[("sv0", bf, 0), ("sv0", bf, 1), "negsinkb"], writes=[("sv1", bf)])
            for g in range(4):
                bank, half = gb_[g]
                S.op("act", lambda e, g=g, bank=bank, half=half: e.activation(out=PG[:, bf, g, :], in_=ps[bank][:, half * 256:(half + 1) * 256], func=AF.Exp,
                                                                                bias=v[:, 4 + g:5 + g], scale=0.125, accum_out=v[:, 8 + g:9 + g]),
                     reads=[PSK(bank), ("sv1", bf)], writes=[("PG", bf), ("sv2", bf, g)])
            S.op("dve", lambda e: e.tensor_tensor(out=v[:, 12:16], in0=v[:, 4:8], in1=sk, op=ALU.add), reads=[("sv1", bf), "sinkb"], writes=[("sv3", bf)])
            S.op("act", lambda e: e.activation(out=v[:, 16:20], in_=v[:, 12:16], func=AF.Exp), reads=[("sv3", bf)], writes=[("sv4", bf)])

        def attn_group_b(it):
            bf = it % NB
            v = svG[:, bf, :]
            S.op("dve", lambda e: e.tensor_tensor(out=v[:, 20:24], in0=v[:, 8:12], in1=v[:, 16:20], op=ALU.add),
                 reads=[("sv2", bf, g) for g in range(4)] + [("sv4", bf)], writes=[("sv5", bf)])
            S.op("dve", lambda e: e.reciprocal(out=v[:, 20:24], in_=v[:, 20:24]), reads=[("sv5", bf)], writes=[("sv5", bf)])
            S.op("dve", lambda e: e.tensor_tensor(out=PG[:, bf, :, :], in0=PG[:, bf, :, :], in1=v[:, 20:24].unsqueeze(2).to_broadcast([128, 4, 256]), op=ALU.mult),
                 reads=[("PG", bf), ("sv5", bf)], writes=[("PG", bf)])

        def attn_tail(j, ks, bi, it):
            bf = it % NB
            pb4 = ps[4][:, :].bitcast(BF16)
            for g in range(4):
                for kb in range(2):
                    S.op("pe", lambda e, g=g, kb=kb: e.transpose(out=pb4[:, (g * 2 + kb) * 128:(g * 2 + kb + 1) * 128], in_=PG[:, bf, g, kb * 128:(kb + 1) * 128],
                                                                 identity=ident_b[:]),
                         reads=[("PG", bf), "ident_b"], writes=[PSK(4)])
            S.op("dve", lambda e: e.tensor_copy(out=PT[:].rearrange("p a b -> p (a b)"), in_=pb4), reads=[PSK(4)], writes=["PT"])

        def attn_tail_pv(j, ks, bi, it):
            hb = it % 2
            for g in range(4):
                base = (g % 2) * 64
                oo = ps[5][base:base + 64, (hb * 2 + g // 2) * 128:(hb * 2 + g // 2 + 1) * 128]
                for kb in range(2):
                    S.op("pe", lambda e, g=g, kb=kb, oo=oo, base=base: e.matmul(oo, lhsT=vA[:, ks - 1 + kb, j * 64:(j + 1) * 64], rhs=PT[:, g, kb * 128:(kb + 1) * 128],
                                                                             start=(kb == 0), stop=(kb == 1), tile_position=(0, base)),
                         reads=[("vA", ks - 1 + kb), "PT"], writes=[PSK(5)])
            S.op("act", lambda e: e.copy(out=scr[:, 2 * j:2 * j + 2, bi * 128:(bi + 1) * 128],
                                         in_=ps[5][:, hb * 256:(hb + 1) * 256].rearrange("p (c q) -> p c q", q=128)),
                 reads=[PSK(5)], writes=[("scr", 2 * j), ("scr", 2 * j + 1)])

        def attn_tile(l, c0, tile_blocks, pbase, hkey, first_done=0, next_first=None):
            LAGG = NB - 1
            items = []
            for (cb, tb, ks, mi, bi) in tile_blocks:
                for j in range(4):
                    items.append((cb, tb, ks, mi, bi, j))
            if first_done < 1:
                attn_prologue(l, tile_blocks[0][0], tile_blocks[0][1], pbase, hkey)
            if first_done < 2:
                attn_prologue2()
            for i in range(len(items) + LAGG):
                if i >= LAGG:
                    (cb, tb, ks, mi, bi, j) = items[i - LAGG]
                    attn_tail(j, ks, bi, cnt["ait"] + i - LAGG)
                if i < len(items):
                    (cb, tb, ks, mi, bi, j) = items[i]
                    attn_group(l, j, ks, mi, cnt["ait"] + i)
                if 1 <= i <= len(items):
                    attn_group_b(cnt["ait"] + i - 1)
                if i >= LAGG:
                    (cb, tb, ks, mi, bi, j) = items[i - LAGG]
                    attn_tail_pv(j, ks, bi, cnt["ait"] + i - LAGG)
                if i < len(items):
                    (cb, tb, ks, mi, bi, j) = items[i]
                    if bi + 1 < len(tile_blocks):
                        nb_ = tile_blocks[bi + 1]
                        if j == 1:
                            attn_prologue(l, nb_[0], nb_[1], pbase, hkey)
                        if j == 3:
                            attn_prologue2()
                    elif next_first is not None:
                        if j == 1:
                            attn_prologue(l, next_first[0], next_first[1], pbase, next_first[2])
                        if j == 3:
                            attn_prologue2()
            cnt["ait"] += len(items)

        def wo_proj(l, c0, n, pbase, sample=False):
            for oc in range(KC):
                bank = 6 + (cnt["s2"] % 2)
                cnt["s2"] += 1
                for kc in range(KC):
                    pslot, pkey = page(pbase + pidx[("wo", l, kc)])
                    S.op("pe", lambda e, kc=kc, pslot=pslot, bank=bank, oc=oc: e.matmul(ps[bank][:, 0:n], lhsT=wring[:, pslot, oc * 128:(oc + 1) * 128],
                                                                                          rhs=(soT[:, kc, :] if sample else scr[:, kc, 0:n]), start=(kc == 0), stop=(kc == KC - 1)),
                         reads=[pkey, ("soT" if sample else ("scr", kc))], writes=[PSK(bank)], inc=(kc == KC - 1))
                resid_add(bank, oc, c0, n)

        def attn_sample(l, cs, pbase, hkey):
            sKT = scr[:, 0:8, :].rearrange("p a (b c) -> p (a b) c", c=128).rearrange("p (n j) c -> p n j c", j=2)
            sST = fp[:, 5, 0:256]
            sPn = PG[:, 1, 0, :].rearrange("p (c s) -> p c s", s=128)
            sPT = PT[:, 0, :]
            SKT_KEYS = [("scr", s_) for s_ in range(8)]
            for n0 in range(0, NS, 4):
                pb = ps[1][:, :].bitcast(BF16)
                for i in range(4):
                    for jp in range(2):
                        S.op("pe", lambda e, i=i, jp=jp, n0=n0: e.transpose(out=pb[:, (i * 2 + jp) * 128:(i * 2 + jp + 1) * 128],
                                                                              in_=sKb[:, n0 + i, jp * 128:(jp + 1) * 128], identity=ident_b[:]),
                             reads=[("sKb", n0 + i), "ident_b"], writes=[PSK(1)])
                S.op("act", lambda e, n0=n0, pb=pb: e.copy(out=scr[:, n0 // 2:n0 // 2 + 2, :].rearrange("p a b -> p (a b)"), in_=pb),
                     reads=[PSK(1)], writes=[("scr", n0 // 2), ("scr", n0 // 2 + 1)])
            for half in range(2):
                for kc in range(KC):
                    pslot, pkey = page(pbase + pidx[("wq", l, kc)])
                    S.op("pe", lambda e, kc=kc, pslot=pslot, half=half: e.matmul(ps[5 + half][0:NS, 0:512], lhsT=hT[:, kc, cs:cs + NS],
                                                                                   rhs=wring[:, pslot, half * 512:(half + 1) * 512],
                                                                                   start=(kc == 0), stop=(kc == KC - 1)),
                         reads=[pkey, hkey], writes=[PSK(5 + half)], inc=(kc == KC - 1))
            for half in range(2):
                qv = qrot[0:NS, half * 512:(half + 1) * 512].rearrange("p (h t f) -> p h t f", t=2, f=32)
                rope_tm(5 + half, NS, 8, scs[:, 0:32].rearrange("p (o f) -> p o f", o=1), scs[:, 32:64].rearrange("p (o f) -> p o f", o=1),
                        qv[:, :, 0, :], qv[:, :, 1, :], ["scs"], ["qrot"])
            for h in range(16):
                j, g = h // 4, h % 4
                pbse = (j % 2) * 64
                col = ((j // 2) * 4 + g) * NS
                S.op("pe", lambda e, h=h, pbse=pbse, col=col: e.matmul(ps[7][pbse:pbse + 64, col:col + NS], lhsT=qrot[0:NS, h * 64:(h + 1) * 64],
                                                                       rhs=ident_b[0:NS, 0:NS], start=True, stop=True, tile_position=(0, pbse)),
                     reads=["qrot", "ident_b"], writes=[PSK(7)])
            S.op("act", lambda e: e.copy(out=sqT[:], in_=ps[7][:, 0:128]), reads=[PSK(7)], writes=["sqT"])
            sq3 = sqT[:].rearrange("p (a g n) -> p a g n", g=4, n=NS)
            for n in range(NS):
                for j in range(4):
                    pbse = (j % 2) * 64
                    sbank = 0 if j % 2 == 0 else 2
                    S.op("pe", lambda e, n=n, j=j, pbse=pbse, sbank=sbank: e.matmul(ps[sbank][:, n * 16 + 4 * j:n * 16 + 4 * j + 4], lhsT=sKT[pbse:pbse + 64, n, j // 2, :],
                                                                        rhs=sq3[pbse:pbse + 64, j // 2, :, n], start=True, stop=True),
                         reads=SKT_KEYS + ["sqT"], writes=[PSK(sbank)])
            sst4 = sST[:].rearrange("p (n a b g) -> p n a b g", a=2, b=2, g=4)
            for par_ in range(2):
                sbank = 0 if par_ == 0 else 2
                src4 = ps[sbank][:, 0:256].rearrange("p (n a b g) -> p n a b g", a=2, b=2, g=4)
                for a_ in range(2):
                    S.op("act", lambda e, par_=par_, a_=a_, src4=src4: e.copy(out=sst4[:, :, a_, par_, :], in_=src4[:, :, a_, par_, :]),
                         reads=[PSK(sbank)], writes=[("fp", 5)])
            for c in range(2):
                S.op("pe", lambda e, c=c: e.transpose(out=ps[1][:, c * 128:(c + 1) * 128], in_=sST[:, c * 128:(c + 1) * 128], identity=ident_f[:]),
                     reads=[("fp", 5), "ident_f"], writes=[PSK(1)])
            for c in range(2):
                so = ps[1][:, c * 128:(c + 1) * 128]
                S.op("dve", lambda e, so=so, c=c: e.reduce_max(out=sv[:, c, 0:1], in_=so, axis=AX), reads=[PSK(1)], writes=[("sv0", c)])
                S.op("dve", lambda e, c=c: e.tensor_scalar(out=sv[:, c, 6:7], in0=sv[:, c, 0:1], scalar1=0.125, scalar2=sinkrow[:, l:l + 1], op0=ALU.mult, op1=ALU.max),
                     reads=[("sv0", c), "sinkrow"], writes=[("sv6", c)])
                S.op("dve", lambda e, c=c: e.tensor_scalar(out=sv[:, c, 1:2], in0=sv[:, c, 6:7], scalar1=-1.0, scalar2=None, op0=ALU.mult),
                     reads=[("sv6", c)], writes=[("sv1", c)])
                S.op("act", lambda e, so=so, c=c: e.activation(out=Pb[:, c, 0:128], in_=so, func=AF.Exp, bias=sv[:, c, 1:2], scale=0.125, accum_out=sv[:, c, 2:3]),
                     reads=[PSK(1), ("sv1", c)], writes=[("PG", 0), ("sv2", c)])
                S.op("act", lambda e, c=c: e.activation(out=sv[:, c, 3:4], in_=sv[:, c, 1:2], func=AF.Exp, bias=sinkrow[:, l:l + 1], scale=1.0),
                     reads=[("sv1", c), "sinkrow"], writes=[("sv3", c)])
                S.op("dve", lambda e, c=c: e.tensor_tensor(out=sv[:, c, 4:5], in0=sv[:, c, 2:3], in1=sv[:, c, 3:4], op=ALU.add),
                     reads=[("sv2", c), ("sv3", c)], writes=[("sv4", c)])
                S.op("dve", lambda e, c=c: e.reciprocal(out=sv[:, c, 5:6], in_=sv[:, c, 4:5]), reads=[("sv4", c)], writes=[("sv5", c)])
                S.op("dve", lambda e, c=c: e.tensor_scalar(out=sPn[:, c, :], in0=Pb[:, c, 0:128], scalar1=sv[:, c, 5:6], scalar2=None, op0=ALU.mult),
                     reads=[("PG", 0), ("sv5", c)], writes=[("PG", 1)])
            pb4 = ps[4][:, 0:128].bitcast(BF16)
            for c in range(2):
                S.op("pe", lambda e, c=c: e.transpose(out=pb4[:, c * 128:(c + 1) * 128], in_=sPn[:, c, :], identity=ident_b[:]),
                     reads=[("PG", 1), "ident_b"], writes=[PSK(4)])
            S.op("act", lambda e: e.copy(out=sPT[:], in_=pb4), reads=[PSK(4)], writes=["PT"])
            o3 = ps[5][:, 0:128].rearrange("p (c n) -> p c n", n=NS)
            for n in range(NS):
                for j in range(4):
                    for par in range(2):
                        pbse = par * 64
                        c0_ = n * 16 + 4 * j + par
                        S.op("pe", lambda e, n=n, j=j, pbse=pbse, c0_=c0_: e.matmul(o3[pbse:pbse + 64, 2 * j:2 * j + 2, n], lhsT=sVb[:, n, j * 64:(j + 1) * 64],
                                                                                    rhs=sPT[:, c0_:c0_ + 3:2], start=True, stop=True, tile_position=(0, pbse)),
                             reads=[("sVb", n), "PT"], writes=[PSK(5)])
            S.op("act", lambda e: e.copy(out=soT[:], in_=o3), reads=[PSK(5)], writes=["soT"])

        def out_block(c, nb, dst):
            sl = cnt["item"] % 2
            cnt["item"] += 1
            for half in range(2):
                for q in range(4):
                    kc = half * 4 + q
                    S.op("pe", lambda e, kc=kc, q=q, half=half: e.transpose(out=ps[half][0:nb, q * 128:(q + 1) * 128], in_=xT[:, kc, c:c + nb], identity=ident_f[:]),
                         reads=[("xo", c), "ident_f"], writes=[PSK(half)])
                if half == 0:
                    S.op("act", lambda e, sl=sl: e.copy(out=stg[0:nb, sl, 0:512], in_=ps[0][0:nb, :]), reads=[PSK(0)], writes=[("stg", sl)])
                else:
                    S.op("dve", lambda e, sl=sl: e.tensor_copy(out=stg[0:nb, sl, 512:1024], in_=ps[1][0:nb, :]), reads=[PSK(1)], writes=[("stg", sl)])
            S.op("sp", lambda e, sl=sl: e.dma_start(out=dst, in_=stg[0:nb, sl, :]), reads=[("stg", sl)], dma=("stg", sl))

        def load_block(src, nb, c):
            sl = cnt["item"] % 2
            cnt["item"] += 1
            S.op("sp", lambda e, sl=sl: e.dma_start(out=stg[0:nb, sl, :], in_=src), writes=[("stg", sl)], dma=("stg", sl))
            for half in range(2):
                for q in range(4):
                    kc = half * 4 + q
                    S.op("pe", lambda e, kc=kc, q=q, half=half, sl=sl: e.transpose(out=ps[half][:, q * 128:q * 128 + nb], in_=stg[0:nb, sl, kc * 128:(kc + 1) * 128],
                                                                                    identity=ident_f[0:nb, 0:nb]),
                         reads=[("stg", sl), "ident_f"], writes=[PSK(half)])
                src_ps = ps[half][:, :].rearrange("p (q t) -> p q t", t=128)[:, :, 0:nb]
                if half == 0:
                    S.op("act", lambda e, src_ps=src_ps: e.copy(out=xT[:, 0:4, c:c + nb], in_=src_ps), reads=[PSK(0)], writes=[("xl", c)] + sorted(xkeys, key=str))
                else:
                    S.op("dve", lambda e, src_ps=src_ps: e.tensor_copy(out=xT[:, 4:8, c:c + nb], in_=src_ps), reads=[PSK(1)], writes=[("xl", c)] + sorted(xkeys, key=str))

        def main_program():
          for pas in range(2):
              pbase = pas * NPG
              if pas == 0:
                  tiles = [(0, H, "halo"), (H, 512, "p"), (H + 512, 512, "p")]
                  blocks = [(0, 8, 0)] + [(8 + 128 * i, 128, 8 + 128 * i) for i in range(9)]
                  own0 = H
                  tok0 = 0
              else:
                  tiles = [(0, 512, "p"), (512, 512, "p"), (1024, NS, "sample")]
                  blocks = [(H + 1024 + 128 * i, 128, 128 * i) for i in range(8)]
                  own0 = 0
                  tok0 = 1024
              S.tag = 'p%d.load' % pas
              for (r, nb, c) in blocks:
                  load_block(xp_d[r:r + nb, :], nb, c)
              _stage(10 * pas + 0)
              if pas == 1:
                  load_block(xs_d[:, :], NS, 1024)
                  for l in range(2):
                      S.op("sp", lambda e, l=l: e.dma_start(out=stg[0:NS, 0, :], in_=cconv_d[l, :, 0, :]), writes=[("stg", 0)], dma=("stg", 0))
                      S.op("sp", lambda e, l=l: e.dma_start(out=stg[0:NS, 1, :], in_=cconv_d[l, :, 1, :]), writes=[("stg", 1)], dma=("stg", 1))
                      for r_ in range(2):
                          for j in range(8):
                              S.op("pe", lambda e, r_=r_, j=j: e.transpose(out=ps[r_][:, j * NS:(j + 1) * NS], in_=stg[0:NS, r_, j * 128:(j + 1) * 128],
                                                                            identity=ident_f[0:NS, 0:NS]),
                                   reads=[("stg", r_), "ident_f"], writes=[PSK(r_)])
                          S.op("act", lambda e, r_=r_, l=l: e.copy(out=scT[:, l, r_, :, :].rearrange("p a b -> p (a b)"), in_=ps[r_][:, 0:8 * NS]),
                               reads=[PSK(r_)], writes=["scT"])
                  S.op("dve", lambda e: e.tensor_copy(out=kT[:, :, 0:128], in_=kT[:, :, 8 * 128:9 * 128]), reads=[("kT", 8)], writes=[("kT", 0)])
                  S.op("dve", lambda e: e.tensor_copy(out=vA[:, 0, :], in_=vA[:, 8, :]), reads=[("vA", 8)], writes=[("vA", 0)])
              for (c0, n, kind) in tiles:
                  deps_keys = [("xl", c) for (r, nb, c) in blocks if c >= c0 and c < c0 + n] + ([("xl", 1024)] if kind == "sample" else [])
                  S.op("dve", lambda e: e.memset(sv[:, 3, 7:8], 0.0), reads=deps_keys, writes=[("x", c0), "marker"])
                  xkeys.add(("x", c0))
                  for b_ in range(4):
                      xkeys.add(("xo", c0 + 128 * b_))
              for l in range(2):
                  S.tag = 'p%d.conv%d' % (pas, l)
                  conv_layer(l, tiles, pbase, pas == 1, skip_norm=(l == 1),
                             after_last=(lambda c0, n, kind, l=l: norm_h(c0, n, GI_FFN[l])))
                  _stage(10 * pas + 1 + 2 * l)
                  S.tag = 'p%d.ffn%d' % (pas, l)
                  ffn_layer(l, tiles, pbase, skip_norm=True,
                            after_last=(lambda c0, n, kind, l=l: norm_h(c0, n, GI_A[1] if l == 0 else GI_KV)))
                  _stage(10 * pas + 2 + 2 * l)
              S.tag = 'p%d.kv' % pas
              ptiles0 = [t for t in tiles if t[2] == "p"]
              for (c0, n, kind) in tiles:
                  if (c0, n, kind) == tiles[-1]:
                      fc0 = ptiles0[0][0]
                      gb0 = (tok0 + (fc0 - own0)) // 128 + 1
                      attn_prologue(0, fc0, gb0, pbase, ("h", fc0))
                  if kind == "halo":
                      kv_block(8, 0, 0, pbase, False, ("h", c0))
                  elif kind == "p":
                      for bi in range(4):
                          cb = c0 + bi * 128
                          gb = (tok0 + (cb - own0)) // 128 + 1
                          ks = gb if pas == 0 else gb - 8
                          kv_block(cb, gb, ks, pbase, gb == 16, ("h", c0))
                  else:
                      kv_flush()
                      kv_sample(c0, pbase, ("h", c0))
                  if kind != "halo":
                      norm_h(c0, n, GI_B[0])
              kv_flush()
              _stage(10 * pas + 5)
              def _tbl(c0):
                  tb_list = []
                  for bi in range(4):
                      cb = c0 + bi * 128
                      gb = (tok0 + (cb - own0)) // 128 + 1
                      ks = gb if pas == 0 else gb - 8
                      tb_list.append((cb, gb, ks, 0 if gb == 1 else 1, bi))
                  return tb_list
              ptiles = [t for t in tiles if t[2] == "p"]
              for l in range(2):
                  S.tag = 'p%d.attn%d' % (pas, l)
                  for (c0, n, kind) in tiles:
                      if kind == "halo":
                          continue
                      if kind == "p":
                          ti = [t[0] for t in ptiles].index(c0)
                          nf = None
                          if ti + 1 < len(ptiles):
                              nc0 = ptiles[ti + 1][0]
                              nt = _tbl(nc0)[0]
                              nf = (nt[0], nt[1], ("h", nc0))
                          fd = 2 if ti > 0 else (1 if l == 0 else 0)
                          attn_tile(l, c0, _tbl(c0), pbase, ("h", c0), first_done=fd, next_first=nf)
                      else:
                          attn_sample(l, c0, pbase, ("h", c0))
                      wo_proj(l, c0, n, pbase, sample=(kind == "sample"))
                      norm_h(c0, n, GI_FFN[2 + l])
                  _stage(10 * pas + 6 + l)
                  S.tag = 'p%d.ffn%d' % (pas, 2 + l)
                  def _after(c0, n, kind, l=l):
                      if l == 0:
                          norm_h(c0, n, GI_B[1])
                      else:
                          norm(c0, n, GI_FINAL, lambda kc, c0=c0, n=n: xT[:, kc, c0:c0 + n], [("x", c0)] + [("xo", c0 + 128 * b) for b in range(4)])
                  ffn_layer(2 + l, [t for t in tiles if t[2] != "halo"], pbase, skip_norm=True, after_last=_after)
              S.tag = 'p%d.out' % pas
              for (c0, n, kind) in tiles:
                  if kind == "halo":
                      continue
                  if kind == "p":
                      for bi in range(4):
                          cb = c0 + bi * 128
                          t = tok0 + (cb - own0)
                          out_block(cb, 128, yp_d[t:t + 128, :])
                  else:
                      out_block(c0, NS, ys_d[:, :])

        try:
            main_program()
        except _Stop:
            flush_pending(0)
        S.emit(nc, es)
    return nc


def _host_tables(core):
    q = core % 4
    start = q * TOK
    half = 32
    freqs = (10000.0 ** (-(np.arange(half, dtype=np.float32).astype(np.float64)) / half)).astype(np.float32)
    pos = (start - 128 + np.arange(17 * 128)).astype(np.float32)
    ang = (pos[:, None] * freqs[None, :]).astype(np.float32).astype(np.float64)
    cos = np.cos(ang).astype(np.float32).reshape(17, 128, 32).transpose(1, 0, 2).reshape(128, 17 * 32)
    sin = np.sin(ang).astype(np.float32).reshape(17, 128, 32).transpose(1, 0, 2).reshape(128, 17 * 32)
    angs = (np.float32(8192.0) * freqs).astype(np.float32).astype(np.float64)
    scs = np.concatenate([np.cos(angs), np.sin(angs)]).astype(np.float32)[None, :].repeat(NS, 0)
    a = np.arange(128)[:, None]
    b = np.arange(256)[None, :]
    band = np.where((b > a) & (b <= a + 128), 0.0, NEG).astype(np.float32)
    first = band.copy()
    if q == 0:
        first[:, 0:128] = NEG
    masks = np.stack([first, band], 1).reshape(128, 512)
    return np.ascontiguousarray(cos), np.ascontiguousarray(sin), np.ascontiguousarray(scs), np.ascontiguousarray(masks)


_PROGRAM = None
_LAST_SCHED = None


def kernel(x_prompt, x_sample, cache_conv, cache_k, cache_v, norm_a, w_in_a, conv_w, w_out_a, norm_kv, w_kv,
           norm_b, w_q, w_o, sinks, norm_ffn, w_gu, w_down, norm_final):
    global _PROGRAM
    f = lambda a: np.ascontiguousarray(np.asarray(a, dtype=np.float32))
    x_prompt, x_sample, cache_conv, cache_k, cache_v = map(f, (x_prompt, x_sample, cache_conv, cache_k, cache_v))
    inputs = dict(w_in_a=f(w_in_a), w_out_a=f(w_out_a), w_kv=f(w_kv), w_q=f(w_q), w_o=f(w_o), w_gu=f(w_gu), w_down=f(w_down))
    wpages = build_pages_host(inputs)
    gvecs = np.concatenate([f(norm_a), f(norm_ffn), f(norm_kv)[None], f(norm_b), f(norm_final)[None]], 0)
    gains = np.ascontiguousarray(gvecs.reshape(10, 8, 128).transpose(2, 0, 1).reshape(128, 80))
    convw = np.ascontiguousarray(f(conv_w).reshape(2, 3, 8, 128).transpose(3, 0, 1, 2).reshape(128, 48))
    sk = f(sinks)
    sinkb = np.ascontiguousarray(np.broadcast_to(sk.reshape(1, 32), (128, 32)))
    sinkrow = np.ascontiguousarray(sk[:, np.arange(128) % 16].T)
    ident = np.eye(128, dtype=np.float32)
    if _PROGRAM is None:
        _PROGRAM = build_program()
    nc = _PROGRAM
    in_maps = []
    for core in range(8):
        b, q = core // 4, core % 4
        start = q * TOK
        xp = np.zeros((H + TOK, D), np.float32)
        lo = start - H
        if lo < 0:
            xp[-lo:] = x_prompt[b, 0:start + TOK]
        else:
            xp[:] = x_prompt[b, lo:start + TOK]
        cos, sin, scs, masks = _host_tables(core)
        sl = slice(core * NS, (core + 1) * NS)
        in_maps.append(dict(
            xp=xp, xs=np.ascontiguousarray(x_sample[sl, 0, :]),
            cconv=np.ascontiguousarray(cache_conv[:, sl]),
            ck=np.ascontiguousarray(cache_k[sl].reshape(NS, 128, 256)),
            cv=np.ascontiguousarray(cache_v[sl].reshape(NS, 128, 256)),
            wpages=wpages, gains=gains, convw=convw, sinkb=sinkb, sinkrow=sinkrow,
            cost=cos, sint=sin, scs=scs, masks=masks, ident=ident))
    only = os.environ.get("MK_ONLY")
    if only is not None:
        res = run_bass_kernel_spmd(nc, [in_maps[int(only)]], core_ids=[0])
        R = [res.results[0]] * 8
    else:
        res = run_bass_kernel_spmd(nc, in_maps, core_ids=list(range(8)))
        R = res.results
    y_prompt = np.zeros((2, 8192, D), np.float32)
    y_sample = np.zeros((128, 1, D), np.float32)
    conv_prompt = np.zeros((2, 2, 2, D), np.float32)
    k_win_prompt = np.zeros((2, 128, 4, 64), np.float32)
    v_win_prompt = np.zeros((2, 128, 4, 64), np.float32)
    conv_sample = np.zeros((2, 128, 2, D), np.float32)
    k_win_sample = np.zeros((128, 128, 4, 64), np.float32)
    v_win_sample = np.zeros((128, 128, 4, 64), np.float32)
    for core in range(8):
        b, q = core // 4, core % 4
        r = R[core]
        y_prompt[b, q * TOK:(q + 1) * TOK] = r["yp"]
        sl = slice(core * NS, (core + 1) * NS)
        y_sample[sl, 0] = r["ys"]
        conv_sample[:, sl] = r["convs"]
        k_win_sample[sl] = r["kwins"].reshape(NS, 128, 4, 64)
        v_win_sample[sl] = r["vwins"].reshape(NS, 128, 4, 64)
        if q == 3:
            conv_prompt[:, b] = r["convp"]
            k_win_prompt[b] = r["kwinp"].reshape(128, 4, 64)
            v_win_prompt[b] = r["vwinp"].reshape(128, 4, 64)
    return (y_prompt, y_sample, conv_prompt, k_win_prompt, v_win_prompt, conv_sample, k_win_sample, v_win_sample)
```

```python
import contextlib
import os
import types
import numpy as np
import concourse.bass as bass
import concourse.mybir as mybir
from concourse.bass_utils import run_bass_kernel_spmd

F32 = mybir.dt.float32
BF16 = mybir.dt.bfloat16
AF = mybir.ActivationFunctionType
ALU = mybir.AluOpType
AX = mybir.AxisListType.X

ENG_NAMES = ["pe", "act", "dve", "pool", "sp"]
D = 1024
KC = 8
NJ = 22
H = 136
TOK = 2048
NS = 16
NP = 22
XW = 1160
EPS = 1e-6
NEG = -30000.0
FFN_GROUPS = [list(range(0, 6)), list(range(6, 12)), list(range(12, 17)), list(range(17, 22))]
CONV_GROUPS = [list(range(0, 4)), list(range(4, 8))]
GI_A = [0, 1]
GI_FFN = [2, 3, 4, 5]
GI_KV = 6
GI_B = [7, 8]
GI_FINAL = 9


def _freeze(fn, depth=0):
    if not isinstance(fn, types.FunctionType) or fn.__closure__ is None or depth > 4:
        return fn
    cells = []
    for c in fn.__closure__:
        try:
            v = c.cell_contents
        except ValueError:
            cells.append(c)
            continue
        if isinstance(v, types.FunctionType):
            v = _freeze(v, depth + 1)
        cells.append(types.CellType(v))
    return types.FunctionType(fn.__code__, fn.__globals__, fn.__name__, fn.__defaults__, tuple(cells))


class _Stop(Exception):
    pass


_STOP = int(os.environ.get("MK_STOP", "999"))
_SUB = int(os.environ.get("MK_SUB", "255"))


def _stage(k):
    if k >= _STOP:
        raise _Stop()


class Sched:
    def __init__(self):
        self.ops = {e: [] for e in ENG_NAMES}
        self.last_w = {}
        self.rd_eng = {}
        self.rd_dma = {}
        self.dma_cnt = {}
        self.tag = ""

    def op(self, eng, fn, reads=(), writes=(), inc=True, dma=None):
        idx = len(self.ops[eng])
        psk = [k for k in reads if isinstance(k, tuple) and k[0] == "ps"]
        if psk:
            reads = [k for k in reads if k not in psk]
            writes = list(writes) + psk
        deps = []
        for k in reads:
            t = self.last_w.get(k)
            if t is not None:
                deps.append(t)
        for k in writes:
            t = self.last_w.get(k)
            if t is not None:
                deps.append(t)
            for e2, i2 in self.rd_eng.get(k, {}).items():
                deps.append(("op", e2, i2))
            deps.extend(self.rd_dma.get(k, ()))
        if dma is not None:
            self.dma_cnt[dma] = self.dma_cnt.get(dma, 0) + 16
            tok = ("dma", dma, self.dma_cnt[dma])
        else:
            tok = ("op", eng, idx)
        self.ops[eng].append(dict(tag=self.tag, fn=_freeze(fn), deps=deps, inc=(inc and dma is None), dma=dma))
        for k in reads:
            if tok[0] == "op":
                self.rd_eng.setdefault(k, {})[eng] = idx
            else:
                self.rd_dma.setdefault(k, []).append(tok)
        for k in writes:
            self.last_w[k] = tok
            self.rd_eng[k] = {}
            self.rd_dma[k] = []
        return tok

    def finalize(self):
        self.ms = {}
        for e in ENG_NAMES:
            vals = []
            c = 0
            for o in self.ops[e]:
                if o["inc"]:
                    c += 1
                vals.append(c if o["inc"] else None)
            nxt = None
            res = [None] * len(vals)
            for i in range(len(vals) - 1, -1, -1):
                if vals[i] is not None:
                    nxt = vals[i]
                res[i] = nxt
            self.ms[e] = res

    def resolve(self, t):
        if t[0] == "dma":
            return ("dma", t[1]), t[2]
        v = self.ms[t[1]][t[2]]
        assert v is not None, t
        return ("eng", t[1]), v

    def emit(self, nc, es):
        self.finalize()
        self.sems = {}
        for e in ENG_NAMES:
            self.sems[("eng", e)] = es.enter_context(nc.semaphore("sem_" + e))
        for i, d in enumerate(self.dma_cnt):
            self.sems[("dma", d)] = es.enter_context(nc.semaphore("dsem%d" % i))
        block = es.enter_context(nc.Block())

        def mk(ename):
            def body(e):
                waited = {}
                for op in self.ops[ename]:
                    need = {}
                    for t in op["deps"]:
                        if ename == "pe" and t[0] == "op" and t[1] == "pe":
                            continue
                        s, v = self.resolve(t)
                        if need.get(s, 0) < v:
                            need[s] = v
                    for s, v in need.items():
                        if waited.get(s, 0) < v:
                            e.wait_ge(self.sems[s], v)
                            waited[s] = v
                    ins = op["fn"](e)
                    if op["dma"] is not None:
                        ins.then_inc(self.sems[("dma", op["dma"])], 16)
                    elif op["inc"]:
                        ins.then_inc(self.sems[("eng", ename)], 1)
                if ename == "sp":
                    for d, c in self.dma_cnt.items():
                        e.wait_ge(self.sems[("dma", d)], c)
            return body

        block.tensor(mk("pe"))
        block.scalar(mk("act"))
        block.vector(mk("dve"))
        block.gpsimd(mk("pool"))
        block.sync(mk("sp"))


def page_plan():
    pages = []
    idx = {}

    def add(key, spec):
        idx[key] = len(pages)
        pages.append(spec)

    def conv(l):
        for g, grp in enumerate(CONV_GROUPS):
            for j in grp:
                for s in range(3):
                    add(("cin", l, j, s), ("col", "w_in_a", l, s * 1024 + j * 128))
            for j in grp:
                add(("cout", l, j), ("row", "w_out_a", l, j * 128))

    def ffn(li):
        for grp in FFN_GROUPS:
            for j in grp:
                add(("gu", li, j, 0), ("col", "w_gu", li, j * 128))
                add(("gu", li, j, 1), ("col", "w_gu", li, 2816 + j * 128))
            for j in grp:
                add(("dn", li, j), ("row", "w_down", li, j * 128))

    def attn(l):
        for kc in range(8):
            add(("wq", l, kc), ("row", "w_q", l, kc * 128))
        for kc in range(8):
            add(("wo", l, kc), ("row", "w_o", l, kc * 128))

    conv(0); ffn(0); conv(1); ffn(1)
    for q in range(4):
        add(("kv", q), ("kv", "w_kv", None, q))
    attn(0); ffn(2); attn(1); ffn(3)
    return pages, idx


def build_pages_host(inputs):
    pages, _ = page_plan()
    out = np.empty((len(pages), 128, 1024), np.float32)
    for i, (kind, name, l, off) in enumerate(pages):
        W = inputs[name] if l is None else inputs[name][l]
        if kind == "col":
            out[i] = W[:, off:off + 128].reshape(8, 128, 128).transpose(1, 0, 2).reshape(128, 1024)
        elif kind == "row":
            out[i] = W[off:off + 128, :]
        else:
            q = off
            out[i] = W[2 * q * 128:(2 * q + 2) * 128].reshape(2, 128, 512).transpose(1, 0, 2).reshape(128, 1024)
    return out


def build_program():
    nc = bass.Bass("TRN2", target_bir_lowering=False)
    pages, pidx = page_plan()
    NPG = len(pages)

    def din(name, shape):
        return nc.dram_tensor(name, shape, F32, kind="ExternalInput").ap()

    def dout(name, shape):
        return nc.dram_tensor(name, shape, F32, kind="ExternalOutput").ap()

    xp_d = din("xp", [H + TOK, D])
    xs_d = din("xs", [NS, D])
    cconv_d = din("cconv", [2, NS, 2, D])
    ck_d = din("ck", [NS, 128, 256])
    cv_d = din("cv", [NS, 128, 256])
    wp_d = din("wpages", [NPG, 128, 1024])
    gains_d = din("gains", [128, 10 * 8])
    convw_d = din("convw", [128, 2 * 3 * 8])
    sinkb_d = din("sinkb", [128, 32])
    sinkrow_d = din("sinkrow", [128, 2])
    cos_d = din("cost", [128, 17 * 32])
    sin_d = din("sint", [128, 17 * 32])
    scs_d = din("scs", [NS, 64])
    masks_d = din("masks", [128, 512])
    ident_d = din("ident", [128, 128])

    yp_d = dout("yp", [TOK, D])
    ys_d = dout("ys", [NS, D])
    convp_d = dout("convp", [2, 2, D])
    kwinp_d = dout("kwinp", [128, 256])
    vwinp_d = dout("vwinp", [128, 256])
    convs_d = dout("convs", [2, NS, 2, D])
    kwins_d = dout("kwins", [NS, 128, 256])
    vwins_d = dout("vwins", [NS, 128, 256])

    S = Sched()
    global _LAST_SCHED
    _LAST_SCHED = S
    es = contextlib.ExitStack()
    with es:
        def sb(name, shape, dt):
            return es.enter_context(nc.sbuf_tensor(name, shape, dt))

        xT = sb("xT", [128, KC, XW], F32)
        hT = sb("hT", [128, KC, XW], BF16)
        wring = sb("wring", [128, NP, 1024], BF16)
        kT = sb("kT", [128, 4, 9 * 128], BF16)
        vA = sb("vA", [128, 9, 256], BF16)
        sKb = sb("sKb", [128, NS, 256], BF16)
        sVb = sb("sVb", [128, NS, 256], BF16)
        soT = sb("soT", [128, 8, NS], BF16)
        scr = sb("scr", [128, 12, 512], BF16)
        fp = sb("fp", [128, 8, 512], F32)
        stg = sb("stg", [128, 2, 1024], F32)
        sqb = sb("sqb", [128, 2, 512], BF16)
        uext = sb("uext", [128, 2, 516], F32)
        qrot = sb("qrot", [128, 1024], BF16)
        qT = sb("qT", [128, 8, 128], BF16)
        kdup = sb("kdup", [128, 2, 4, 2, 64], BF16)
        NB = 4
        maskb2 = sb("maskb2", [128, 2, 2, 256], BF16)
        PG = sb("PG", [128, NB, 4, 256], BF16)
        svG = sb("svG", [128, NB, 24], F32)
        Pb = PG[:, 0, :, :]
        PT = sb("PT", [128, 4, 256], BF16)
        negsinkb = sb("negsinkb", [128, 32], F32)
        sv = sb("sv", [128, 4, 8], F32)
        cosT = sb("cosT", [128, 17, 32], F32)
        sinT = sb("sinT", [128, 17, 32], F32)
        scs = sb("scs_sb", [NS, 64], F32)
        masks = sb("masks_sb", [128, 2, 256], F32)
        ident_f = sb("ident_f", [128, 128], F32)
        ident_b = sb("ident_b", [128, 128], BF16)
        ones_b = sb("ones_b", [128, 128], BF16)
        gains = sb("gains_sb", [128, 10, 8], F32)
        convw = sb("convw_sb", [128, 2, 3, 8], F32)
        sinkb = sb("sinkb_sb", [128, 32], F32)
        sinkrow = sb("sinkrow_sb", [128, 2], F32)
        uprev = sb("uprev", [128, 2, 8, 2], F32)
        scT = sb("scT", [128, 2, 2, 8, NS], F32)
        usT = sb("usT", [128, 8, NS], F32)
        sqT = sb("sqT", [128, 128], BF16)
        knew = sb("knew", [NS, 2, 256], BF16)
        ps = [es.enter_context(nc.psum_tensor("ps%d" % i, [128, 512], F32)) for i in range(8)]

        def PSK(i):
            return ("ps", i)

        const_keys = []

        def cload(dst, src, key):
            S.op("sp", lambda e, dst=dst, src=src: e.dma_start(out=dst, in_=src), writes=[key], dma="const")
            const_keys.append(key)

        cload(gains[:].rearrange("p a b -> p (a b)"), gains_d, "gains")
        cload(convw[:].rearrange("p a b c -> p (a b c)"), convw_d, "convw")
        cload(sinkb[:], sinkb_d, "sinkb")
        cload(sinkrow[:], sinkrow_d, "sinkrow")
        cload(cosT[:].rearrange("p a b -> p (a b)"), cos_d, "cosT")
        cload(sinT[:].rearrange("p a b -> p (a b)"), sin_d, "sinT")
        cload(scs[:], scs_d, "scs")
        cload(masks[:].rearrange("p a b -> p (a b)"), masks_d, "masks")
        cload(ident_f[:], ident_d, "ident_f")
        for k in const_keys:
            S.last_w[k] = ("dma", "const", S.dma_cnt["const"])
        S.op("dve", lambda e: e.tensor_copy(out=ident_b[:], in_=ident_f[:]), reads=["ident_f"], writes=["ident_b"])
        S.op("dve", lambda e: e.memset(ones_b[:], 1.0), writes=["ones_b"])
        for dup_ in range(2):
            S.op("dve", lambda e, dup_=dup_: e.tensor_scalar(out=maskb2[:, :, dup_, :], in0=masks[:], scalar1=8.0, scalar2=None, op0=ALU.mult),
                 reads=["masks"], writes=["maskb"])
        S.op("dve", lambda e: e.tensor_scalar(out=negsinkb[:], in0=sinkb[:], scalar1=-1.0, scalar2=None, op0=ALU.mult), reads=["sinkb"], writes=["negsinkb"])
        S.op("dve", lambda e: e.memset(uprev[:].rearrange("p a b c -> p (a b c)"), 0.0), writes=["uprev"])

        st = dict(next_load=0, slot_page={}, extra=[])

        def sample_cache_load(n_):
            def run():
                S.op("pool", lambda e: e.dma_start(out=sKb[0:127, n_, :], in_=ck_d[n_, 1:128, :]), writes=[("sKb", n_)], dma=("sc", n_))
                S.op("pool", lambda e: e.dma_start(out=sVb[0:127, n_, :], in_=cv_d[n_, 1:128, :]), writes=[("sVb", n_)], dma=("sc", n_))
                S.last_w[("sKb", n_)] = ("dma", ("sc", n_), 32)
                S.last_w[("sVb", n_)] = ("dma", ("sc", n_), 32)
            return run
        for n_ in range(NS):
            st["extra"].append(sample_cache_load(n_))

        def page(seq):
            while st["next_load"] <= seq:
                i = st["next_load"]
                slot = i % NP
                src = wp_d[i % NPG]
                S.op("pool", lambda e, slot=slot, src=src: e.dma_start(out=wring[:, slot, :], in_=src),
                     writes=[("wp", slot)], dma=("wp", slot))
                st["slot_page"][slot] = i
                st["next_load"] += 1
                if st["extra"] and i >= 4:
                    st["extra"].pop(0)()
            slot = seq % NP
            assert st["slot_page"][slot] == seq, (seq, slot, st["slot_page"][slot])
            return slot, ("wp", slot)

        def norm(c0, n, gi, out_fn, out_keys):
            for kc in range(KC):
                sl = kc % 2
                S.op("act", lambda e, kc=kc, sl=sl: e.activation(out=sqb[:, sl, 0:n], in_=xT[:, kc, c0:c0 + n], func=AF.Square),
                     reads=[("x", c0)], writes=[("sqb", sl)])
                S.op("pe", lambda e, kc=kc, sl=sl: e.matmul(ps[7][:, 0:n], lhsT=ones_b[:], rhs=sqb[:, sl, 0:n],
                                                              start=(kc == 0), stop=(kc == KC - 1)),
                     reads=[("sqb", sl), "ones_b"], writes=[PSK(7)])
            S.op("act", lambda e: e.activation(out=fp[:, 0, 0:n], in_=ps[7][:, 0:n], func=AF.Sqrt, scale=1.0 / D, bias=EPS),
                 reads=[PSK(7)], writes=[("fp", 0)])
            S.op("dve", lambda e: e.reciprocal(out=fp[:, 1, 0:n], in_=fp[:, 0, 0:n]), reads=[("fp", 0)], writes=[("fp", 1)])
            for kc in range(KC):
                S.op("dve", lambda e, kc=kc: e.scalar_tensor_tensor(out=out_fn(kc), in0=xT[:, kc, c0:c0 + n],
                                                                      scalar=gains[:, gi, kc:kc + 1], in1=fp[:, 1, 0:n],
                                                                      op0=ALU.mult, op1=ALU.mult),
                     reads=[("x", c0), ("fp", 1), "gains"], writes=out_keys)

        def norm_h(c0, n, gi):
            norm(c0, n, gi, lambda kc: hT[:, kc, c0:c0 + n], [("h", c0)])

        def proj_fm(bank, n, pslot, pkey, sub, rhs_fn, rkeys, nk=KC, first=True, last=True, lhs_cols=None, co=0):
            for kc in range(nk):
                S.op("pe", lambda e, kc=kc: e.matmul(ps[bank][:, co:co + n], lhsT=wring[:, pslot, kc * 128:(kc + 1) * 128],
                                                       rhs=rhs_fn(kc), start=(first and kc == 0), stop=(last and kc == nk - 1)),
                     reads=[pkey] + rkeys, writes=[PSK(bank)], inc=(kc == nk - 1))

        def resid_add(bank, oc, c0, n):
            S.op("dve", lambda e: e.tensor_tensor(out=xT[:, oc, c0:c0 + n], in0=ps[bank][:, 0:n], in1=xT[:, oc, c0:c0 + n], op=ALU.add),
                 reads=[PSK(bank), ("x", c0)], writes=[("x", c0)])

        cnt = dict(item=0, pair=0, trip=0, s2=0, ait=0, kvb=0)
        xkeys = set()

        pending = []

        def flush_pending(keep=0):
            while len(pending) > keep:
                pending.pop(0)()

        def down_stage(c0, n, par, grp, row_key_fn, pbase, after=None):
            def run_small():
                bank = 6 + (cnt["s2"] % 2)
                cnt["s2"] += 1
                for oc in range(KC):
                    for jj, j in enumerate(grp):
                        pslot, pkey = page(pbase + pidx[row_key_fn(j)])
                        S.op("pe", lambda e, jj=jj, pslot=pslot, oc=oc: e.matmul(
                            ps[bank][:, oc * n:(oc + 1) * n], lhsT=wring[:, pslot, oc * 128:(oc + 1) * 128], rhs=scr[:, par * 6 + jj, 0:n],
                            start=(jj == 0), stop=(jj == len(grp) - 1)),
                            reads=[pkey, ("scr", par * 6 + jj)], writes=[PSK(bank)], inc=(jj == len(grp) - 1))
                S.op("dve", lambda e: e.tensor_tensor(out=xT[:, :, c0:c0 + n], in0=ps[bank][:, 0:KC * n].rearrange("p (o t) -> p o t", t=n),
                                                      in1=xT[:, :, c0:c0 + n], op=ALU.add),
                     reads=[PSK(bank), ("x", c0)], writes=[("x", c0)])
                if after is not None:
                    after()
            if n <= 32:
                return run_small

            def run():
                for oc in range(KC):
                    bank = 6 + (cnt["s2"] % 2)
                    cnt["s2"] += 1
                    for jj, j in enumerate(grp):
                        pslot, pkey = page(pbase + pidx[row_key_fn(j)])
                        S.op("pe", lambda e, jj=jj, pslot=pslot, oc=oc, bank=bank: e.matmul(
                            ps[bank][:, 0:n], lhsT=wring[:, pslot, oc * 128:(oc + 1) * 128], rhs=scr[:, par * 6 + jj, 0:n],
                            start=(jj == 0), stop=(jj == len(grp) - 1)),
                            reads=[pkey, ("scr", par * 6 + jj)], writes=[PSK(bank)], inc=(jj == len(grp) - 1))
                    resid_add(bank, oc, c0, n)
                if after is not None:
                    after()
            return run

        def ffn_layer(li, tiles, pbase, skip_norm=False, after_last=None):
            if not skip_norm:
                for (c0, n, kind) in tiles:
                    norm_h(c0, n, GI_FFN[li])
            for grp in FFN_GROUPS:
                for (c0, n, kind) in tiles:
                    par = cnt["item"] % 2
                    cnt["item"] += 1
                    if n <= 32:
                        G = len(grp)
                        pr = cnt["pair"] % 3
                        cnt["pair"] += 1
                        bg, bu = 2 * pr, 2 * pr + 1
                        for jj, j in enumerate(grp):
                            for s_, bank in ((0, bg), (1, bu)):
                                pslot, pkey = page(pbase + pidx[("gu", li, j, s_)])
                                proj_fm(bank, n, pslot, pkey, s_, lambda kc: hT[:, kc, c0:c0 + n], [("h", c0)], co=jj * n)
                        sl = 5 + (cnt["pair"] % 2)
                        S.op("act", lambda e: e.activation(out=fp[:, sl, 0:G * n], in_=ps[bg][:, 0:G * n], func=AF.Silu),
                             reads=[PSK(bg)], writes=[("fp", sl)])
                        S.op("dve", lambda e: e.tensor_tensor(out=scr[:, par * 6:par * 6 + G, 0:n], in0=ps[bu][:, 0:G * n].rearrange("p (g t) -> p g t", t=n),
                                                              in1=fp[:, sl, 0:G * n].rearrange("p (g t) -> p g t", t=n), op=ALU.mult),
                             reads=[PSK(bu), ("fp", sl)], writes=[("scr", par * 6 + jj) for jj in range(G)])
                        flush_pending(0)
                        aft = None
                        if after_last is not None and grp is FFN_GROUPS[-1]:
                            aft = (lambda c0=c0, n=n, kind=kind: after_last(c0, n, kind))
                        pending.append(down_stage(c0, n, par, grp, lambda j: ("dn", li, j), pbase, aft))
                        continue
                    for jj, j in enumerate(grp):
                        pr = cnt["pair"] % 3
                        cnt["pair"] += 1
                        bg, bu = 2 * pr, 2 * pr + 1
                        for s, bank in ((0, bg), (1, bu)):
                            pslot, pkey = page(pbase + pidx[("gu", li, j, s)])
                            proj_fm(bank, n, pslot, pkey, s, lambda kc: hT[:, kc, c0:c0 + n], [("h", c0)])
                        sl = 5 + (cnt["pair"] % 2)
                        S.op("act", lambda e, bg=bg, sl=sl: e.activation(out=fp[:, sl, 0:n], in_=ps[bg][:, 0:n], func=AF.Silu),
                             reads=[PSK(bg)], writes=[("fp", sl)])
                        S.op("dve", lambda e, bu=bu, sl=sl, jj=jj: e.tensor_tensor(out=scr[:, par * 6 + jj, 0:n], in0=ps[bu][:, 0:n],
                                                                                     in1=fp[:, sl, 0:n], op=ALU.mult),
                             reads=[PSK(bu), ("fp", sl)], writes=[("scr", par * 6 + jj)])
                    flush_pending(0)
                    aft = None
                    if after_last is not None and grp is FFN_GROUPS[-1]:
                        aft = (lambda c0=c0, n=n, kind=kind: after_last(c0, n, kind))
                    pending.append(down_stage(c0, n, par, grp, lambda j: ("dn", li, j), pbase, aft))
            flush_pending(0)

        def conv_layer(l, tiles, pbase, is_last_pass, skip_norm=False, after_last=None):
            if not skip_norm:
                for (c0, n, kind) in tiles:
                    norm_h(c0, n, GI_A[l])
            for grp in CONV_GROUPS:
                for (c0, n, kind) in tiles:
                    par = cnt["item"] % 2
                    cnt["item"] += 1
                    if kind == "sample":
                        G = len(grp)
                        j0 = grp[0]
                        tr = cnt["trip"] % 2
                        cnt["trip"] += 1
                        banks = [3 * tr, 3 * tr + 1, 3 * tr + 2]
                        bb, bc, bx = banks
                        for jj, j in enumerate(grp):
                            for s_ in (1, 2, 0):
                                pslot, pkey = page(pbase + pidx[("cin", l, j, s_)])
                                proj_fm(banks[s_], n, pslot, pkey, s_, lambda kc: hT[:, kc, c0:c0 + n], [("h", c0)], co=jj * n)

                        def v3(ap):
                            return ap.rearrange("p (g t) -> p g t", t=n)

                        def wb(r_):
                            return convw[:, l, r_, j0:j0 + G].unsqueeze(2).to_broadcast([128, G, n])
                        ukeys = [("usT", j) for j in grp]
                        S.op("act", lambda e: e.copy(out=fp[:, 2, 0:G * n], in_=ps[bc][:, 0:G * n]), reads=[PSK(bc)], writes=[("fp", 2)])
                        S.op("dve", lambda e: e.tensor_tensor(out=usT[:, j0:j0 + G, :], in0=v3(ps[bx][:, 0:G * n]), in1=v3(fp[:, 2, 0:G * n]), op=ALU.mult),
                             reads=[PSK(bx), ("fp", 2)], writes=ukeys)
                        S.op("dve", lambda e: e.tensor_tensor(out=v3(fp[:, 3, 0:G * n]), in0=scT[:, l, 0, j0:j0 + G, :], in1=wb(0), op=ALU.mult),
                             reads=["scT", "convw"], writes=[("fp", 3)])
                        S.op("dve", lambda e: e.tensor_tensor(out=v3(fp[:, 4, 0:G * n]), in0=scT[:, l, 1, j0:j0 + G, :], in1=wb(1), op=ALU.mult),
                             reads=["scT", "convw"], writes=[("fp", 4)])
                        S.op("dve", lambda e: e.tensor_tensor(out=fp[:, 3, 0:G * n], in0=fp[:, 3, 0:G * n], in1=fp[:, 4, 0:G * n], op=ALU.add),
                             reads=[("fp", 3), ("fp", 4)], writes=[("fp", 3)])
                        S.op("dve", lambda e: e.tensor_tensor(out=v3(fp[:, 4, 0:G * n]), in0=usT[:, j0:j0 + G, :], in1=wb(2), op=ALU.mult),
                             reads=ukeys + ["convw"], writes=[("fp", 4)])
                        S.op("dve", lambda e: e.tensor_tensor(out=fp[:, 3, 0:G * n], in0=fp[:, 3, 0:G * n], in1=fp[:, 4, 0:G * n], op=ALU.add),
                             reads=[("fp", 3), ("fp", 4)], writes=[("fp", 3)])
                        S.op("dve", lambda e: e.tensor_tensor(out=scr[:, par * 6:par * 6 + G, 0:n], in0=v3(ps[bb][:, 0:G * n]), in1=v3(fp[:, 3, 0:G * n]), op=ALU.mult),
                             reads=[PSK(bb), ("fp", 3)], writes=[("scr", par * 6 + jj) for jj in range(G)])
                        flush_pending(0)
                        aft = None
                        if after_last is not None and grp is CONV_GROUPS[-1]:
                            aft = (lambda c0=c0, n=n, kind=kind: after_last(c0, n, kind))
                        pending.append(down_stage(c0, n, par, grp, lambda j: ("cout", l, j), pbase, aft))
                        continue
                    for jj, j in enumerate(grp):
                        tr = cnt["trip"] % 2
                        cnt["trip"] += 1
                        banks = [3 * tr, 3 * tr + 1, 3 * tr + 2]
                        for s in (1, 2, 0):
                            pslot, pkey = page(pbase + pidx[("cin", l, j, s)])
                            proj_fm(banks[s], n, pslot, pkey, s, lambda kc: hT[:, kc, c0:c0 + n], [("h", c0)])
                        bb, bc, bx = banks
                        ub = tr
                        S.op("act", lambda e, bc=bc: e.copy(out=fp[:, 2, 0:n], in_=ps[bc][:, 0:n]), reads=[PSK(bc)], writes=[("fp", 2)])
                        if kind == "sample":
                            S.op("dve", lambda e, bx=bx, j=j: e.tensor_tensor(out=usT[:, j, :], in0=ps[bx][:, 0:n], in1=fp[:, 2, 0:n], op=ALU.mult),
                                 reads=[PSK(bx), ("fp", 2)], writes=[("usT", j)])
                            a0 = scT[:, l, 0, j, :]
                            a1 = scT[:, l, 1, j, :]
                            a2 = usT[:, j, :]
                            rk = ["scT", ("usT", j)]
                        else:
                            S.op("dve", lambda e, bx=bx, ub=ub: e.tensor_tensor(out=uext[:, ub, 2:2 + n], in0=ps[bx][:, 0:n], in1=fp[:, 2, 0:n], op=ALU.mult),
                                 reads=[PSK(bx), ("fp", 2)], writes=[("uext", ub)])
                            S.op("act", lambda e, ub=ub, j=j: e.copy(out=uext[:, ub, 0:2], in_=uprev[:, l, j, :]),
                                 reads=[("uprev", l, j), "uprev"], writes=[("uext", ub)])
                            S.op("act", lambda e, ub=ub, j=j: e.copy(out=uprev[:, l, j, :], in_=uext[:, ub, n:n + 2]),
                                 reads=[("uext", ub)], writes=[("uprev", l, j)])
                            a0 = uext[:, ub, 0:n]
                            a1 = uext[:, ub, 1:n + 1]
                            a2 = uext[:, ub, 2:n + 2]
                            rk = [("uext", ub)]
                        S.op("act", lambda e, a0=a0, j=j: e.activation(out=fp[:, 3, 0:n], in_=a0, func=AF.Copy, scale=convw[:, l, 0, j:j + 1]),
                             reads=rk + ["convw"], writes=[("fp", 3)])
                        S.op("dve", lambda e, a1=a1, j=j: e.scalar_tensor_tensor(out=fp[:, 4, 0:n], in0=a1, scalar=convw[:, l, 1, j:j + 1], in1=fp[:, 3, 0:n],
                                                                                   op0=ALU.mult, op1=ALU.add),
                             reads=rk + ["convw", ("fp", 3)], writes=[("fp", 4)])
                        S.op("dve", lambda e, a2=a2, j=j: e.scalar_tensor_tensor(out=fp[:, 3, 0:n], in0=a2, scalar=convw[:, l, 2, j:j + 1], in1=fp[:, 4, 0:n],
                                                                                   op0=ALU.mult, op1=ALU.add),
                             reads=rk + ["convw", ("fp", 4)], writes=[("fp", 3)])
                        S.op("dve", lambda e, bb=bb, jj=jj: e.tensor_tensor(out=scr[:, par * 6 + jj, 0:n], in0=ps[bb][:, 0:n], in1=fp[:, 3, 0:n], op=ALU.mult),
                             reads=[PSK(bb), ("fp", 3)], writes=[("scr", par * 6 + jj)])
                    flush_pending(0)
                    aft = None
                    if after_last is not None and grp is CONV_GROUPS[-1]:
                        aft = (lambda c0=c0, n=n, kind=kind: after_last(c0, n, kind))
                    pending.append(down_stage(c0, n, par, grp, lambda j: ("cout", l, j), pbase, aft))
            flush_pending(0)
            if is_last_pass:
                for r_ in range(2):
                    dst = convp_d[l, r_].rearrange("(j p) -> p j", p=128)
                    S.op("sp", lambda e: e.dma_start(out=dst, in_=uprev[:, l, :, r_], allow_slow_non_contiguous=True),
                         reads=[("uprev", l, j) for j in range(8)], dma="o_convp")
                S.op("sp", lambda e: e.dma_start(out=convs_d[l, :, 0, :], in_=cconv_d[l, :, 1, :]), dma="o_convs0")
                for half in range(2):
                    for q in range(4):
                        j = half * 4 + q
                        S.op("pe", lambda e, j=j, half=half, q=q: e.transpose(out=ps[half][0:NS, q * 128:(q + 1) * 128], in_=usT[:, j, :], identity=ident_f[:]),
                             reads=[("usT", j), "ident_f"], writes=[PSK(half)], inc=(q == 3))
                    S.op("act", lambda e, half=half: e.copy(out=stg[0:NS, 0, half * 512:(half + 1) * 512], in_=ps[half][0:NS, :]),
                         reads=[PSK(half)], writes=[("stg", 0)])
                S.op("sp", lambda e: e.dma_start(out=convs_d[l, :, 1, :], in_=stg[0:NS, 0, :]), reads=[("stg", 0)], dma=("stg", 0))

        def rope_tm(bank, rows, nh, cos_ap, sin_ap, out1, out2, srckeys, outkeys):
            v = ps[bank][0:rows, 0:nh * 64].rearrange("p (h t f) -> p h t f", t=2, f=32)
            x1 = v[:, :, 0, :]
            x2 = v[:, :, 1, :]
            cb = cos_ap.to_broadcast([rows, nh, 32])
            sbb = sin_ap.to_broadcast([rows, nh, 32])
            W = nh * 32

            def t(i):
                return fp[0:rows, i, 0:W].rearrange("p (h f) -> p h f", f=32)
            S.op("dve", lambda e: e.tensor_tensor(out=t(2), in0=x1, in1=cb, op=ALU.mult), reads=[PSK(bank)] + srckeys, writes=[("fp", 2)])
            S.op("dve", lambda e: e.tensor_tensor(out=t(3), in0=x2, in1=sbb, op=ALU.mult), reads=[PSK(bank)] + srckeys, writes=[("fp", 3)])
            S.op("dve", lambda e: e.tensor_tensor(out=out1, in0=t(2), in1=t(3), op=ALU.subtract), reads=[("fp", 2), ("fp", 3)], writes=outkeys)
            S.op("dve", lambda e: e.tensor_tensor(out=t(4), in0=x2, in1=cb, op=ALU.mult), reads=[PSK(bank)] + srckeys, writes=[("fp", 4)])
            S.op("dve", lambda e: e.tensor_tensor(out=t(7), in0=x1, in1=sbb, op=ALU.mult), reads=[PSK(bank)] + srckeys, writes=[("fp", 7)])
            S.op("dve", lambda e: e.tensor_tensor(out=out2, in0=t(4), in1=t(7), op=ALU.add), reads=[("fp", 4), ("fp", 7)], writes=outkeys)

        kv_pending = []

        def kv_flush():
            while kv_pending:
                kv_pending.pop(0)()

        def kv_block(cb, tb, ks, pbase, write_win, hkey):
            par = cnt["kvb"] % 2
            cnt["kvb"] += 1
            b0, b1 = 2 * par, 2 * par + 1
            ksl = 5 + par
            for kc in range(KC):
                pslot, pkey = page(pbase + pidx[("kv", kc // 2)])
                S.op("pe", lambda e, kc=kc, pslot=pslot: e.matmul(ps[b0][:, 0:512], lhsT=hT[:, kc, cb:cb + 128],
                                                                    rhs=wring[:, pslot, (kc % 2) * 512:(kc % 2) * 512 + 512],
                                                                    start=(kc == 0), stop=(kc == KC - 1)),
                     reads=[pkey, hkey], writes=[PSK(b0)], inc=(kc == KC - 1))
            kv_flush()
            kf = fp[:, ksl, 0:256].rearrange("p (h t f) -> p h t f", t=2, f=32)
            S.op("act", lambda e: e.copy(out=vA[:, ks, :], in_=ps[b0][:, 256:512]), reads=[PSK(b0)], writes=[("vA", ks)])
            if write_win:
                S.op("act", lambda e: e.copy(out=fp[:, 1, 0:256], in_=ps[b0][:, 256:512]), reads=[PSK(b0)], writes=[("fp", 1)])
                S.op("sp", lambda e: e.dma_start(out=vwinp_d, in_=fp[:, 1, 0:256]), reads=[("fp", 1)], dma="o_vwinp")
            rope_tm(b0, 128, 4, cosT[:, tb:tb + 1, :], sinT[:, tb:tb + 1, :], kf[:, :, 0, :], kf[:, :, 1, :], ["cosT", "sinT"], [("fp", ksl)])
            kf2 = fp[:, ksl, 0:256].rearrange("p (h d) -> p h d", d=64)
            for dup in range(2):
                S.op("act", lambda e, dup=dup: e.copy(out=kdup[:, par, :, dup, :], in_=kf2), reads=[("fp", ksl)], writes=[("kdup", par)])
            if write_win:
                S.op("sp", lambda e: e.dma_start(out=kwinp_d, in_=fp[:, ksl, 0:256]), reads=[("fp", ksl)], dma="o_kwinp")

            def stage2():
                pb = ps[b1][:, 0:256].bitcast(BF16)
                for j in range(4):
                    S.op("pe", lambda e, j=j: e.transpose(out=pb[:, j * 128:(j + 1) * 128], in_=kdup[:, par, j, :, :].rearrange("p a b -> p (a b)"), identity=ident_b[:]),
                         reads=[("kdup", par), "ident_b"], writes=[PSK(b1)], inc=(j == 3))
                S.op("act", lambda e: e.copy(out=kT[:, :, ks * 128:(ks + 1) * 128], in_=pb.rearrange("p (j t) -> p j t", t=128)),
                     reads=[PSK(b1)], writes=[("kT", ks)])
            kv_pending.append(stage2)

        def kv_sample(cs, pbase, hkey):
            for kc in range(KC):
                pslot, pkey = page(pbase + pidx[("kv", kc // 2)])
                S.op("pe", lambda e, kc=kc, pslot=pslot: e.matmul(ps[0][0:NS, 0:512], lhsT=hT[:, kc, cs:cs + NS],
                                                                    rhs=wring[:, pslot, (kc % 2) * 512:(kc % 2) * 512 + 512],
                                                                    start=(kc == 0), stop=(kc == KC - 1)),
                     reads=[pkey, hkey], writes=[PSK(0)], inc=(kc == KC - 1))
            kf = fp[0:NS, 0, 0:256].rearrange("p (h t f) -> p h t f", t=2, f=32)
            rope_tm(0, NS, 4, scs[:, 0:32].rearrange("p (o f) -> p o f", o=1), scs[:, 32:64].rearrange("p (o f) -> p o f", o=1),
                    kf[:, :, 0, :], kf[:, :, 1, :], ["scs"], [("fp", 0)])
            S.op("act", lambda e: e.copy(out=fp[0:NS, 1, 0:256], in_=ps[0][0:NS, 256:512]), reads=[PSK(0)], writes=[("fp", 1)])
            S.op("act", lambda e: e.copy(out=knew[:, 0, :], in_=fp[0:NS, 0, 0:256]), reads=[("fp", 0)], writes=["knew0"])
            S.op("act", lambda e: e.copy(out=knew[:, 1, :], in_=fp[0:NS, 1, 0:256]), reads=[("fp", 1)], writes=["knew1"])
            if _SUB & 1:
                S.op("sp", lambda e: e.dma_start(out=kwins_d[:, 0:127, :], in_=ck_d[:, 1:128, :]), dma="o_kw0")
                S.op("sp", lambda e: e.dma_start(out=vwins_d[:, 0:127, :], in_=cv_d[:, 1:128, :]), dma="o_vw0")
            if _SUB & 2:
                S.op("sp", lambda e: e.dma_start(out=kwins_d[:, 127, :], in_=fp[0:NS, 0, 0:256]), reads=[("fp", 0)], dma="o_kw1")
                S.op("sp", lambda e: e.dma_start(out=vwins_d[:, 127, :], in_=fp[0:NS, 1, 0:256]), reads=[("fp", 1)], dma="o_vw1")
            if not (_SUB & 4):
                return
            for n_ in range(NS):
                S.op("sp", lambda e, n_=n_: e.dma_start(out=sKb[127:128, n_, :], in_=knew[n_:n_ + 1, 0, :]), reads=["knew0"], writes=[("sKb", n_)], dma=("sc2", n_ % 4))
                S.op("sp", lambda e, n_=n_: e.dma_start(out=sVb[127:128, n_, :], in_=knew[n_:n_ + 1, 1, :]), reads=["knew1"], writes=[("sVb", n_)], dma=("sc2", n_ % 4))
            for n_ in range(NS):
                S.last_w[("sKb", n_)] = ("dma", ("sc2", n_ % 4), S.dma_cnt[("sc2", n_ % 4)])
                S.last_w[("sVb", n_)] = ("dma", ("sc2", n_ % 4), S.dma_cnt[("sc2", n_ % 4)])

        def attn_prologue(l, cb, tb, pbase, hkey):
            for half in range(2):
                for kc in range(KC):
                    pslot, pkey = page(pbase + pidx[("wq", l, kc)])
                    S.op("pe", lambda e, kc=kc, pslot=pslot, half=half: e.matmul(ps[6 + half][:, 0:512], lhsT=hT[:, kc, cb:cb + 128],
                                                                                   rhs=wring[:, pslot, half * 512:(half + 1) * 512],
                                                                                   start=(kc == 0), stop=(kc == KC - 1)),
                         reads=[pkey, hkey], writes=[PSK(6 + half)], inc=(kc == KC - 1))
            for half in range(2):
                qv = qrot[:, half * 512:(half + 1) * 512].rearrange("p (h t f) -> p h t f", t=2, f=32)
                rope_tm(6 + half, 128, 8, cosT[:, tb:tb + 1, :], sinT[:, tb:tb + 1, :], qv[:, :, 0, :], qv[:, :, 1, :], ["cosT", "sinT"], ["qrot"])

        def attn_prologue2():
            pb6 = ps[6][:, :].bitcast(BF16)
            for c in range(8):
                S.op("pe", lambda e, c=c: e.transpose(out=pb6[:, c * 128:(c + 1) * 128], in_=qrot[:, c * 128:(c + 1) * 128], identity=ident_b[:]),
                     reads=["qrot", "ident_b"], writes=[PSK(6)], inc=(c == 7))
            S.op("act", lambda e: e.copy(out=qT[:].rearrange("p a b -> p (a b)"), in_=pb6), reads=[PSK(6)], writes=["qT"])

        def attn_group(l, j, ks, mi, it):
            bf = it % NB
            par = it % 2
            bA, bB = 2 * par, 2 * par + 1
            gb_ = []
            for bank in (bA, bB):
                S.op("pe", lambda e, bank=bank: e.matmul(ps[bank][:, :], lhsT=ident_b[:], rhs=maskb2[:, mi, :, :].rearrange("p a b -> p (a b)"),
                                                         start=True, stop=False),
                     reads=["ident_b", "maskb"], writes=[PSK(bank)], inc=False)
            for g in range(4):
                h = 4 * j + g
                base = (g % 2) * 64
                bank = bA if g % 2 == 0 else bB
                half = g // 2
                gb_.append((bank, half))
                S.op("pe", lambda e, h=h, base=base, bank=bank, half=half, g=g: e.matmul(
                    ps[bank][:, half * 256:(half + 1) * 256], lhsT=qT[base:base + 64, h // 2, :],
                    rhs=kT[base:base + 64, j, (ks - 1) * 128:(ks + 1) * 128], start=False, stop=(g >= 2)),
                    reads=["qT", ("kT", ks - 1), ("kT", ks)], writes=[PSK(bank)], inc=(g == 3))
            v = svG[:, bf, :]
            sk = sinkb[:, l * 16 + 4 * j:l * 16 + 4 * j + 4]
            nsk = negsinkb[:, l * 16 + 4 * j:l * 16 + 4 * j + 4]
            mv = v[:, 0:4].rearrange("q (hf p) -> q hf p", p=2)
            for pr, bank in ((0, bA), (1, bB)):
                S.op("dve", lambda e, pr=pr, bank=bank: e.reduce_max(out=mv[:, :, pr], in_=ps[bank][:, :].rearrange("q (hf s) -> q hf s", s=256), axis=AX),
                     reads=[PSK(bank)], writes=[("sv0", bf, pr)])
            S.op("dve", lambda e: e.scalar_tensor_tensor(out=v[:, 4:8], in0=v[:, 0:4], scalar=-0.125, in1=nsk, op0=ALU.mult, op1=ALU.min),
                 reads=[("sv0", bf, 0), ("sv0", bf, 1), "negsinkb"], writes=[("sv1", bf)])
            for g in range(4):
                bank, half = gb_[g]
                S.op("act", lambda e, g=g, bank=bank, half=half: e.activation(out=PG[:, bf, g, :], in_=ps[bank][:, half * 256:(half + 1) * 256], func=AF.Exp,
                                                                                bias=v[:, 4 + g:5 + g], scale=0.125, accum_out=v[:, 8 + g:9 + g]),
                     reads=[PSK(bank), ("sv1", bf)], writes=[("PG", bf), ("sv2", bf, g)])
            S.op("dve", lambda e: e.tensor_tensor(out=v[:, 12:16], in0=v[:, 4:8], in1=sk, op=ALU.add), reads=[("sv1", bf), "sinkb"], writes=[("sv3", bf)])
            S.op("act", lambda e: e.activation(out=v[:, 16:20], in_=v[:, 12:16], func=AF.Exp), reads=[("sv3", bf)], writes=[("sv4", bf)])

        def attn_group_b(it):
            bf = it % NB
            v = svG[:, bf, :]
            S.op("dve", lambda e: e.tensor_tensor(out=v[:, 20:24], in0=v[:, 8:12], in1=v[:, 16:20], op=ALU.add),
                 reads=[("sv2", bf, g) for g in range(4)] + [("sv4", bf)], writes=[("sv5", bf)])
            S.op("dve", lambda e: e.reciprocal(out=v[:, 20:24], in_=v[:, 20:24]), reads=[("sv5", bf)], writes=[("sv5", bf)])
            S.op("dve", lambda e: e.tensor_tensor(out=PG[:, bf, :, :], in0=PG[:, bf, :, :], in1=v[:, 20:24].unsqueeze(2).to_broadcast([128, 4, 256]), op=ALU.mult),
                 reads=[("PG", bf), ("sv5", bf)], writes=[("PG", bf)])

        def attn_tail(j, ks, bi, it):
            bf = it % NB
            pb4 = ps[4][:, :].bitcast(BF16)
            for g in range(4):
                for kb in range(2):
                    S.op("pe", lambda e, g=g, kb=kb: e.transpose(out=pb4[:, (g * 2 + kb) * 128:(g * 2 + kb + 1) * 128], in_=PG[:, bf, g, kb * 128:(kb + 1) * 128],
                                                                 identity=ident_b[:]),
                         reads=[("PG", bf), "ident_b"], writes=[PSK(4)], inc=(g == 3 and kb == 1))
            S.op("dve", lambda e: e.tensor_copy(out=PT[:].rearrange("p a b -> p (a b)"), in_=pb4), reads=[PSK(4)], writes=["PT"])

        def attn_tail_pv(j, ks, bi, it):
            hb = it % 2
            for g in range(4):
                base = (g % 2) * 64
                oo = ps[5][base:base + 64, (hb * 2 + g // 2) * 128:(hb * 2 + g // 2 + 1) * 128]
                for kb in range(2):
                    S.op("pe", lambda e, g=g, kb=kb, oo=oo, base=base: e.matmul(oo, lhsT=vA[:, ks - 1 + kb, j * 64:(j + 1) * 64], rhs=PT[:, g, kb * 128:(kb + 1) * 128],
                                                                             start=(kb == 0), stop=(kb == 1), tile_position=(0, base)),
                         reads=[("vA", ks - 1 + kb), "PT"], writes=[PSK(5)], inc=(g == 3 and kb == 1))
            S.op("act", lambda e: e.copy(out=scr[:, 2 * j:2 * j + 2, bi * 128:(bi + 1) * 128],
                                         in_=ps[5][:, hb * 256:(hb + 1) * 256].rearrange("p (c q) -> p c q", q=128)),
                 reads=[PSK(5)], writes=[("scr", 2 * j), ("scr", 2 * j + 1)])

        def attn_tile(l, c0, tile_blocks, pbase, hkey, first_done=0, next_first=None):
            LAGG = NB - 1
            items = []
            for (cb, tb, ks, mi, bi) in tile_blocks:
                for j in range(4):
                    items.append((cb, tb, ks, mi, bi, j))
            if first_done < 1:
                attn_prologue(l, tile_blocks[0][0], tile_blocks[0][1], pbase, hkey)
            if first_done < 2:
                attn_prologue2()
            for i in range(len(items) + LAGG):
                if i >= LAGG:
                    (cb, tb, ks, mi, bi, j) = items[i - LAGG]
                    attn_tail(j, ks, bi, cnt["ait"] + i - LAGG)
                if i < len(items):
                    (cb, tb, ks, mi, bi, j) = items[i]
                    attn_group(l, j, ks, mi, cnt["ait"] + i)
                if 1 <= i <= len(items):
                    attn_group_b(cnt["ait"] + i - 1)
                if i >= LAGG:
                    (cb, tb, ks, mi, bi, j) = items[i - LAGG]
                    attn_tail_pv(j, ks, bi, cnt["ait"] + i - LAGG)
                if i < len(items):
                    (cb, tb, ks, mi, bi, j) = items[i]
                    if bi + 1 < len(tile_blocks):
                        nb_ = tile_blocks[bi + 1]
                        if j == 1:
                            attn_prologue(l, nb_[0], nb_[1], pbase, hkey)
                        if j == 3:
                            attn_prologue2()
                    elif next_first is not None:
                        if j == 1:
                            attn_prologue(l, next_first[0], next_first[1], pbase, next_first[2])
                        if j == 3:
                            attn_prologue2()
            cnt["ait"] += len(items)

        def wo_proj(l, c0, n, pbase, sample=False):
            for oc in range(KC):
                bank = 6 + (cnt["s2"] % 2)
                cnt["s2"] += 1
                for kc in range(KC):
                    pslot, pkey = page(pbase + pidx[("wo", l, kc)])
                    S.op("pe", lambda e, kc=kc, pslot=pslot, bank=bank, oc=oc: e.matmul(ps[bank][:, 0:n], lhsT=wring[:, pslot, oc * 128:(oc + 1) * 128],
                                                                                          rhs=(soT[:, kc, :] if sample else scr[:, kc, 0:n]), start=(kc == 0), stop=(kc == KC - 1)),
                         reads=[pkey, ("soT" if sample else ("scr", kc))], writes=[PSK(bank)], inc=(kc == KC - 1))
                resid_add(bank, oc, c0, n)

        def attn_sample(l, cs, pbase, hkey):
            sKT = scr[:, 0:8, :].rearrange("p a (b c) -> p (a b) c", c=128).rearrange("p (n j) c -> p n j c", j=2)
            sST = fp[:, 5, 0:256]
            sPn = PG[:, 1, 0, :].rearrange("p (c s) -> p c s", s=128)
            sPT = PT[:, 0, :]
            SKT_KEYS = [("scr", s_) for s_ in range(8)]
            for n0 in range(0, NS, 4):
                pb = ps[1][:, :].bitcast(BF16)
                for i in range(4):
                    for jp in range(2):
                        S.op("pe", lambda e, i=i, jp=jp, n0=n0: e.transpose(out=pb[:, (i * 2 + jp) * 128:(i * 2 + jp + 1) * 128],
                                                                              in_=sKb[:, n0 + i, jp * 128:(jp + 1) * 128], identity=ident_b[:]),
                             reads=[("sKb", n0 + i), "ident_b"], writes=[PSK(1)], inc=(i == 3 and jp == 1))
                S.op("act", lambda e, n0=n0, pb=pb: e.copy(out=scr[:, n0 // 2:n0 // 2 + 2, :].rearrange("p a b -> p (a b)"), in_=pb),
                     reads=[PSK(1)], writes=[("scr", n0 // 2), ("scr", n0 // 2 + 1)])
            for half in range(2):
                for kc in range(KC):
                    pslot, pkey = page(pbase + pidx[("wq", l, kc)])
                    S.op("pe", lambda e, kc=kc, pslot=pslot, half=half: e.matmul(ps[5 + half][0:NS, 0:512], lhsT=hT[:, kc, cs:cs + NS],
                                                                                   rhs=wring[:, pslot, half * 512:(half + 1) * 512],
                                                                                   start=(kc == 0), stop=(kc == KC - 1)),
                         reads=[pkey, hkey], writes=[PSK(5 + half)], inc=(kc == KC - 1))
            for half in range(2):
                qv = qrot[0:NS, half * 512:(half + 1) * 512].rearrange("p (h t f) -> p h t f", t=2, f=32)
                rope_tm(5 + half, NS, 8, scs[:, 0:32].rearrange("p (o f) -> p o f", o=1), scs[:, 32:64].rearrange("p (o f) -> p o f", o=1),
                        qv[:, :, 0, :], qv[:, :, 1, :], ["scs"], ["qrot"])
            for h in range(16):
                j, g = h // 4, h % 4
                pbse = (j % 2) * 64
                col = ((j // 2) * 4 + g) * NS
                S.op("pe", lambda e, h=h, pbse=pbse, col=col: e.matmul(ps[7][pbse:pbse + 64, col:col + NS], lhsT=qrot[0:NS, h * 64:(h + 1) * 64],
                                                                       rhs=ident_b[0:NS, 0:NS], start=True, stop=True, tile_position=(0, pbse)),
                     reads=["qrot", "ident_b"], writes=[PSK(7)], inc=(h == 15))
            S.op("act", lambda e: e.copy(out=sqT[:], in_=ps[7][:, 0:128]), reads=[PSK(7)], writes=["sqT"])
            sq3 = sqT[:].rearrange("p (a g n) -> p a g n", g=4, n=NS)
            for n in range(NS):
                for j in range(4):
                    pbse = (j % 2) * 64
                    sbank = 0 if j % 2 == 0 else 2
                    S.op("pe", lambda e, n=n, j=j, pbse=pbse, sbank=sbank: e.matmul(ps[sbank][:, n * 16 + 4 * j:n * 16 + 4 * j + 4], lhsT=sKT[pbse:pbse + 64, n, j // 2, :],
                                                                        rhs=sq3[pbse:pbse + 64, j // 2, :, n], start=True, stop=True),
                         reads=SKT_KEYS + ["sqT"], writes=[PSK(sbank)], inc=(n == NS - 1 and j >= 2))
            sst4 = sST[:].rearrange("p (n a b g) -> p n a b g", a=2, b=2, g=4)
            for par_ in range(2):
                sbank = 0 if par_ == 0 else 2
                src4 = ps[sbank][:, 0:256].rearrange("p (n a b g) -> p n a b g", a=2, b=2, g=4)
                for a_ in range(2):
                    S.op("act", lambda e, par_=par_, a_=a_, src4=src4: e.copy(out=sst4[:, :, a_, par_, :], in_=src4[:, :, a_, par_, :]),
                         reads=[PSK(sbank)], writes=[("fp", 5)])
            for c in range(2):
                S.op("pe", lambda e, c=c: e.transpose(out=ps[1][:, c * 128:(c + 1) * 128], in_=sST[:, c * 128:(c + 1) * 128], identity=ident_f[:]),
                     reads=[("fp", 5), "ident_f"], writes=[PSK(1)])
            for c in range(2):
                so = ps[1][:, c * 128:(c + 1) * 128]
                S.op("dve", lambda e, so=so, c=c: e.reduce_max(out=sv[:, c, 0:1], in_=so, axis=AX), reads=[PSK(1)], writes=[("sv0", c)])
                S.op("dve", lambda e, c=c: e.tensor_scalar(out=sv[:, c, 6:7], in0=sv[:, c, 0:1], scalar1=0.125, scalar2=sinkrow[:, l:l + 1], op0=ALU.mult, op1=ALU.max),
                     reads=[("sv0", c), "sinkrow"], writes=[("sv6", c)])
                S.op("dve", lambda e, c=c: e.tensor_scalar(out=sv[:, c, 1:2], in0=sv[:, c, 6:7], scalar1=-1.0, scalar2=None, op0=ALU.mult),
                     reads=[("sv6", c)], writes=[("sv1", c)])
                S.op("act", lambda e, so=so, c=c: e.activation(out=Pb[:, c, 0:128], in_=so, func=AF.Exp, bias=sv[:, c, 1:2], scale=0.125, accum_out=sv[:, c, 2:3]),
                     reads=[PSK(1), ("sv1", c)], writes=[("PG", 0), ("sv2", c)])
                S.op("act", lambda e, c=c: e.activation(out=sv[:, c, 3:4], in_=sv[:, c, 1:2], func=AF.Exp, bias=sinkrow[:, l:l + 1], scale=1.0),
                     reads=[("sv1", c), "sinkrow"], writes=[("sv3", c)])
                S.op("dve", lambda e, c=c: e.tensor_tensor(out=sv[:, c, 4:5], in0=sv[:, c, 2:3], in1=sv[:, c, 3:4], op=ALU.add),
                     reads=[("sv2", c), ("sv3", c)], writes=[("sv4", c)])
                S.op("dve", lambda e, c=c: e.reciprocal(out=sv[:, c, 5:6], in_=sv[:, c, 4:5]), reads=[("sv4", c)], writes=[("sv5", c)])
                S.op("dve", lambda e, c=c: e.tensor_scalar(out=sPn[:, c, :], in0=Pb[:, c, 0:128], scalar1=sv[:, c, 5:6], scalar2=None, op0=ALU.mult),
                     reads=[("PG", 0), ("sv5", c)], writes=[("PG", 1)])
            pb4 = ps[4][:, 0:128].bitcast(BF16)
            for c in range(2):
                S.op("pe", lambda e, c=c: e.transpose(out=pb4[:, c * 128:(c + 1) * 128], in_=sPn[:, c, :], identity=ident_b[:]),
                     reads=[("PG", 1), "ident_b"], writes=[PSK(4)])
            S.op("act", lambda e: e.copy(out=sPT[:], in_=pb4), reads=[PSK(4)], writes=["PT"])
            o3 = ps[5][:, 0:128].rearrange("p (c n) -> p c n", n=NS)
            for n in range(NS):
                for j in range(4):
                    for par in range(2):
                        pbse = par * 64
                        c0_ = n * 16 + 4 * j + par
                        S.op("pe", lambda e, n=n, j=j, pbse=pbse, c0_=c0_: e.matmul(o3[pbse:pbse + 64, 2 * j:2 * j + 2, n], lhsT=sVb[:, n, j * 64:(j + 1) * 64],
                                                                                    rhs=sPT[:, c0_:c0_ + 3:2], start=True, stop=True, tile_position=(0, pbse)),
                             reads=[("sVb", n), "PT"], writes=[PSK(5)], inc=(n == NS - 1 and j == 3 and par == 1))
            S.op("act", lambda e: e.copy(out=soT[:], in_=o3), reads=[PSK(5)], writes=["soT"])

        def out_block(c, nb, dst):
            sl = cnt["item"] % 2
            cnt["item"] += 1
            for half in range(2):
                for q in range(4):
                    kc = half * 4 + q
                    S.op("pe", lambda e, kc=kc, q=q, half=half: e.transpose(out=ps[half][0:nb, q * 128:(q + 1) * 128], in_=xT[:, kc, c:c + nb], identity=ident_f[:]),
                         reads=[("xo", c), "ident_f"], writes=[PSK(half)], inc=(q == 3))
                if half == 0:
                    S.op("act", lambda e, sl=sl: e.copy(out=stg[0:nb, sl, 0:512], in_=ps[0][0:nb, :]), reads=[PSK(0)], writes=[("stg", sl)])
                else:
                    S.op("dve", lambda e, sl=sl: e.tensor_copy(out=stg[0:nb, sl, 512:1024], in_=ps[1][0:nb, :]), reads=[PSK(1)], writes=[("stg", sl)])
            S.op("sp", lambda e, sl=sl: e.dma_start(out=dst, in_=stg[0:nb, sl, :]), reads=[("stg", sl)], dma=("stg", sl))

        def load_block(src, nb, c):
            sl = cnt["item"] % 2
            cnt["item"] += 1
            S.op("sp", lambda e, sl=sl: e.dma_start(out=stg[0:nb, sl, :], in_=src), writes=[("stg", sl)], dma=("stg", sl))
            for half in range(2):
                for q in range(4):
                    kc = half * 4 + q
                    S.op("pe", lambda e, kc=kc, q=q, half=half, sl=sl: e.transpose(out=ps[half][:, q * 128:q * 128 + nb], in_=stg[0:nb, sl, kc * 128:(kc + 1) * 128],
                                                                                    identity=ident_f[0:nb, 0:nb]),
                         reads=[("stg", sl), "ident_f"], writes=[PSK(half)], inc=(q == 3))
                src_ps = ps[half][:, :].rearrange("p (q t) -> p q t", t=128)[:, :, 0:nb]
                if half == 0:
                    S.op("act", lambda e, src_ps=src_ps: e.copy(out=xT[:, 0:4, c:c + nb], in_=src_ps), reads=[PSK(0)], writes=[("xl", c)] + sorted(xkeys, key=str))
                else:
                    S.op("dve", lambda e, src_ps=src_ps: e.tensor_copy(out=xT[:, 4:8, c:c + nb], in_=src_ps), reads=[PSK(1)], writes=[("xl", c)] + sorted(xkeys, key=str))

        def main_program():
          for pas in range(2):
              pbase = pas * NPG
              if pas == 0:
                  tiles = [(0, H, "halo"), (H, 512, "p"), (H + 512, 512, "p")]
                  blocks = [(0, 8, 0)] + [(8 + 128 * i, 128, 8 + 128 * i) for i in range(9)]
                  own0 = H
                  tok0 = 0
              else:
                  tiles = [(0, 512, "p"), (512, 512, "p"), (1024, NS, "sample")]
                  blocks = [(H + 1024 + 128 * i, 128, 128 * i) for i in range(8)]
                  own0 = 0
                  tok0 = 1024
              S.tag = 'p%d.load' % pas
              for (r, nb, c) in blocks:
                  load_block(xp_d[r:r + nb, :], nb, c)
              _stage(10 * pas + 0)
              if pas == 1:
                  load_block(xs_d[:, :], NS, 1024)
                  for l in range(2):
                      S.op("sp", lambda e, l=l: e.dma_start(out=stg[0:NS, 0, :], in_=cconv_d[l, :, 0, :]), writes=[("stg", 0)], dma=("stg", 0))
                      S.op("sp", lambda e, l=l: e.dma_start(out=stg[0:NS, 1, :], in_=cconv_d[l, :, 1, :]), writes=[("stg", 1)], dma=("stg", 1))
                      for r_ in range(2):
                          for j in range(8):
                              S.op("pe", lambda e, r_=r_, j=j: e.transpose(out=ps[r_][:, j * NS:(j + 1) * NS], in_=stg[0:NS, r_, j * 128:(j + 1) * 128],
                                                                            identity=ident_f[0:NS, 0:NS]),
                                   reads=[("stg", r_), "ident_f"], writes=[PSK(r_)], inc=(j == 7))
                          S.op("act", lambda e, r_=r_, l=l: e.copy(out=scT[:, l, r_, :, :].rearrange("p a b -> p (a b)"), in_=ps[r_][:, 0:8 * NS]),
                               reads=[PSK(r_)], writes=["scT"])
                  S.op("dve", lambda e: e.tensor_copy(out=kT[:, :, 0:128], in_=kT[:, :, 8 * 128:9 * 128]), reads=[("kT", 8)], writes=[("kT", 0)])
                  S.op("dve", lambda e: e.tensor_copy(out=vA[:, 0, :], in_=vA[:, 8, :]), reads=[("vA", 8)], writes=[("vA", 0)])
              for (c0, n, kind) in tiles:
                  deps_keys = [("xl", c) for (r, nb, c) in blocks if c >= c0 and c < c0 + n] + ([("xl", 1024)] if kind == "sample" else [])
                  S.op("dve", lambda e: e.memset(sv[:, 3, 7:8], 0.0), reads=deps_keys, writes=[("x", c0), "marker"])
                  xkeys.add(("x", c0))
                  for b_ in range(4):
                      xkeys.add(("xo", c0 + 128 * b_))
              for l in range(2):
                  S.tag = 'p%d.conv%d' % (pas, l)
                  conv_layer(l, tiles, pbase, pas == 1, skip_norm=(l == 1),
                             after_last=(lambda c0, n, kind, l=l: norm_h(c0, n, GI_FFN[l])))
                  _stage(10 * pas + 1 + 2 * l)
                  S.tag = 'p%d.ffn%d' % (pas, l)
                  ffn_layer(l, tiles, pbase, skip_norm=True,
                            after_last=(lambda c0, n, kind, l=l: norm_h(c0, n, GI_A[1] if l == 0 else GI_KV)))
                  _stage(10 * pas + 2 + 2 * l)
              S.tag = 'p%d.kv' % pas
              ptiles0 = [t for t in tiles if t[2] == "p"]
              for (c0, n, kind) in tiles:
                  if (c0, n, kind) == tiles[-1]:
                      fc0 = ptiles0[0][0]
                      gb0 = (tok0 + (fc0 - own0)) // 128 + 1
                      attn_prologue(0, fc0, gb0, pbase, ("h", fc0))
                  if kind == "halo":
                      kv_block(8, 0, 0, pbase, False, ("h", c0))
                  elif kind == "p":
                      for bi in range(4):
                          cb = c0 + bi * 128
                          gb = (tok0 + (cb - own0)) // 128 + 1
                          ks = gb if pas == 0 else gb - 8
                          kv_block(cb, gb, ks, pbase, gb == 16, ("h", c0))
                  else:
                      kv_flush()
                      kv_sample(c0, pbase, ("h", c0))
                  if kind != "halo":
                      norm_h(c0, n, GI_B[0])
              kv_flush()
              _stage(10 * pas + 5)
              def _tbl(c0):
                  tb_list = []
                  for bi in range(4):
                      cb = c0 + bi * 128
                      gb = (tok0 + (cb - own0)) // 128 + 1
                      ks = gb if pas == 0 else gb - 8
                      tb_list.append((cb, gb, ks, 0 if gb == 1 else 1, bi))
                  return tb_list
              ptiles = [t for t in tiles if t[2] == "p"]
              for l in range(2):
                  S.tag = 'p%d.attn%d' % (pas, l)
                  for (c0, n, kind) in tiles:
                      if kind == "halo":
                          continue
                      if kind == "p":
                          ti = [t[0] for t in ptiles].index(c0)
                          nf = None
                          if ti + 1 < len(ptiles):
                              nc0 = ptiles[ti + 1][0]
                              nt = _tbl(nc0)[0]
                              nf = (nt[0], nt[1], ("h", nc0))
                          fd = 2 if ti > 0 else (1 if l == 0 else 0)
                          attn_tile(l, c0, _tbl(c0), pbase, ("h", c0), first_done=fd, next_first=nf)
                      else:
                          attn_sample(l, c0, pbase, ("h", c0))
                      wo_proj(l, c0, n, pbase, sample=(kind == "sample"))
                      norm_h(c0, n, GI_FFN[2 + l])
                  _stage(10 * pas + 6 + l)
                  S.tag = 'p%d.ffn%d' % (pas, 2 + l)
                  def _after(c0, n, kind, l=l):
                      if l == 0:
                          norm_h(c0, n, GI_B[1])
                      else:
                          norm(c0, n, GI_FINAL, lambda kc, c0=c0, n=n: xT[:, kc, c0:c0 + n], [("x", c0)] + [("xo", c0 + 128 * b) for b in range(4)])
                  ffn_layer(2 + l, [t for t in tiles if t[2] != "halo"], pbase, skip_norm=True, after_last=_after)
              S.tag = 'p%d.out' % pas
              for (c0, n, kind) in tiles:
                  if kind == "halo":
                      continue
                  if kind == "p":
                      for bi in range(4):
                          cb = c0 + bi * 128
                          t = tok0 + (cb - own0)
                          out_block(cb, 128, yp_d[t:t + 128, :])
                  else:
                      out_block(c0, NS, ys_d[:, :])

        try:
            main_program()
        except _Stop:
            flush_pending(0)
        S.emit(nc, es)
    return nc


def _host_tables(core):
    q = core % 4
    start = q * TOK
    half = 32
    freqs = (10000.0 ** (-(np.arange(half, dtype=np.float32).astype(np.float64)) / half)).astype(np.float32)
    pos = (start - 128 + np.arange(17 * 128)).astype(np.float32)
    ang = (pos[:, None] * freqs[None, :]).astype(np.float32).astype(np.float64)
    cos = np.cos(ang).astype(np.float32).reshape(17, 128, 32).transpose(1, 0, 2).reshape(128, 17 * 32)
    sin = np.sin(ang).astype(np.float32).reshape(17, 128, 32).transpose(1, 0, 2).reshape(128, 17 * 32)
    angs = (np.float32(8192.0) * freqs).astype(np.float32).astype(np.float64)
    scs = np.concatenate([np.cos(angs), np.sin(angs)]).astype(np.float32)[None, :].repeat(NS, 0)
    a = np.arange(128)[:, None]
    b = np.arange(256)[None, :]
    band = np.where((b > a) & (b <= a + 128), 0.0, NEG).astype(np.float32)
    first = band.copy()
    if q == 0:
        first[:, 0:128] = NEG
    masks = np.stack([first, band], 1).reshape(128, 512)
    return np.ascontiguousarray(cos), np.ascontiguousarray(sin), np.ascontiguousarray(scs), np.ascontiguousarray(masks)


_PROGRAM = None
_LAST_SCHED = None


def kernel(x_prompt, x_sample, cache_conv, cache_k, cache_v, norm_a, w_in_a, conv_w, w_out_a, norm_kv, w_kv,
           norm_b, w_q, w_o, sinks, norm_ffn, w_gu, w_down, norm_final):
    global _PROGRAM
    f = lambda a: np.ascontiguousarray(np.asarray(a, dtype=np.float32))
    x_prompt, x_sample, cache_conv, cache_k, cache_v = map(f, (x_prompt, x_sample, cache_conv, cache_k, cache_v))
    inputs = dict(w_in_a=f(w_in_a), w_out_a=f(w_out_a), w_kv=f(w_kv), w_q=f(w_q), w_o=f(w_o), w_gu=f(w_gu), w_down=f(w_down))
    wpages = build_pages_host(inputs)
    gvecs = np.concatenate([f(norm_a), f(norm_ffn), f(norm_kv)[None], f(norm_b), f(norm_final)[None]], 0)
    gains = np.ascontiguousarray(gvecs.reshape(10, 8, 128).transpose(2, 0, 1).reshape(128, 80))
    convw = np.ascontiguousarray(f(conv_w).reshape(2, 3, 8, 128).transpose(3, 0, 1, 2).reshape(128, 48))
    sk = f(sinks)
    sinkb = np.ascontiguousarray(np.broadcast_to(sk.reshape(1, 32), (128, 32)))
    sinkrow = np.ascontiguousarray(sk[:, np.arange(128) % 16].T)
    ident = np.eye(128, dtype=np.float32)
    if _PROGRAM is None:
        _PROGRAM = build_program()
    nc = _PROGRAM
    in_maps = []
    for core in range(8):
        b, q = core // 4, core % 4
        start = q * TOK
        xp = np.zeros((H + TOK, D), np.float32)
        lo = start - H
        if lo < 0:
            xp[-lo:] = x_prompt[b, 0:start + TOK]
        else:
            xp[:] = x_prompt[b, lo:start + TOK]
        cos, sin, scs, masks = _host_tables(core)
        sl = slice(core * NS, (core + 1) * NS)
        in_maps.append(dict(
            xp=xp, xs=np.ascontiguousarray(x_sample[sl, 0, :]),
            cconv=np.ascontiguousarray(cache_conv[:, sl]),
            ck=np.ascontiguousarray(cache_k[sl].reshape(NS, 128, 256)),
            cv=np.ascontiguousarray(cache_v[sl].reshape(NS, 128, 256)),
            wpages=wpages, gains=gains, convw=convw, sinkb=sinkb, sinkrow=sinkrow,
            cost=cos, sint=sin, scs=scs, masks=masks, ident=ident))
    only = os.environ.get("MK_ONLY")
    if only is not None:
        res = run_bass_kernel_spmd(nc, [in_maps[int(only)]], core_ids=[0])
        R = [res.results[0]] * 8
    else:
        res = run_bass_kernel_spmd(nc, in_maps, core_ids=list(range(8)))
        R = res.results
    y_prompt = np.zeros((2, 8192, D), np.float32)
    y_sample = np.zeros((128, 1, D), np.float32)
    conv_prompt = np.zeros((2, 2, 2, D), np.float32)
    k_win_prompt = np.zeros((2, 128, 4, 64), np.float32)
    v_win_prompt = np.zeros((2, 128, 4, 64), np.float32)
    conv_sample = np.zeros((2, 128, 2, D), np.float32)
    k_win_sample = np.zeros((128, 128, 4, 64), np.float32)
    v_win_sample = np.zeros((128, 128, 4, 64), np.float32)
    for core in range(8):
        b, q = core // 4, core % 4
        r = R[core]
        y_prompt[b, q * TOK:(q + 1) * TOK] = r["yp"]
        sl = slice(core * NS, (core + 1) * NS)
        y_sample[sl, 0] = r["ys"]
        conv_sample[:, sl] = r["convs"]
        k_win_sample[sl] = r["kwins"].reshape(NS, 128, 4, 64)
        v_win_sample[sl] = r["vwins"].reshape(NS, 128, 4, 64)
        if q == 3:
            conv_prompt[:, b] = r["convp"]
            k_win_prompt[b] = r["kwinp"].reshape(128, 4, 64)
            v_win_prompt[b] = r["vwinp"].reshape(128, 4, 64)
    return (y_prompt, y_sample, conv_prompt, k_win_prompt, v_win_prompt, conv_sample, k_win_sample, v_win_sample)
```

```python
import contextlib
import os
import types
import numpy as np
import concourse.bass as bass
import concourse.mybir as mybir
from concourse.bass_utils import run_bass_kernel_spmd

F32 = mybir.dt.float32
BF16 = mybir.dt.bfloat16
AF = mybir.ActivationFunctionType
ALU = mybir.AluOpType
AX = mybir.AxisListType.X

ENG_NAMES = ["pe", "act", "dve", "pool", "sp"]
D = 1024
KC = 8
NJ = 22
H = 136
TOK = 2048
NS = 16
NP = 22
XW = 1160
EPS = 1e-6
NEG = -30000.0
FFN_GROUPS = [list(range(0, 6)), list(range(6, 12)), list(range(12, 17)), list(range(17, 22))]
CONV_GROUPS = [list(range(0, 4)), list(range(4, 8))]
GI_A = [0, 1]
GI_FFN = [2, 3, 4, 5]
GI_KV = 6
GI_B = [7, 8]
GI_FINAL = 9


def _freeze(fn, depth=0):
    if not isinstance(fn, types.FunctionType) or fn.__closure__ is None or depth > 4:
        return fn
    cells = []
    for c in fn.__closure__:
        try:
            v = c.cell_contents
        except ValueError:
            cells.append(c)
            continue
        if isinstance(v, types.FunctionType):
            v = _freeze(v, depth + 1)
        cells.append(types.CellType(v))
    return types.FunctionType(fn.__code__, fn.__globals__, fn.__name__, fn.__defaults__, tuple(cells))


class _Stop(Exception):
    pass


_STOP = int(os.environ.get("MK_STOP", "999"))
_SUB = int(os.environ.get("MK_SUB", "255"))


def _stage(k):
    if k >= _STOP:
        raise _Stop()


class Sched:
    def __init__(self):
        self.ops = {e: [] for e in ENG_NAMES}
        self.last_w = {}
        self.rd_eng = {}
        self.rd_dma = {}
        self.dma_cnt = {}
        self.tag = ""

    def op(self, eng, fn, reads=(), writes=(), inc=True, dma=None):
        idx = len(self.ops[eng])
        psk = [k for k in reads if isinstance(k, tuple) and k[0] == "ps"]
        if psk:
            reads = [k for k in reads if k not in psk]
            writes = list(writes) + psk
        deps = []
        for k in reads:
            t = self.last_w.get(k)
            if t is not None:
                deps.append(t)
        for k in writes:
            t = self.last_w.get(k)
            if t is not None:
                deps.append(t)
            for e2, i2 in self.rd_eng.get(k, {}).items():
                deps.append(("op", e2, i2))
            deps.extend(self.rd_dma.get(k, ()))
        if dma is not None:
            self.dma_cnt[dma] = self.dma_cnt.get(dma, 0) + 16
            tok = ("dma", dma, self.dma_cnt[dma])
        else:
            tok = ("op", eng, idx)
        self.ops[eng].append(dict(tag=self.tag, fn=_freeze(fn), deps=deps, inc=(inc and dma is None), dma=dma))
        for k in reads:
            if tok[0] == "op":
                self.rd_eng.setdefault(k, {})[eng] = idx
            else:
                self.rd_dma.setdefault(k, []).append(tok)
        for k in writes:
            self.last_w[k] = tok
            self.rd_eng[k] = {}
            self.rd_dma[k] = []
        return tok

    def finalize(self):
        self.ms = {}
        for e in ENG_NAMES:
            vals = []
            c = 0
            for o in self.ops[e]:
                if o["inc"]:
                    c += 1
                vals.append(c if o["inc"] else None)
            nxt = None
            res = [None] * len(vals)
            for i in range(len(vals) - 1, -1, -1):
                if vals[i] is not None:
                    nxt = vals[i]
                res[i] = nxt
            self.ms[e] = res

    def resolve(self, t):
        if t[0] == "dma":
            return ("dma", t[1]), t[2]
        v = self.ms[t[1]][t[2]]
        assert v is not None, t
        return ("eng", t[1]), v

    def emit(self, nc, es):
        self.finalize()
        self.sems = {}
        for e in ENG_NAMES:
            self.sems[("eng", e)] = es.enter_context(nc.semaphore("sem_" + e))
        for i, d in enumerate(self.dma_cnt):
            self.sems[("dma", d)] = es.enter_context(nc.semaphore("dsem%d" % i))
        block = es.enter_context(nc.Block())

        def mk(ename):
            def body(e):
                waited = {}
                for op in self.ops[ename]:
                    need = {}
                    for t in op["deps"]:
                        if ename == "pe" and t[0] == "op" and t[1] == "pe":
                            continue
                        s, v = self.resolve(t)
                        if need.get(s, 0) < v:
                            need[s] = v
                    for s, v in need.items():
                        if waited.get(s, 0) < v:
                            e.wait_ge(self.sems[s], v)
                            waited[s] = v
                    ins = op["fn"](e)
                    if op["dma"] is not None:
                        ins.then_inc(self.sems[("dma", op["dma"])], 16)
                    elif op["inc"]:
                        ins.then_inc(self.sems[("eng", ename)], 1)
                if ename == "sp":
                    for d, c in self.dma_cnt.items():
                        e.wait_ge(self.sems[("dma", d)], c)
            return body

        block.tensor(mk("pe"))
        block.scalar(mk("act"))
        block.vector(mk("dve"))
        block.gpsimd(mk("pool"))
        block.sync(mk("sp"))


def page_plan():
    pages = []
    idx = {}

    def add(key, spec):
        idx[key] = len(pages)
        pages.append(spec)

    def conv(l):
        for g, grp in enumerate(CONV_GROUPS):
            for j in grp:
                for s in range(3):
                    add(("cin", l, j, s), ("col", "w_in_a", l, s * 1024 + j * 128))
            for j in grp:
                add(("cout", l, j), ("row", "w_out_a", l, j * 128))

    def ffn(li):
        for grp in FFN_GROUPS:
            for j in grp:
                add(("gu", li, j, 0), ("col", "w_gu", li, j * 128))
                add(("gu", li, j, 1), ("col", "w_gu", li, 2816 + j * 128))
            for j in grp:
                add(("dn", li, j), ("row", "w_down", li, j * 128))

    def attn(l):
        for kc in range(8):
            add(("wq", l, kc), ("row", "w_q", l, kc * 128))
        for kc in range(8):
            add(("wo", l, kc), ("row", "w_o", l, kc * 128))

    conv(0); ffn(0); conv(1); ffn(1)
    for q in range(4):
        add(("kv", q), ("kv", "w_kv", None, q))
    attn(0); ffn(2); attn(1); ffn(3)
    return pages, idx


def build_pages_host(inputs):
    pages, _ = page_plan()
    out = np.empty((len(pages), 128, 1024), np.float32)
    for i, (kind, name, l, off) in enumerate(pages):
        W = inputs[name] if l is None else inputs[name][l]
        if kind == "col":
            out[i] = W[:, off:off + 128].reshape(8, 128, 128).transpose(1, 0, 2).reshape(128, 1024)
        elif kind == "row":
            out[i] = W[off:off + 128, :]
        else:
            q = off
            out[i] = W[2 * q * 128:(2 * q + 2) * 128].reshape(2, 128, 512).transpose(1, 0, 2).reshape(128, 1024)
    return out


def build_program():
    nc = bass.Bass("TRN2", target_bir_lowering=False)
    pages, pidx = page_plan()
    NPG = len(pages)

    def din(name, shape):
        return nc.dram_tensor(name, shape, F32, kind="ExternalInput").ap()

    def dout(name, shape):
        return nc.dram_tensor(name, shape, F32, kind="ExternalOutput").ap()

    xp_d = din("xp", [H + TOK, D])
    xs_d = din("xs", [NS, D])
    cconv_d = din("cconv", [2, NS, 2, D])
    ck_d = din("ck", [NS, 128, 256])
    cv_d = din("cv", [NS, 128, 256])
    wp_d = din("wpages", [NPG, 128, 1024])
    gains_d = din("gains", [128, 10 * 8])
    convw_d = din("convw", [128, 2 * 3 * 8])
    sinkb_d = din("sinkb", [128, 32])
    sinkrow_d = din("sinkrow", [128, 2])
    cos_d = din("cost", [128, 17 * 32])
    sin_d = din("sint", [128, 17 * 32])
    scs_d = din("scs", [NS, 64])
    masks_d = din("masks", [128, 512])
    ident_d = din("ident", [128, 128])

    yp_d = dout("yp", [TOK, D])
    ys_d = dout("ys", [NS, D])
    convp_d = dout("convp", [2, 2, D])
    kwinp_d = dout("kwinp", [128, 256])
    vwinp_d = dout("vwinp", [128, 256])
    convs_d = dout("convs", [2, NS, 2, D])
    kwins_d = dout("kwins", [NS, 128, 256])
    vwins_d = dout("vwins", [NS, 128, 256])

    S = Sched()
    global _LAST_SCHED
    _LAST_SCHED = S
    es = contextlib.ExitStack()
    with es:
        def sb(name, shape, dt):
            return es.enter_context(nc.sbuf_tensor(name, shape, dt))

        xT = sb("xT", [128, KC, XW], F32)
        hT = sb("hT", [128, KC, XW], BF16)
        wring = sb("wring", [128, NP, 1024], BF16)
        kT = sb("kT", [128, 4, 9 * 128], BF16)
        vA = sb("vA", [128, 9, 256], BF16)
        sKb = sb("sKb", [128, NS, 256], BF16)
        sVb = sb("sVb", [128, NS, 256], BF16)
        soT = sb("soT", [128, 8, NS], BF16)
        scr = sb("scr", [128, 12, 512], BF16)
        fp = sb("fp", [128, 8, 512], F32)
        stg = sb("stg", [128, 2, 1024], F32)
        sqb = sb("sqb", [128, 2, 512], BF16)
        uext = sb("uext", [128, 2, 516], F32)
        qrot = sb("qrot", [128, 1024], BF16)
        qT = sb("qT", [128, 8, 128], BF16)
        kdup = sb("kdup", [128, 2, 4, 2, 64], BF16)
        NB = 4
        maskb2 = sb("maskb2", [128, 2, 2, 256], BF16)
        PG = sb("PG", [128, NB, 4, 256], BF16)
        svG = sb("svG", [128, NB, 24], F32)
        Pb = PG[:, 0, :, :]
        PT = sb("PT", [128, 4, 256], BF16)
        negsinkb = sb("negsinkb", [128, 32], F32)
        sv = sb("sv", [128, 4, 8], F32)
        cosT = sb("cosT", [128, 17, 32], F32)
        sinT = sb("sinT", [128, 17, 32], F32)
        scs = sb("scs_sb", [NS, 64], F32)
        masks = sb("masks_sb", [128, 2, 256], F32)
        ident_f = sb("ident_f", [128, 128], F32)
        ident_b = sb("ident_b", [128, 128], BF16)
        ones_b = sb("ones_b", [128, 128], BF16)
        gains = sb("gains_sb", [128, 10, 8], F32)
        convw = sb("convw_sb", [128, 2, 3, 8], F32)
        sinkb = sb("sinkb_sb", [128, 32], F32)
        sinkrow = sb("sinkrow_sb", [128, 2], F32)
        uprev = sb("uprev", [128, 2, 8, 2], F32)
        scT = sb("scT", [128, 2, 2, 8, NS], F32)
        usT = sb("usT", [128, 8, NS], F32)
        sqT = sb("sqT", [128, 128], BF16)
        knew = sb("knew", [NS, 2, 256], BF16)
        ps = [es.enter_context(nc.psum_tensor("ps%d" % i, [128, 512], F32)) for i in range(8)]

        def PSK(i):
            return ("ps", i)

        const_keys = []

        def cload(dst, src, key):
            S.op("sp", lambda e, dst=dst, src=src: e.dma_start(out=dst, in_=src), writes=[key], dma="const")
            const_keys.append(key)

        cload(gains[:].rearrange("p a b -> p (a b)"), gains_d, "gains")
        cload(convw[:].rearrange("p a b c -> p (a b c)"), convw_d, "convw")
        cload(sinkb[:], sinkb_d, "sinkb")
        cload(sinkrow[:], sinkrow_d, "sinkrow")
        cload(cosT[:].rearrange("p a b -> p (a b)"), cos_d, "cosT")
        cload(sinT[:].rearrange("p a b -> p (a b)"), sin_d, "sinT")
        cload(scs[:], scs_d, "scs")
        cload(masks[:].rearrange("p a b -> p (a b)"), masks_d, "masks")
        cload(ident_f[:], ident_d, "ident_f")
        for k in const_keys:
            S.last_w[k] = ("dma", "const", S.dma_cnt["const"])
        S.op("dve", lambda e: e.tensor_copy(out=ident_b[:], in_=ident_f[:]), reads=["ident_f"], writes=["ident_b"])
        S.op("dve", lambda e: e.memset(ones_b[:], 1.0), writes=["ones_b"])
        for dup_ in range(2):
            S.op("dve", lambda e, dup_=dup_: e.tensor_scalar(out=maskb2[:, :, dup_, :], in0=masks[:], scalar1=8.0, scalar2=None, op0=ALU.mult),
                 reads=["masks"], writes=["maskb"])
        S.op("dve", lambda e: e.tensor_scalar(out=negsinkb[:], in0=sinkb[:], scalar1=-1.0, scalar2=None, op0=ALU.mult), reads=["sinkb"], writes=["negsinkb"])
        S.op("dve", lambda e: e.memset(uprev[:].rearrange("p a b c -> p (a b c)"), 0.0), writes=["uprev"])

        st = dict(next_load=0, slot_page={}, extra=[])

        def sample_cache_load(n_):
            def run():
                S.op("pool", lambda e: e.dma_start(out=sKb[0:127, n_, :], in_=ck_d[n_, 1:128, :]), writes=[("sKb", n_)], dma=("sc", n_))
                S.op("pool", lambda e: e.dma_start(out=sVb[0:127, n_, :], in_=cv_d[n_, 1:128, :]), writes=[("sVb", n_)], dma=("sc", n_))
                S.last_w[("sKb", n_)] = ("dma", ("sc", n_), 32)
                S.last_w[("sVb", n_)] = ("dma", ("sc", n_), 32)
            return run
        for n_ in range(NS):
            st["extra"].append(sample_cache_load(n_))

        def page(seq):
            while st["next_load"] <= seq:
                i = st["next_load"]
                slot = i % NP
                src = wp_d[i % NPG]
                S.op("pool", lambda e, slot=slot, src=src: e.dma_start(out=wring[:, slot, :], in_=src),
                     writes=[("wp", slot)], dma=("wp", slot))
                st["slot_page"][slot] = i
                st["next_load"] += 1
                if st["extra"] and i >= 4:
                    st["extra"].pop(0)()
            slot = seq % NP
            assert st["slot_page"][slot] == seq, (seq, slot, st["slot_page"][slot])
            return slot, ("wp", slot)

        def norm(c0, n, gi, out_fn, out_keys):
            for kc in range(KC):
                sl = kc % 2
                S.op("act", lambda e, kc=kc, sl=sl: e.activation(out=sqb[:, sl, 0:n], in_=xT[:, kc, c0:c0 + n], func=AF.Square),
                     reads=[("x", c0)], writes=[("sqb", sl)])
                S.op("pe", lambda e, kc=kc, sl=sl: e.matmul(ps[7][:, 0:n], lhsT=ones_b[:], rhs=sqb[:, sl, 0:n],
                                                              start=(kc == 0), stop=(kc == KC - 1)),
                     reads=[("sqb", sl), "ones_b"], writes=[PSK(7)])
            S.op("act", lambda e: e.activation(out=fp[:, 0, 0:n], in_=ps[7][:, 0:n], func=AF.Sqrt, scale=1.0 / D, bias=EPS),
                 reads=[PSK(7)], writes=[("fp", 0)])
            S.op("dve", lambda e: e.reciprocal(out=fp[:, 1, 0:n], in_=fp[:, 0, 0:n]), reads=[("fp", 0)], writes=[("fp", 1)])
            for kc in range(KC):
                S.op("dve", lambda e, kc=kc: e.scalar_tensor_tensor(out=out_fn(kc), in0=xT[:, kc, c0:c0 + n],
                                                                      scalar=gains[:, gi, kc:kc + 1], in1=fp[:, 1, 0:n],
                                                                      op0=ALU.mult, op1=ALU.mult),
                     reads=[("x", c0), ("fp", 1), "gains"], writes=out_keys)

        def norm_h(c0, n, gi):
            norm(c0, n, gi, lambda kc: hT[:, kc, c0:c0 + n], [("h", c0)])

        def proj_fm(bank, n, pslot, pkey, sub, rhs_fn, rkeys, nk=KC, first=True, last=True, lhs_cols=None, co=0):
            for kc in range(nk):
                S.op("pe", lambda e, kc=kc: e.matmul(ps[bank][:, co:co + n], lhsT=wring[:, pslot, kc * 128:(kc + 1) * 128],
                                                       rhs=rhs_fn(kc), start=(first and kc == 0), stop=(last and kc == nk - 1)),
                     reads=[pkey] + rkeys, writes=[PSK(bank)], inc=(kc == nk - 1))

        def resid_add(bank, oc, c0, n):
            S.op("dve", lambda e: e.tensor_tensor(out=xT[:, oc, c0:c0 + n], in0=ps[bank][:, 0:n], in1=xT[:, oc, c0:c0 + n], op=ALU.add),
                 reads=[PSK(bank), ("x", c0)], writes=[("x", c0)])

        cnt = dict(item=0, pair=0, trip=0, s2=0, ait=0, kvb=0)
        xkeys = set()

        pending = []

        def flush_pending(keep=0):
            while len(pending) > keep:
                pending.pop(0)()

        def down_stage(c0, n, par, grp, row_key_fn, pbase, after=None):
            def run_small():
                bank = 6 + (cnt["s2"] % 2)
                cnt["s2"] += 1
                for oc in range(KC):
                    for jj, j in enumerate(grp):
                        pslot, pkey = page(pbase + pidx[row_key_fn(j)])
                        S.op("pe", lambda e, jj=jj, pslot=pslot, oc=oc: e.matmul(
                            ps[bank][:, oc * n:(oc + 1) * n], lhsT=wring[:, pslot, oc * 128:(oc + 1) * 128], rhs=scr[:, par * 6 + jj, 0:n],
                            start=(jj == 0), stop=(jj == len(grp) - 1)),
                            reads=[pkey, ("scr", par * 6 + jj)], writes=[PSK(bank)], inc=(jj == len(grp) - 1))
                S.op("dve", lambda e: e.tensor_tensor(out=xT[:, :, c0:c0 + n], in0=ps[bank][:, 0:KC * n].rearrange("p (o t) -> p o t", t=n),
                                                      in1=xT[:, :, c0:c0 + n], op=ALU.add),
                     reads=[PSK(bank), ("x", c0)], writes=[("x", c0)])
                if after is not None:
                    after()
            if n <= 32:
                return run_small

            def run():
                for oc in range(KC):
                    bank = 6 + (cnt["s2"] % 2)
                    cnt["s2"] += 1
                    for jj, j in enumerate(grp):
                        pslot, pkey = page(pbase + pidx[row_key_fn(j)])
                        S.op("pe", lambda e, jj=jj, pslot=pslot, oc=oc, bank=bank: e.matmul(
                            ps[bank][:, 0:n], lhsT=wring[:, pslot, oc * 128:(oc + 1) * 128], rhs=scr[:, par * 6 + jj, 0:n],
                            start=(jj == 0), stop=(jj == len(grp) - 1)),
                            reads=[pkey, ("scr", par * 6 + jj)], writes=[PSK(bank)], inc=(jj == len(grp) - 1))
                    resid_add(bank, oc, c0, n)
                if after is not None:
                    after()
            return run

        def ffn_layer(li, tiles, pbase, skip_norm=False, after_last=None):
            if not skip_norm:
                for (c0, n, kind) in tiles:
                    norm_h(c0, n, GI_FFN[li])
            for grp in FFN_GROUPS:
                for (c0, n, kind) in tiles:
                    par = cnt["item"] % 2
                    cnt["item"] += 1
                    if n <= 32:
                        G = len(grp)
                        pr = cnt["pair"] % 3
                        cnt["pair"] += 1
                        bg, bu = 2 * pr, 2 * pr + 1
                        for jj, j in enumerate(grp):
                            for s_, bank in ((0, bg), (1, bu)):
                                pslot, pkey = page(pbase + pidx[("gu", li, j, s_)])
                                proj_fm(bank, n, pslot, pkey, s_, lambda kc: hT[:, kc, c0:c0 + n], [("h", c0)], co=jj * n)
                        sl = 5 + (cnt["pair"] % 2)
                        S.op("act", lambda e: e.activation(out=fp[:, sl, 0:G * n], in_=ps[bg][:, 0:G * n], func=AF.Silu),
                             reads=[PSK(bg)], writes=[("fp", sl)])
                        S.op("dve", lambda e: e.tensor_tensor(out=scr[:, par * 6:par * 6 + G, 0:n], in0=ps[bu][:, 0:G * n].rearrange("p (g t) -> p g t", t=n),
                                                              in1=fp[:, sl, 0:G * n].rearrange("p (g t) -> p g t", t=n), op=ALU.mult),
                             reads=[PSK(bu), ("fp", sl)], writes=[("scr", par * 6 + jj) for jj in range(G)])
                        flush_pending(0)
                        aft = None
                        if after_last is not None and grp is FFN_GROUPS[-1]:
                            aft = (lambda c0=c0, n=n, kind=kind: after_last(c0, n, kind))
                        pending.append(down_stage(c0, n, par, grp, lambda j: ("dn", li, j), pbase, aft))
                        continue
                    for jj, j in enumerate(grp):
                        pr = cnt["pair"] % 3
                        cnt["pair"] += 1
                        bg, bu = 2 * pr, 2 * pr + 1
                        for s, bank in ((0, bg), (1, bu)):
                            pslot, pkey = page(pbase + pidx[("gu", li, j, s)])
                            proj_fm(bank, n, pslot, pkey, s, lambda kc: hT[:, kc, c0:c0 + n], [("h", c0)])
                        sl = 5 + (cnt["pair"] % 2)
                        S.op("act", lambda e, bg=bg, sl=sl: e.activation(out=fp[:, sl, 0:n], in_=ps[bg][:, 0:n], func=AF.Silu),
                             reads=[PSK(bg)], writes=[("fp", sl)])
                        S.op("dve", lambda e, bu=bu, sl=sl, jj=jj: e.tensor_tensor(out=scr[:, par * 6 + jj, 0:n], in0=ps[bu][:, 0:n],
                                                                                     in1=fp[:, sl, 0:n], op=ALU.mult),
                             reads=[PSK(bu), ("fp", sl)], writes=[("scr", par * 6 + jj)])
                    flush_pending(0)
                    aft = None
                    if after_last is not None and grp is FFN_GROUPS[-1]:
                        aft = (lambda c0=c0, n=n, kind=kind: after_last(c0, n, kind))
                    pending.append(down_stage(c0, n, par, grp, lambda j: ("dn", li, j), pbase, aft))
            flush_pending(0)

        def conv_layer(l, tiles, pbase, is_last_pass, skip_norm=False, after_last=None):
            if not skip_norm:
                for (c0, n, kind) in tiles:
                    norm_h(c0, n, GI_A[l])
            for grp in CONV_GROUPS:
                for (c0, n, kind) in tiles:
                    par = cnt["item"] % 2
                    cnt["item"] += 1
                    if kind == "sample":
                        G = len(grp)
                        j0 = grp[0]
                        tr = cnt["trip"] % 2
                        cnt["trip"] += 1
                        banks = [3 * tr, 3 * tr + 1, 3 * tr + 2]
                        bb, bc, bx = banks
                        for jj, j in enumerate(grp):
                            for s_ in (1, 2, 0):
                                pslot, pkey = page(pbase + pidx[("cin", l, j, s_)])
                                proj_fm(banks[s_], n, pslot, pkey, s_, lambda kc: hT[:, kc, c0:c0 + n], [("h", c0)], co=jj * n)

                        def v3(ap):
                            return ap.rearrange("p (g t) -> p g t", t=n)

                        def wb(r_):
                            return convw[:, l, r_, j0:j0 + G].unsqueeze(2).to_broadcast([128, G, n])
                        ukeys = [("usT", j) for j in grp]
                        S.op("act", lambda e: e.copy(out=fp[:, 2, 0:G * n], in_=ps[bc][:, 0:G * n]), reads=[PSK(bc)], writes=[("fp", 2)])
                        S.op("dve", lambda e: e.tensor_tensor(out=usT[:, j0:j0 + G, :], in0=v3(ps[bx][:, 0:G * n]), in1=v3(fp[:, 2, 0:G * n]), op=ALU.mult),
                             reads=[PSK(bx), ("fp", 2)], writes=ukeys)
                        S.op("dve", lambda e: e.tensor_tensor(out=v3(fp[:, 3, 0:G * n]), in0=scT[:, l, 0, j0:j0 + G, :], in1=wb(0), op=ALU.mult),
                             reads=["scT", "convw"], writes=[("fp", 3)])
                        S.op("dve", lambda e: e.tensor_tensor(out=v3(fp[:, 4, 0:G * n]), in0=scT[:, l, 1, j0:j0 + G, :], in1=wb(1), op=ALU.mult),
                             reads=["scT", "convw"], writes=[("fp", 4)])
                        S.op("dve", lambda e: e.tensor_tensor(out=fp[:, 3, 0:G * n], in0=fp[:, 3, 0:G * n], in1=fp[:, 4, 0:G * n], op=ALU.add),
                             reads=[("fp", 3), ("fp", 4)], writes=[("fp", 3)])
                        S.op("dve", lambda e: e.tensor_tensor(out=v3(fp[:, 4, 0:G * n]), in0=usT[:, j0:j0 + G, :], in1=wb(2), op=ALU.mult),
                             reads=ukeys + ["convw"], writes=[("fp", 4)])
                        S.op("dve", lambda e: e.tensor_tensor(out=fp[:, 3, 0:G * n], in0=fp[:, 3, 0:G * n], in1=fp[:, 4, 0:G * n], op=ALU.add),
                             reads=[("fp", 3), ("fp", 4)], writes=[("fp", 3)])
                        S.op("dve", lambda e: e.tensor_tensor(out=scr[:, par * 6:par * 6 + G, 0:n], in0=v3(ps[bb][:, 0:G * n]), in1=v3(fp[:, 3, 0:G * n]), op=ALU.mult),
                             reads=[PSK(bb), ("fp", 3)], writes=[("scr", par * 6 + jj) for jj in range(G)])
                        flush_pending(0)
                        aft = None
                        if after_last is not None and grp is CONV_GROUPS[-1]:
                            aft = (lambda c0=c0, n=n, kind=kind: after_last(c0, n, kind))
                        pending.append(down_stage(c0, n, par, grp, lambda j: ("cout", l, j), pbase, aft))
                        continue
                    for jj, j in enumerate(grp):
                        tr = cnt["trip"] % 2
                        cnt["trip"] += 1
                        banks = [3 * tr, 3 * tr + 1, 3 * tr + 2]
                        for s in (1, 2, 0):
                            pslot, pkey = page(pbase + pidx[("cin", l, j, s)])
                            proj_fm(banks[s], n, pslot, pkey, s, lambda kc: hT[:, kc, c0:c0 + n], [("h", c0)])
                        bb, bc, bx = banks
                        ub = tr
                        S.op("act", lambda e, bc=bc: e.copy(out=fp[:, 2, 0:n], in_=ps[bc][:, 0:n]), reads=[PSK(bc)], writes=[("fp", 2)])
                        if kind == "sample":
                            S.op("dve", lambda e, bx=bx, j=j: e.tensor_tensor(out=usT[:, j, :], in0=ps[bx][:, 0:n], in1=fp[:, 2, 0:n], op=ALU.mult),
                                 reads=[PSK(bx), ("fp", 2)], writes=[("usT", j)])
                            a0 = scT[:, l, 0, j, :]
                            a1 = scT[:, l, 1, j, :]
                            a2 = usT[:, j, :]
                            rk = ["scT", ("usT", j)]
                        else:
                            S.op("dve", lambda e, bx=bx, ub=ub: e.tensor_tensor(out=uext[:, ub, 2:2 + n], in0=ps[bx][:, 0:n], in1=fp[:, 2, 0:n], op=ALU.mult),
                                 reads=[PSK(bx), ("fp", 2)], writes=[("uext", ub)])
                            S.op("act", lambda e, ub=ub, j=j: e.copy(out=uext[:, ub, 0:2], in_=uprev[:, l, j, :]),
                                 reads=[("uprev", l, j), "uprev"], writes=[("uext", ub)])
                            S.op("act", lambda e, ub=ub, j=j: e.copy(out=uprev[:, l, j, :], in_=uext[:, ub, n:n + 2]),
                                 reads=[("uext", ub)], writes=[("uprev", l, j)])
                            a0 = uext[:, ub, 0:n]
                            a1 = uext[:, ub, 1:n + 1]
                            a2 = uext[:, ub, 2:n + 2]
                            rk = [("uext", ub)]
                        S.op("dve", lambda e, a0=a0, j=j: e.tensor_scalar(out=fp[:, 3, 0:n], in0=a0, scalar1=convw[:, l, 0, j:j + 1], scalar2=None, op0=ALU.mult),
                             reads=rk + ["convw"], writes=[("fp", 3)])
                        S.op("dve", lambda e, a1=a1, j=j: e.scalar_tensor_tensor(out=fp[:, 4, 0:n], in0=a1, scalar=convw[:, l, 1, j:j + 1], in1=fp[:, 3, 0:n],
                                                                                   op0=ALU.mult, op1=ALU.add),
                             reads=rk + ["convw", ("fp", 3)], writes=[("fp", 4)])
                        S.op("dve", lambda e, a2=a2, j=j: e.scalar_tensor_tensor(out=fp[:, 3, 0:n], in0=a2, scalar=convw[:, l, 2, j:j + 1], in1=fp[:, 4, 0:n],
                                                                                   op0=ALU.mult, op1=ALU.add),
                             reads=rk + ["convw", ("fp", 4)], writes=[("fp", 3)])
                        S.op("dve", lambda e, bb=bb, jj=jj: e.tensor_tensor(out=scr[:, par * 6 + jj, 0:n], in0=ps[bb][:, 0:n], in1=fp[:, 3, 0:n], op=ALU.mult),
                             reads=[PSK(bb), ("fp", 3)], writes=[("scr", par * 6 + jj)])
                    flush_pending(0)
                    aft = None
                    if after_last is not None and grp is CONV_GROUPS[-1]:
                        aft = (lambda c0=c0, n=n, kind=kind: after_last(c0, n, kind))
                    pending.append(down_stage(c0, n, par, grp, lambda j: ("cout", l, j), pbase, aft))
            flush_pending(0)
            if is_last_pass:
                for r_ in range(2):
                    dst = convp_d[l, r_].rearrange("(j p) -> p j", p=128)
                    S.op("sp", lambda e: e.dma_start(out=dst, in_=uprev[:, l, :, r_], allow_slow_non_contiguous=True),
                         reads=[("uprev", l, j) for j in range(8)], dma="o_convp")
                S.op("sp", lambda e: e.dma_start(out=convs_d[l, :, 0, :], in_=cconv_d[l, :, 1, :]), dma="o_convs0")
                for half in range(2):
                    for q in range(4):
                        j = half * 4 + q
                        S.op("pe", lambda e, j=j, half=half, q=q: e.transpose(out=ps[half][0:NS, q * 128:(q + 1) * 128], in_=usT[:, j, :], identity=ident_f[:]),
                             reads=[("usT", j), "ident_f"], writes=[PSK(half)], inc=(q == 3))
                    S.op("act", lambda e, half=half: e.copy(out=stg[0:NS, 0, half * 512:(half + 1) * 512], in_=ps[half][0:NS, :]),
                         reads=[PSK(half)], writes=[("stg", 0)])
                S.op("sp", lambda e: e.dma_start(out=convs_d[l, :, 1, :], in_=stg[0:NS, 0, :]), reads=[("stg", 0)], dma=("stg", 0))

        def rope_tm(bank, rows, nh, cos_ap, sin_ap, out1, out2, srckeys, outkeys):
            v = ps[bank][0:rows, 0:nh * 64].rearrange("p (h t f) -> p h t f", t=2, f=32)
            x1 = v[:, :, 0, :]
            x2 = v[:, :, 1, :]
            cb = cos_ap.to_broadcast([rows, nh, 32])
            sbb = sin_ap.to_broadcast([rows, nh, 32])
            W = nh * 32

            def t(i):
                return fp[0:rows, i, 0:W].rearrange("p (h f) -> p h f", f=32)
            S.op("dve", lambda e: e.tensor_tensor(out=t(2), in0=x1, in1=cb, op=ALU.mult), reads=[PSK(bank)] + srckeys, writes=[("fp", 2)])
            S.op("dve", lambda e: e.tensor_tensor(out=t(3), in0=x2, in1=sbb, op=ALU.mult), reads=[PSK(bank)] + srckeys, writes=[("fp", 3)])
            S.op("dve", lambda e: e.tensor_tensor(out=out1, in0=t(2), in1=t(3), op=ALU.subtract), reads=[("fp", 2), ("fp", 3)], writes=outkeys)
            S.op("dve", lambda e: e.tensor_tensor(out=t(4), in0=x2, in1=cb, op=ALU.mult), reads=[PSK(bank)] + srckeys, writes=[("fp", 4)])
            S.op("dve", lambda e: e.tensor_tensor(out=t(7), in0=x1, in1=sbb, op=ALU.mult), reads=[PSK(bank)] + srckeys, writes=[("fp", 7)])
            S.op("dve", lambda e: e.tensor_tensor(out=out2, in0=t(4), in1=t(7), op=ALU.add), reads=[("fp", 4), ("fp", 7)], writes=outkeys)

        kv_pending = []

        def kv_flush():
            while kv_pending:
                kv_pending.pop(0)()

        def kv_block(cb, tb, ks, pbase, write_win, hkey):
            par = cnt["kvb"] % 2
            cnt["kvb"] += 1
            b0, b1 = 2 * par, 2 * par + 1
            ksl = 5 + par
            for kc in range(KC):
                pslot, pkey = page(pbase + pidx[("kv", kc // 2)])
                S.op("pe", lambda e, kc=kc, pslot=pslot: e.matmul(ps[b0][:, 0:512], lhsT=hT[:, kc, cb:cb + 128],
                                                                    rhs=wring[:, pslot, (kc % 2) * 512:(kc % 2) * 512 + 512],
                                                                    start=(kc == 0), stop=(kc == KC - 1)),
                     reads=[pkey, hkey], writes=[PSK(b0)], inc=(kc == KC - 1))
            kv_flush()
            kf = fp[:, ksl, 0:256].rearrange("p (h t f) -> p h t f", t=2, f=32)
            S.op("act", lambda e: e.copy(out=vA[:, ks, :], in_=ps[b0][:, 256:512]), reads=[PSK(b0)], writes=[("vA", ks)])
            if write_win:
                S.op("act", lambda e: e.copy(out=fp[:, 1, 0:256], in_=ps[b0][:, 256:512]), reads=[PSK(b0)], writes=[("fp", 1)])
                S.op("sp", lambda e: e.dma_start(out=vwinp_d, in_=fp[:, 1, 0:256]), reads=[("fp", 1)], dma="o_vwinp")
            rope_tm(b0, 128, 4, cosT[:, tb:tb + 1, :], sinT[:, tb:tb + 1, :], kf[:, :, 0, :], kf[:, :, 1, :], ["cosT", "sinT"], [("fp", ksl)])
            kf2 = fp[:, ksl, 0:256].rearrange("p (h d) -> p h d", d=64)
            for dup in range(2):
                S.op("act", lambda e, dup=dup: e.copy(out=kdup[:, par, :, dup, :], in_=kf2), reads=[("fp", ksl)], writes=[("kdup", par)])
            if write_win:
                S.op("sp", lambda e: e.dma_start(out=kwinp_d, in_=fp[:, ksl, 0:256]), reads=[("fp", ksl)], dma="o_kwinp")

            def stage2():
                pb = ps[b1][:, 0:256].bitcast(BF16)
                for j in range(4):
                    S.op("pe", lambda e, j=j: e.transpose(out=pb[:, j * 128:(j + 1) * 128], in_=kdup[:, par, j, :, :].rearrange("p a b -> p (a b)"), identity=ident_b[:]),
                         reads=[("kdup", par), "ident_b"], writes=[PSK(b1)], inc=(j == 3))
                S.op("act", lambda e: e.copy(out=kT[:, :, ks * 128:(ks + 1) * 128], in_=pb.rearrange("p (j t) -> p j t", t=128)),
                     reads=[PSK(b1)], writes=[("kT", ks)])
            kv_pending.append(stage2)

        def kv_sample(cs, pbase, hkey):
            for kc in range(KC):
                pslot, pkey = page(pbase + pidx[("kv", kc // 2)])
                S.op("pe", lambda e, kc=kc, pslot=pslot: e.matmul(ps[0][0:NS, 0:512], lhsT=hT[:, kc, cs:cs + NS],
                                                                    rhs=wring[:, pslot, (kc % 2) * 512:(kc % 2) * 512 + 512],
                                                                    start=(kc == 0), stop=(kc == KC - 1)),
                     reads=[pkey, hkey], writes=[PSK(0)], inc=(kc == KC - 1))
            kf = fp[0:NS, 0, 0:256].rearrange("p (h t f) -> p h t f", t=2, f=32)
            rope_tm(0, NS, 4, scs[:, 0:32].rearrange("p (o f) -> p o f", o=1), scs[:, 32:64].rearrange("p (o f) -> p o f", o=1),
                    kf[:, :, 0, :], kf[:, :, 1, :], ["scs"], [("fp", 0)])
            S.op("act", lambda e: e.copy(out=fp[0:NS, 1, 0:256], in_=ps[0][0:NS, 256:512]), reads=[PSK(0)], writes=[("fp", 1)])
            S.op("act", lambda e: e.copy(out=knew[:, 0, :], in_=fp[0:NS, 0, 0:256]), reads=[("fp", 0)], writes=["knew0"])
            S.op("act", lambda e: e.copy(out=knew[:, 1, :], in_=fp[0:NS, 1, 0:256]), reads=[("fp", 1)], writes=["knew1"])
            if _SUB & 1:
                S.op("sp", lambda e: e.dma_start(out=kwins_d[:, 0:127, :], in_=ck_d[:, 1:128, :]), dma="o_kw0")
                S.op("sp", lambda e: e.dma_start(out=vwins_d[:, 0:127, :], in_=cv_d[:, 1:128, :]), dma="o_vw0")
            if _SUB & 2:
                S.op("sp", lambda e: e.dma_start(out=kwins_d[:, 127, :], in_=fp[0:NS, 0, 0:256]), reads=[("fp", 0)], dma="o_kw1")
                S.op("sp", lambda e: e.dma_start(out=vwins_d[:, 127, :], in_=fp[0:NS, 1, 0:256]), reads=[("fp", 1)], dma="o_vw1")
            if not (_SUB & 4):
                return
            for n_ in range(NS):
                S.op("sp", lambda e, n_=n_: e.dma_start(out=sKb[127:128, n_, :], in_=knew[n_:n_ + 1, 0, :]), reads=["knew0"], writes=[("sKb", n_)], dma=("sc2", n_ % 4))
                S.op("sp", lambda e, n_=n_: e.dma_start(out=sVb[127:128, n_, :], in_=knew[n_:n_ + 1, 1, :]), reads=["knew1"], writes=[("sVb", n_)], dma=("sc2", n_ % 4))
            for n_ in range(NS):
                S.last_w[("sKb", n_)] = ("dma", ("sc2", n_ % 4), S.dma_cnt[("sc2", n_ % 4)])
                S.last_w[("sVb", n_)] = ("dma", ("sc2", n_ % 4), S.dma_cnt[("sc2", n_ % 4)])

        def attn_prologue(l, cb, tb, pbase, hkey):
            for half in range(2):
                for kc in range(KC):
                    pslot, pkey = page(pbase + pidx[("wq", l, kc)])
                    S.op("pe", lambda e, kc=kc, pslot=pslot, half=half: e.matmul(ps[6 + half][:, 0:512], lhsT=hT[:, kc, cb:cb + 128],
                                                                                   rhs=wring[:, pslot, half * 512:(half + 1) * 512],
                                                                                   start=(kc == 0), stop=(kc == KC - 1)),
                         reads=[pkey, hkey], writes=[PSK(6 + half)], inc=(kc == KC - 1))
            for half in range(2):
                qv = qrot[:, half * 512:(half + 1) * 512].rearrange("p (h t f) -> p h t f", t=2, f=32)
                rope_tm(6 + half, 128, 8, cosT[:, tb:tb + 1, :], sinT[:, tb:tb + 1, :], qv[:, :, 0, :], qv[:, :, 1, :], ["cosT", "sinT"], ["qrot"])

        def attn_prologue2():
            pb6 = ps[6][:, :].bitcast(BF16)
            for c in range(8):
                S.op("pe", lambda e, c=c: e.transpose(out=pb6[:, c * 128:(c + 1) * 128], in_=qrot[:, c * 128:(c + 1) * 128], identity=ident_b[:]),
                     reads=["qrot", "ident_b"], writes=[PSK(6)], inc=(c == 7))
            S.op("act", lambda e: e.copy(out=qT[:].rearrange("p a b -> p (a b)"), in_=pb6), reads=[PSK(6)], writes=["qT"])

        def attn_group(l, j, ks, mi, it):
            bf = it % NB
            par = it % 2
            bA, bB = 2 * par, 2 * par + 1
            gb_ = []
            for bank in (bA, bB):
                S.op("pe", lambda e, bank=bank: e.matmul(ps[bank][:, :], lhsT=ident_b[:], rhs=maskb2[:, mi, :, :].rearrange("p a b -> p (a b)"),
                                                         start=True, stop=False),
                     reads=["ident_b", "maskb"], writes=[PSK(bank)], inc=False)
            for g in range(4):
                h = 4 * j + g
                base = (g % 2) * 64
                bank = bA if g % 2 == 0 else bB
                half = g // 2
                gb_.append((bank, half))
                S.op("pe", lambda e, h=h, base=base, bank=bank, half=half, g=g: e.matmul(
                    ps[bank][:, half * 256:(half + 1) * 256], lhsT=qT[base:base + 64, h // 2, :],
                    rhs=kT[base:base + 64, j, (ks - 1) * 128:(ks + 1) * 128], start=False, stop=(g >= 2)),
                    reads=["qT", ("kT", ks - 1), ("kT", ks)], writes=[PSK(bank)], inc=(g == 3))
            v = svG[:, bf, :]
            sk = sinkb[:, l * 16 + 4 * j:l * 16 + 4 * j + 4]
            nsk = negsinkb[:, l * 16 + 4 * j:l * 16 + 4 * j + 4]
            mv = v[:, 0:4].rearrange("q (hf p) -> q hf p", p=2)
            for pr, bank in ((0, bA), (1, bB)):
                S.op("dve", lambda e, pr=pr, bank=bank: e.reduce_max(out=mv[:, :, pr], in_=ps[bank][:, :].rearrange("q (hf s) -> q hf s", s=256), axis=AX),
                     reads=[PSK(bank)], writes=[("sv0", bf, pr)])
            S.op("dve", lambda e: e.scalar_tensor_tensor(out=v[:, 4:8], in0=v[:, 0:4], scalar=-0.125, in1=nsk, op0=ALU.mult, op1=ALU.min),
                 reads=[("sv0", bf, 0), ("sv0", bf, 1), "negsinkb"], writes=[("sv1", bf)])
            for g in range(4):
                bank, half = gb_[g]
                S.op("act", lambda e, g=g, bank=bank, half=half: e.activation(out=PG[:, bf, g, :], in_=ps[bank][:, half * 256:(half + 1) * 256], func=AF.Exp,
                                                                                bias=v[:, 4 + g:5 + g], scale=0.125, accum_out=v[:, 8 + g:9 + g]),
                     reads=[PSK(bank), ("sv1", bf)], writes=[("PG", bf), ("sv2", bf, g)])
            S.op("dve", lambda e: e.tensor_tensor(out=v[:, 12:16], in0=v[:, 4:8], in1=sk, op=ALU.add), reads=[("sv1", bf), "sinkb"], writes=[("sv3", bf)])
            S.op("act", lambda e: e.activation(out=v[:, 16:20], in_=v[:, 12:16], func=AF.Exp), reads=[("sv3", bf)], writes=[("sv4", bf)])

        def attn_group_b(it):
            bf = it % NB
            v = svG[:, bf, :]
            S.op("dve", lambda e: e.tensor_tensor(out=v[:, 20:24], in0=v[:, 8:12], in1=v[:, 16:20], op=ALU.add),
                 reads=[("sv2", bf, g) for g in range(4)] + [("sv4", bf)], writes=[("sv5", bf)])
            S.op("dve", lambda e: e.reciprocal(out=v[:, 20:24], in_=v[:, 20:24]), reads=[("sv5", bf)], writes=[("sv5", bf)])
            S.op("dve", lambda e: e.tensor_tensor(out=PG[:, bf, :, :], in0=PG[:, bf, :, :], in1=v[:, 20:24].unsqueeze(2).to_broadcast([128, 4, 256]), op=ALU.mult),
                 reads=[("PG", bf), ("sv5", bf)], writes=[("PG", bf)])

        def attn_tail(j, ks, bi, it):
            bf = it % NB
            pb4 = ps[4][:, :].bitcast(BF16)
            for g in range(4):
                for kb in range(2):
                    S.op("pe", lambda e, g=g, kb=kb: e.transpose(out=pb4[:, (g * 2 + kb) * 128:(g * 2 + kb + 1) * 128], in_=PG[:, bf, g, kb * 128:(kb + 1) * 128],
                                                                 identity=ident_b[:]),
                         reads=[("PG", bf), "ident_b"], writes=[PSK(4)], inc=(g == 3 and kb == 1))
            S.op("dve", lambda e: e.tensor_copy(out=PT[:].rearrange("p a b -> p (a b)"), in_=pb4), reads=[PSK(4)], writes=["PT"])

        def attn_tail_pv(j, ks, bi, it):
            hb = it % 2
            for g in range(4):
                base = (g % 2) * 64
                oo = ps[5][base:base + 64, (hb * 2 + g // 2) * 128:(hb * 2 + g // 2 + 1) * 128]
                for kb in range(2):
                    S.op("pe", lambda e, g=g, kb=kb, oo=oo, base=base: e.matmul(oo, lhsT=vA[:, ks - 1 + kb, j * 64:(j + 1) * 64], rhs=PT[:, g, kb * 128:(kb + 1) * 128],
                                                                             start=(kb == 0), stop=(kb == 1), tile_position=(0, base)),
                         reads=[("vA", ks - 1 + kb), "PT"], writes=[PSK(5)], inc=(g == 3 and kb == 1))
            S.op("act", lambda e: e.copy(out=scr[:, 2 * j:2 * j + 2, bi * 128:(bi + 1) * 128],
                                         in_=ps[5][:, hb * 256:(hb + 1) * 256].rearrange("p (c q) -> p c q", q=128)),
                 reads=[PSK(5)], writes=[("scr", 2 * j), ("scr", 2 * j + 1)])

        def attn_tile(l, c0, tile_blocks, pbase, hkey, first_done=0, next_first=None):
            LAGG = NB - 1
            items = []
            for (cb, tb, ks, mi, bi) in tile_blocks:
                for j in range(4):
                    items.append((cb, tb, ks, mi, bi, j))
            if first_done < 1:
                attn_prologue(l, tile_blocks[0][0], tile_blocks[0][1], pbase, hkey)
            if first_done < 2:
                attn_prologue2()
            for i in range(len(items) + LAGG):
                if i >= LAGG:
                    (cb, tb, ks, mi, bi, j) = items[i - LAGG]
                    attn_tail(j, ks, bi, cnt["ait"] + i - LAGG)
                if i < len(items):
                    (cb, tb, ks, mi, bi, j) = items[i]
                    attn_group(l, j, ks, mi, cnt["ait"] + i)
                if 1 <= i <= len(items):
                    attn_group_b(cnt["ait"] + i - 1)
                if i >= LAGG:
                    (cb, tb, ks, mi, bi, j) = items[i - LAGG]
                    attn_tail_pv(j, ks, bi, cnt["ait"] + i - LAGG)
                if i < len(items):
                    (cb, tb, ks, mi, bi, j) = items[i]
                    if bi + 1 < len(tile_blocks):
                        nb_ = tile_blocks[bi + 1]
                        if j == 1:
                            attn_prologue(l, nb_[0], nb_[1], pbase, hkey)
                        if j == 3:
                            attn_prologue2()
                    elif next_first is not None:
                        if j == 1:
                            attn_prologue(l, next_first[0], next_first[1], pbase, next_first[2])
                        if j == 3:
                            attn_prologue2()
            cnt["ait"] += len(items)

        def wo_proj(l, c0, n, pbase, sample=False):
            for oc in range(KC):
                bank = 6 + (cnt["s2"] % 2)
                cnt["s2"] += 1
                for kc in range(KC):
                    pslot, pkey = page(pbase + pidx[("wo", l, kc)])
                    S.op("pe", lambda e, kc=kc, pslot=pslot, bank=bank, oc=oc: e.matmul(ps[bank][:, 0:n], lhsT=wring[:, pslot, oc * 128:(oc + 1) * 128],
                                                                                          rhs=(soT[:, kc, :] if sample else scr[:, kc, 0:n]), start=(kc == 0), stop=(kc == KC - 1)),
                         reads=[pkey, ("soT" if sample else ("scr", kc))], writes=[PSK(bank)], inc=(kc == KC - 1))
                resid_add(bank, oc, c0, n)

        def attn_sample(l, cs, pbase, hkey):
            sKT = scr[:, 0:8, :].rearrange("p a (b c) -> p (a b) c", c=128).rearrange("p (n j) c -> p n j c", j=2)
            sST = fp[:, 5, 0:256]
            sPn = PG[:, 1, 0, :].rearrange("p (c s) -> p c s", s=128)
            sPT = PT[:, 0, :]
            SKT_KEYS = [("scr", s_) for s_ in range(8)]
            for n0 in range(0, NS, 4):
                pb = ps[1][:, :].bitcast(BF16)
                for i in range(4):
                    for jp in range(2):
                        S.op("pe", lambda e, i=i, jp=jp, n0=n0: e.transpose(out=pb[:, (i * 2 + jp) * 128:(i * 2 + jp + 1) * 128],
                                                                              in_=sKb[:, n0 + i, jp * 128:(jp + 1) * 128], identity=ident_b[:]),
                             reads=[("sKb", n0 + i), "ident_b"], writes=[PSK(1)], inc=(i == 3 and jp == 1))
                S.op("act", lambda e, n0=n0, pb=pb: e.copy(out=scr[:, n0 // 2:n0 // 2 + 2, :].rearrange("p a b -> p (a b)"), in_=pb),
                     reads=[PSK(1)], writes=[("scr", n0 // 2), ("scr", n0 // 2 + 1)])
            for half in range(2):
                for kc in range(KC):
                    pslot, pkey = page(pbase + pidx[("wq", l, kc)])
                    S.op("pe", lambda e, kc=kc, pslot=pslot, half=half: e.matmul(ps[5 + half][0:NS, 0:512], lhsT=hT[:, kc, cs:cs + NS],
                                                                                   rhs=wring[:, pslot, half * 512:(half + 1) * 512],
                                                                                   start=(kc == 0), stop=(kc == KC - 1)),
                         reads=[pkey, hkey], writes=[PSK(5 + half)], inc=(kc == KC - 1))
            for half in range(2):
                qv = qrot[0:NS, half * 512:(half + 1) * 512].rearrange("p (h t f) -> p h t f", t=2, f=32)
                rope_tm(5 + half, NS, 8, scs[:, 0:32].rearrange("p (o f) -> p o f", o=1), scs[:, 32:64].rearrange("p (o f) -> p o f", o=1),
                        qv[:, :, 0, :], qv[:, :, 1, :], ["scs"], ["qrot"])
            for h in range(16):
                j, g = h // 4, h % 4
                pbse = (j % 2) * 64
                col = ((j // 2) * 4 + g) * NS
                S.op("pe", lambda e, h=h, pbse=pbse, col=col: e.matmul(ps[7][pbse:pbse + 64, col:col + NS], lhsT=qrot[0:NS, h * 64:(h + 1) * 64],
                                                                       rhs=ident_b[0:NS, 0:NS], start=True, stop=True, tile_position=(0, pbse)),
                     reads=["qrot", "ident_b"], writes=[PSK(7)], inc=(h == 15))
            S.op("act", lambda e: e.copy(out=sqT[:], in_=ps[7][:, 0:128]), reads=[PSK(7)], writes=["sqT"])
            sq3 = sqT[:].rearrange("p (a g n) -> p a g n", g=4, n=NS)
            for n in range(NS):
                for j in range(4):
                    pbse = (j % 2) * 64
                    sbank = 0 if j % 2 == 0 else 2
                    S.op("pe", lambda e, n=n, j=j, pbse=pbse, sbank=sbank: e.matmul(ps[sbank][:, n * 16 + 4 * j:n * 16 + 4 * j + 4], lhsT=sKT[pbse:pbse + 64, n, j // 2, :],
                                                                        rhs=sq3[pbse:pbse + 64, j // 2, :, n], start=True, stop=True),
                         reads=SKT_KEYS + ["sqT"], writes=[PSK(sbank)], inc=(n == NS - 1 and j >= 2))
            sst4 = sST[:].rearrange("p (n a b g) -> p n a b g", a=2, b=2, g=4)
            for par_ in range(2):
                sbank = 0 if par_ == 0 else 2
                src4 = ps[sbank][:, 0:256].rearrange("p (n a b g) -> p n a b g", a=2, b=2, g=4)
                for a_ in range(2):
                    S.op("act", lambda e, par_=par_, a_=a_, src4=src4: e.copy(out=sst4[:, :, a_, par_, :], in_=src4[:, :, a_, par_, :]),
                         reads=[PSK(sbank)], writes=[("fp", 5)])
            for c in range(2):
                S.op("pe", lambda e, c=c: e.transpose(out=ps[1][:, c * 128:(c + 1) * 128], in_=sST[:, c * 128:(c + 1) * 128], identity=ident_f[:]),
                     reads=[("fp", 5), "ident_f"], writes=[PSK(1)])
            for c in range(2):
                so = ps[1][:, c * 128:(c + 1) * 128]
                S.op("dve", lambda e, so=so, c=c: e.reduce_max(out=sv[:, c, 0:1], in_=so, axis=AX), reads=[PSK(1)], writes=[("sv0", c)])
                S.op("dve", lambda e, c=c: e.tensor_scalar(out=sv[:, c, 6:7], in0=sv[:, c, 0:1], scalar1=0.125, scalar2=sinkrow[:, l:l + 1], op0=ALU.mult, op1=ALU.max),
                     reads=[("sv0", c), "sinkrow"], writes=[("sv6", c)])
                S.op("dve", lambda e, c=c: e.tensor_scalar(out=sv[:, c, 1:2], in0=sv[:, c, 6:7], scalar1=-1.0, scalar2=None, op0=ALU.mult),
                     reads=[("sv6", c)], writes=[("sv1", c)])
                S.op("act", lambda e, so=so, c=c: e.activation(out=Pb[:, c, 0:128], in_=so, func=AF.Exp, bias=sv[:, c, 1:2], scale=0.125, accum_out=sv[:, c, 2:3]),
                     reads=[PSK(1), ("sv1", c)], writes=[("PG", 0), ("sv2", c)])
                S.op("act", lambda e, c=c: e.activation(out=sv[:, c, 3:4], in_=sv[:, c, 1:2], func=AF.Exp, bias=sinkrow[:, l:l + 1], scale=1.0),
                     reads=[("sv1", c), "sinkrow"], writes=[("sv3", c)])
                S.op("dve", lambda e, c=c: e.tensor_tensor(out=sv[:, c, 4:5], in0=sv[:, c, 2:3], in1=sv[:, c, 3:4], op=ALU.add),
                     reads=[("sv2", c), ("sv3", c)], writes=[("sv4", c)])
                S.op("dve", lambda e, c=c: e.reciprocal(out=sv[:, c, 5:6], in_=sv[:, c, 4:5]), reads=[("sv4", c)], writes=[("sv5", c)])
                S.op("dve", lambda e, c=c: e.tensor_scalar(out=sPn[:, c, :], in0=Pb[:, c, 0:128], scalar1=sv[:, c, 5:6], scalar2=None, op0=ALU.mult),
                     reads=[("PG", 0), ("sv5", c)], writes=[("PG", 1)])
            pb4 = ps[4][:, 0:128].bitcast(BF16)
            for c in range(2):
                S.op("pe", lambda e, c=c: e.transpose(out=pb4[:, c * 128:(c + 1) * 128], in_=sPn[:, c, :], identity=ident_b[:]),
                     reads=[("PG", 1), "ident_b"], writes=[PSK(4)])
            S.op("act", lambda e: e.copy(out=sPT[:], in_=pb4), reads=[PSK(4)], writes=["PT"])
            o3 = ps[5][:, 0:128].rearrange("p (c n) -> p c n", n=NS)
            for n in range(NS):
                for j in range(4):
                    for par in range(2):
                        pbse = par * 64
                        c0_ = n * 16 + 4 * j + par
                        S.op("pe", lambda e, n=n, j=j, pbse=pbse, c0_=c0_: e.matmul(o3[pbse:pbse + 64, 2 * j:2 * j + 2, n], lhsT=sVb[:, n, j * 64:(j + 1) * 64],
                                                                                    rhs=sPT[:, c0_:c0_ + 3:2], start=True, stop=True, tile_position=(0, pbse)),
                             reads=[("sVb", n), "PT"], writes=[PSK(5)], inc=(n == NS - 1 and j == 3 and par == 1))
            S.op("act", lambda e: e.copy(out=soT[:], in_=o3), reads=[PSK(5)], writes=["soT"])

        def out_block(c, nb, dst):
            sl = cnt["item"] % 2
            cnt["item"] += 1
            for half in range(2):
                for q in range(4):
                    kc = half * 4 + q
                    S.op("pe", lambda e, kc=kc, q=q, half=half: e.transpose(out=ps[half][0:nb, q * 128:(q + 1) * 128], in_=xT[:, kc, c:c + nb], identity=ident_f[:]),
                         reads=[("xo", c), "ident_f"], writes=[PSK(half)], inc=(q == 3))
                if half == 0:
                    S.op("act", lambda e, sl=sl: e.copy(out=stg[0:nb, sl, 0:512], in_=ps[0][0:nb, :]), reads=[PSK(0)], writes=[("stg", sl)])
                else:
                    S.op("dve", lambda e, sl=sl: e.tensor_copy(out=stg[0:nb, sl, 512:1024], in_=ps[1][0:nb, :]), reads=[PSK(1)], writes=[("stg", sl)])
            S.op("sp", lambda e, sl=sl: e.dma_start(out=dst, in_=stg[0:nb, sl, :]), reads=[("stg", sl)], dma=("stg", sl))

        def load_block(src, nb, c):
            sl = cnt["item"] % 2
            cnt["item"] += 1
            S.op("sp", lambda e, sl=sl: e.dma_start(out=stg[0:nb, sl, :], in_=src), writes=[("stg", sl)], dma=("stg", sl))
            for half in range(2):
                for q in range(4):
                    kc = half * 4 + q
                    S.op("pe", lambda e, kc=kc, q=q, half=half, sl=sl: e.transpose(out=ps[half][:, q * 128:q * 128 + nb], in_=stg[0:nb, sl, kc * 128:(kc + 1) * 128],
                                                                                    identity=ident_f[0:nb, 0:nb]),
                         reads=[("stg", sl), "ident_f"], writes=[PSK(half)], inc=(q == 3))
                src_ps = ps[half][:, :].rearrange("p (q t) -> p q t", t=128)[:, :, 0:nb]
                if half == 0:
                    S.op("act", lambda e, src_ps=src_ps: e.copy(out=xT[:, 0:4, c:c + nb], in_=src_ps), reads=[PSK(0)], writes=[("xl", c)] + sorted(xkeys, key=str))
                else:
                    S.op("dve", lambda e, src_ps=src_ps: e.tensor_copy(out=xT[:, 4:8, c:c + nb], in_=src_ps), reads=[PSK(1)], writes=[("xl", c)] + sorted(xkeys, key=str))

        def main_program():
          for pas in range(2):
              pbase = pas * NPG
              if pas == 0:
                  tiles = [(0, H, "halo"), (H, 512, "p"), (H + 512, 512, "p")]
                  blocks = [(0, 8, 0)] + [(8 + 128 * i, 128, 8 + 128 * i) for i in range(9)]
                  own0 = H
                  tok0 = 0
              else:
                  tiles = [(0, 512, "p"), (512, 512, "p"), (1024, NS, "sample")]
                  blocks = [(H + 1024 + 128 * i, 128, 128 * i) for i in range(8)]
                  own0 = 0
                  tok0 = 1024
              S.tag = 'p%d.load' % pas
              for (r, nb, c) in blocks:
                  load_block(xp_d[r:r + nb, :], nb, c)
              _stage(10 * pas + 0)
              if pas == 1:
                  load_block(xs_d[:, :], NS, 1024)
                  for l in range(2):
                      S.op("sp", lambda e, l=l: e.dma_start(out=stg[0:NS, 0, :], in_=cconv_d[l, :, 0, :]), writes=[("stg", 0)], dma=("stg", 0))
                      S.op("sp", lambda e, l=l: e.dma_start(out=stg[0:NS, 1, :], in_=cconv_d[l, :, 1, :]), writes=[("stg", 1)], dma=("stg", 1))
                      for r_ in range(2):
                          for j in range(8):
                              S.op("pe", lambda e, r_=r_, j=j: e.transpose(out=ps[r_][:, j * NS:(j + 1) * NS], in_=stg[0:NS, r_, j * 128:(j + 1) * 128],
                                                                            identity=ident_f[0:NS, 0:NS]),
                                   reads=[("stg", r_), "ident_f"], writes=[PSK(r_)], inc=(j == 7))
                          S.op("act", lambda e, r_=r_, l=l: e.copy(out=scT[:, l, r_, :, :].rearrange("p a b -> p (a b)"), in_=ps[r_][:, 0:8 * NS]),
                               reads=[PSK(r_)], writes=["scT"])
                  S.op("dve", lambda e: e.tensor_copy(out=kT[:, :, 0:128], in_=kT[:, :, 8 * 128:9 * 128]), reads=[("kT", 8)], writes=[("kT", 0)])
                  S.op("dve", lambda e: e.tensor_copy(out=vA[:, 0, :], in_=vA[:, 8, :]), reads=[("vA", 8)], writes=[("vA", 0)])
              for (c0, n, kind) in tiles:
                  deps_keys = [("xl", c) for (r, nb, c) in blocks if c >= c0 and c < c0 + n] + ([("xl", 1024)] if kind == "sample" else [])
                  S.op("dve", lambda e: e.memset(sv[:, 3, 7:8], 0.0), reads=deps_keys, writes=[("x", c0), "marker"])
                  xkeys.add(("x", c0))
                  for b_ in range(4):
                      xkeys.add(("xo", c0 + 128 * b_))
              for l in range(2):
                  S.tag = 'p%d.conv%d' % (pas, l)
                  conv_layer(l, tiles, pbase, pas == 1, skip_norm=(l == 1),
                             after_last=(lambda c0, n, kind, l=l: norm_h(c0, n, GI_FFN[l])))
                  _stage(10 * pas + 1 + 2 * l)
                  S.tag = 'p%d.ffn%d' % (pas, l)
                  ffn_layer(l, tiles, pbase, skip_norm=True,
                            after_last=(lambda c0, n, kind, l=l: norm_h(c0, n, GI_A[1] if l == 0 else GI_KV)))
                  _stage(10 * pas + 2 + 2 * l)
              S.tag = 'p%d.kv' % pas
              ptiles0 = [t for t in tiles if t[2] == "p"]
              for (c0, n, kind) in tiles:
                  if (c0, n, kind) == tiles[-1]:
                      fc0 = ptiles0[0][0]
                      gb0 = (tok0 + (fc0 - own0)) // 128 + 1
                      attn_prologue(0, fc0, gb0, pbase, ("h", fc0))
                  if kind == "halo":
                      kv_block(8, 0, 0, pbase, False, ("h", c0))
                  elif kind == "p":
                      for bi in range(4):
                          cb = c0 + bi * 128
                          gb = (tok0 + (cb - own0)) // 128 + 1
                          ks = gb if pas == 0 else gb - 8
                          kv_block(cb, gb, ks, pbase, gb == 16, ("h", c0))
                  else:
                      kv_flush()
                      kv_sample(c0, pbase, ("h", c0))
                  if kind != "halo":
                      norm_h(c0, n, GI_B[0])
              kv_flush()
              _stage(10 * pas + 5)
              def _tbl(c0):
                  tb_list = []
                  for bi in range(4):
                      cb = c0 + bi * 128
                      gb = (tok0 + (cb - own0)) // 128 + 1
                      ks = gb if pas == 0 else gb - 8
                      tb_list.append((cb, gb, ks, 0 if gb == 1 else 1, bi))
                  return tb_list
              ptiles = [t for t in tiles if t[2] == "p"]
              for l in range(2):
                  S.tag = 'p%d.attn%d' % (pas, l)
                  for (c0, n, kind) in tiles:
                      if kind == "halo":
                          continue
                      if kind == "p":
                          ti = [t[0] for t in ptiles].index(c0)
                          nf = None
                          if ti + 1 < len(ptiles):
                              nc0 = ptiles[ti + 1][0]
                              nt = _tbl(nc0)[0]
                              nf = (nt[0], nt[1], ("h", nc0))
                          fd = 2 if ti > 0 else (1 if l == 0 else 0)
                          attn_tile(l, c0, _tbl(c0), pbase, ("h", c0), first_done=fd, next_first=nf)
                      else:
                          attn_sample(l, c0, pbase, ("h", c0))
                      wo_proj(l, c0, n, pbase, sample=(kind == "sample"))
                      norm_h(c0, n, GI_FFN[2 + l])
                  _stage(10 * pas + 6 + l)
                  S.tag = 'p%d.ffn%d' % (pas, 2 + l)
                  def _after(c0, n, kind, l=l):
                      if l == 0:
                          norm_h(c0, n, GI_B[1])
                      else:
                          norm(c0, n, GI_FINAL, lambda kc, c0=c0, n=n: xT[:, kc, c0:c0 + n], [("x", c0)] + [("xo", c0 + 128 * b) for b in range(4)])
                  ffn_layer(2 + l, [t for t in tiles if t[2] != "halo"], pbase, skip_norm=True, after_last=_after)
              S.tag = 'p%d.out' % pas
              for (c0, n, kind) in tiles:
                  if kind == "halo":
                      continue
                  if kind == "p":
                      for bi in range(4):
                          cb = c0 + bi * 128
                          t = tok0 + (cb - own0)
                          out_block(cb, 128, yp_d[t:t + 128, :])
                  else:
                      out_block(c0, NS, ys_d[:, :])

        try:
            main_program()
        except _Stop:
            flush_pending(0)
        S.emit(nc, es)
    return nc


def _host_tables(core):
    q = core % 4
    start = q * TOK
    half = 32
    freqs = (10000.0 ** (-(np.arange(half, dtype=np.float32).astype(np.float64)) / half)).astype(np.float32)
    pos = (start - 128 + np.arange(17 * 128)).astype(np.float32)
    ang = (pos[:, None] * freqs[None, :]).astype(np.float32).astype(np.float64)
    cos = np.cos(ang).astype(np.float32).reshape(17, 128, 32).transpose(1, 0, 2).reshape(128, 17 * 32)
    sin = np.sin(ang).astype(np.float32).reshape(17, 128, 32).transpose(1, 0, 2).reshape(128, 17 * 32)
    angs = (np.float32(8192.0) * freqs).astype(np.float32).astype(np.float64)
    scs = np.concatenate([np.cos(angs), np.sin(angs)]).astype(np.float32)[None, :].repeat(NS, 0)
    a = np.arange(128)[:, None]
    b = np.arange(256)[None, :]
    band = np.where((b > a) & (b <= a + 128), 0.0, NEG).astype(np.float32)
    first = band.copy()
    if q == 0:
        first[:, 0:128] = NEG
    masks = np.stack([first, band], 1).reshape(128, 512)
    return np.ascontiguousarray(cos), np.ascontiguousarray(sin), np.ascontiguousarray(scs), np.ascontiguousarray(masks)


_PROGRAM = None
_LAST_SCHED = None


def kernel(x_prompt, x_sample, cache_conv, cache_k, cache_v, norm_a, w_in_a, conv_w, w_out_a, norm_kv, w_kv,
           norm_b, w_q, w_o, sinks, norm_ffn, w_gu, w_down, norm_final):
    global _PROGRAM
    f = lambda a: np.ascontiguousarray(np.asarray(a, dtype=np.float32))
    x_prompt, x_sample, cache_conv, cache_k, cache_v = map(f, (x_prompt, x_sample, cache_conv, cache_k, cache_v))
    inputs = dict(w_in_a=f(w_in_a), w_out_a=f(w_out_a), w_kv=f(w_kv), w_q=f(w_q), w_o=f(w_o), w_gu=f(w_gu), w_down=f(w_down))
    wpages = build_pages_host(inputs)
    gvecs = np.concatenate([f(norm_a), f(norm_ffn), f(norm_kv)[None], f(norm_b), f(norm_final)[None]], 0)
    gains = np.ascontiguousarray(gvecs.reshape(10, 8, 128).transpose(2, 0, 1).reshape(128, 80))
    convw = np.ascontiguousarray(f(conv_w).reshape(2, 3, 8, 128).transpose(3, 0, 1, 2).reshape(128, 48))
    sk = f(sinks)
    sinkb = np.ascontiguousarray(np.broadcast_to(sk.reshape(1, 32), (128, 32)))
    sinkrow = np.ascontiguousarray(sk[:, np.arange(128) % 16].T)
    ident = np.eye(128, dtype=np.float32)
    if _PROGRAM is None:
        _PROGRAM = build_program()
    nc = _PROGRAM
    in_maps = []
    for core in range(8):
        b, q = core // 4, core % 4
        start = q * TOK
        xp = np.zeros((H + TOK, D), np.float32)
        lo = start - H
        if lo < 0:
            xp[-lo:] = x_prompt[b, 0:start + TOK]
        else:
            xp[:] = x_prompt[b, lo:start + TOK]
        cos, sin, scs, masks = _host_tables(core)
        sl = slice(core * NS, (core + 1) * NS)
        in_maps.append(dict(
            xp=xp, xs=np.ascontiguousarray(x_sample[sl, 0, :]),
            cconv=np.ascontiguousarray(cache_conv[:, sl]),
            ck=np.ascontiguousarray(cache_k[sl].reshape(NS, 128, 256)),
            cv=np.ascontiguousarray(cache_v[sl].reshape(NS, 128, 256)),
            wpages=wpages, gains=gains, convw=convw, sinkb=sinkb, sinkrow=sinkrow,
            cost=cos, sint=sin, scs=scs, masks=masks, ident=ident))
    only = os.environ.get("MK_ONLY")
    if only is not None:
        res = run_bass_kernel_spmd(nc, [in_maps[int(only)]], core_ids=[0])
        R = [res.results[0]] * 8
    else:
        res = run_bass_kernel_spmd(nc, in_maps, core_ids=list(range(8)))
        R = res.results
    y_prompt = np.zeros((2, 8192, D), np.float32)
    y_sample = np.zeros((128, 1, D), np.float32)
    conv_prompt = np.zeros((2, 2, 2, D), np.float32)
    k_win_prompt = np.zeros((2, 128, 4, 64), np.float32)
    v_win_prompt = np.zeros((2, 128, 4, 64), np.float32)
    conv_sample = np.zeros((2, 128, 2, D), np.float32)
    k_win_sample = np.zeros((128, 128, 4, 64), np.float32)
    v_win_sample = np.zeros((128, 128, 4, 64), np.float32)
    for core in range(8):
        b, q = core // 4, core % 4
        r = R[core]
        y_prompt[b, q * TOK:(q + 1) * TOK] = r["yp"]
        sl = slice(core * NS, (core + 1) * NS)
        y_sample[sl, 0] = r["ys"]
        conv_sample[:, sl] = r["convs"]
        k_win_sample[sl] = r["kwins"].reshape(NS, 128, 4, 64)
        v_win_sample[sl] = r["vwins"].reshape(NS, 128, 4, 64)
        if q == 3:
            conv_prompt[:, b] = r["convp"]
            k_win_prompt[b] = r["kwinp"].reshape(128, 4, 64)
            v_win_prompt[b] = r["vwinp"].reshape(128, 4, 64)
    return (y_prompt, y_sample, conv_prompt, k_win_prompt, v_win_prompt, conv_sample, k_win_sample, v_win_sample)
```
